# Optimizing a Trainium2 kernel written in Bass

```python
import math
import jax
import jax.numpy as jnp
from jax import lax
import numpy as np

D_MODEL = 1024
BATCH = 32
SEQ = 256
DEPTH = 4
DEC_BATCH = 8
DEC_SEQ = 4096
PAST_LEN = 512

GRID_W = 64
EPS = 1e-6
POS_BASE = 10000.0
NEG_BIG = -1e30
CHUNK = 64
H_CHUNK = 32

M_HEADS = 4
M_DH = 128
M_W = M_HEADS * M_DH
H_HEADS = 4
H_DK = 128
H_DV = 128
H_W = H_HEADS * H_DV
S_HEADS = 16
S_P = 64
S_W = S_HEADS * S_P
S_GROUPS = 4
S_N = 128
CONV_K = 3
CONV_CH = S_W + 2 * S_GROUPS * S_N
D_FF = ((8 * D_MODEL + 3 * 256 - 1) // (3 * 256)) * 256

IN_SIZES = (M_W, M_W, M_W, M_W, 2 * M_HEADS, 2 * M_HEADS,
            H_HEADS * H_DK, H_W, H_W, 2 * H_HEADS * H_DK,
            S_W, CONV_CH, 2 * S_HEADS,
            3 * D_MODEL)
N_IN = sum(IN_SIZES)
SPLIT_POINTS = tuple(int(s) for s in np.cumsum(IN_SIZES)[:-1])

kernel_name = "bidir_mlstm_hgrn2_ssd_diffusion_step"


def rmsnorm(x, w):
    xf = x.astype(jnp.float32)
    y = xf * lax.rsqrt(jnp.mean(xf * xf, axis=-1, keepdims=True) + EPS)
    return (y * w.astype(jnp.float32)).astype(x.dtype)


def head_rmsnorm(x, w):
    y = x * lax.rsqrt(jnp.mean(x * x, axis=-1, keepdims=True) + EPS)
    return y.reshape(x.shape[:2] + (-1,)) * w.astype(jnp.float32)


def grid_pos_embed(rows):
    quarter = D_MODEL // 4
    freq = POS_BASE ** (-jnp.arange(quarter, dtype=jnp.float32) / quarter)
    r = jnp.arange(rows, dtype=jnp.float32)[:, None] * freq
    cl = jnp.arange(GRID_W, dtype=jnp.float32)[:, None] * freq
    row_e = jnp.concatenate([jnp.sin(r), jnp.cos(r)], axis=-1)
    col_e = jnp.concatenate([jnp.sin(cl), jnp.cos(cl)], axis=-1)
    emb = jnp.concatenate([jnp.broadcast_to(row_e[:, None], (rows, GRID_W, D_MODEL // 2)),
                           jnp.broadcast_to(col_e[None], (rows, GRID_W, D_MODEL // 2))], axis=-1)
    return emb.reshape(rows * GRID_W, D_MODEL)


def dwconv(u, w, b):
    pad = CONV_K // 2
    y = lax.conv_general_dilated(u, w[:, None, :].astype(u.dtype), window_strides=(1,),
                                 padding=[(pad, pad)], dimension_numbers=("NWC", "WIO", "NWC"),
                                 feature_group_count=u.shape[-1])
    return y + b.astype(u.dtype)


def _chunks(a, c):
    b, l, hh = a.shape[:3]
    a = a.reshape((b, l // c, c, hh) + a.shape[3:])
    return jnp.moveaxis(a, (1, 3), (0, 2))


def _unchunk(a):
    a = jnp.moveaxis(a, (0, 2), (1, 3))
    return a.reshape((a.shape[0], a.shape[1] * a.shape[2]) + a.shape[3:])


def mlstm_scan(q, k, v, ig, lf, c0, n0, m0):
    causal = jnp.tril(jnp.ones((CHUNK, CHUNK), dtype=bool))

    def step(carry, inp):
        cmat, nvec, m = carry
        qc, kc, vc, ic, fc = inp
        b = jnp.cumsum(fc, axis=-1)
        dlog = jnp.where(causal, b[..., :, None] - b[..., None, :] + ic[..., None, :], NEG_BIG)
        inter = b + m[..., None]
        mt = jnp.maximum(inter, jnp.max(dlog, axis=-1))
        s = jnp.einsum("bhtd,bhsd->bhts", qc, kc) * jnp.exp(dlog - mt[..., None])
        wi = jnp.exp(inter - mt)
        num = jnp.einsum("bhts,bhse->bhte", s, vc) + wi[..., None] * jnp.einsum("bhtd,bhde->bhte", qc, cmat)
        den = jnp.sum(s, axis=-1) + wi * jnp.einsum("bhtd,bhd->bht", qc, nvec)
        hc = num / jnp.maximum(jnp.abs(den), jnp.exp(-mt))[..., None]
        m_end = mt[..., -1]
        wk = jnp.exp(b[..., -1:] - b + ic - m_end[..., None])
        carry_decay = jnp.exp(b[..., -1] + m - m_end)
        c_new = carry_decay[..., None, None] * cmat + jnp.einsum("bhs,bhsd,bhse->bhde", wk, kc, vc)
        n_new = carry_decay[..., None] * nvec + jnp.einsum("bhs,bhsd->bhd", wk, kc)
        return (c_new, n_new, m_end), hc

    xs = tuple(_chunks(a, CHUNK) for a in (q, k, v, ig, lf))
    (c_f, n_f, m_f), h = lax.scan(step, (c0, n0, m0), xs)
    return _unchunk(h), (c_f, n_f, m_f)


def hgrn_scan(q, k, v, logf, s0):
    causal = jnp.tril(jnp.ones((H_CHUNK, H_CHUNK), dtype=bool))[..., None]

    def step(s, inp):
        qc, kc, vc, gc = inp
        g = jnp.cumsum(gc, axis=-2)
        diff = jnp.where(causal, g[..., :, None, :] - g[..., None, :, :], 0.0)
        w = jnp.exp(diff) * causal
        att = jnp.einsum("bhtk,bhsk,bhtsk->bhts", qc, kc, w)
        o = jnp.einsum("bhts,bhsv->bhtv", att, vc) + jnp.einsum("bhtk,bhkv->bhtv", qc * jnp.exp(g), s)
        g_end = g[..., -1:, :]
        s_new = jnp.exp(g_end[..., 0, :])[..., None] * s + jnp.einsum("bhsk,bhsv->bhkv", kc * jnp.exp(g_end - g), vc)
        return s_new, o

    xs = tuple(_chunks(a, H_CHUNK) for a in (q, k, v, logf))
    s_fin, o = lax.scan(step, s0, xs)
    return _unchunk(o), (s_fin,)


def ssd_scan(x, dt, bm, cm, a, h0):
    causal = jnp.tril(jnp.ones((CHUNK, CHUNK), dtype=bool))
    n_rep = S_HEADS // S_GROUPS

    def step(h, inp):
        xc, dtc, bc, cc = inp
        bsz = xc.shape[0]
        la = jnp.cumsum(dtc * a[:, None], axis=-1)
        diff = jnp.where(causal, la[..., :, None] - la[..., None, :], 0.0)
        seg = jnp.exp(diff) * causal * dtc[..., None, :]
        seg = seg.reshape(bsz, S_GROUPS, n_rep, CHUNK, CHUNK)
        cb = jnp.einsum("bgtn,bgsn->bgts", cc, bc)
        xg = xc.reshape(bsz, S_GROUPS, n_rep, CHUNK, S_P)
        y_intra = jnp.einsum("bgrts,bgts,bgrsp->bgrtp", seg, cb, xg)
        hg = h.reshape(bsz, S_GROUPS, n_rep, S_P, S_N)
        y_inter = jnp.einsum("bgtn,bgrpn->bgrtp", cc, hg) * jnp.exp(la).reshape(bsz, S_GROUPS, n_rep, CHUNK, 1)
        y = (y_intra + y_inter).reshape(bsz, S_HEADS, CHUNK, S_P)
        la_end = la[..., -1]
        wk = (jnp.exp(la_end[..., None] - la) * dtc).reshape(bsz, S_GROUPS, n_rep, CHUNK)
        h_new = jnp.exp(la_end)[..., None, None] * h + jnp.einsum(
            "bgrs,bgrsp,bgsn->bgrpn", wk, xg, bc).reshape(bsz, S_HEADS, S_P, S_N)
        return h_new, y

    xs = (_chunks(x, CHUNK), _chunks(dt, CHUNK), _chunks(bm, CHUNK), _chunks(cm, CHUNK))
    h_fin, y = lax.scan(step, h0, xs)
    return _unchunk(y), (h_fin,)


def _bidir(scan_fn, seqs_f, seqs_b, init_f, init_b):
    y_f, st_f = scan_fn(*seqs_f, *init_f)
    y_b, st_b = scan_fn(*[jnp.flip(s, axis=1) for s in seqs_b], *init_b)
    return y_f + jnp.flip(y_b, axis=1), st_f, st_b


def mixer(h, lp, init, rows):
    f32 = jnp.float32
    bsz, L, _ = h.shape
    c0, n0, m0, s0, h0 = (s.astype(f32) for s in init)
    parts = jnp.split(h @ lp["w_in"], SPLIT_POINTS, axis=-1)
    mq, mk, mv, mo, mi, mf, hq, hi, hg, hf, sz, sxbc, sdt, bgate = (p.astype(f32) for p in parts)

    q = mq.reshape(bsz, L, M_HEADS, M_DH)
    k = mk.reshape(bsz, L, M_HEADS, M_DH) * (M_DH ** -0.5)
    v = mv.reshape(bsz, L, M_HEADS, M_DH)
    ig = mi.reshape(bsz, L, 2, M_HEADS) + lp["m_bi"]
    lf = jax.nn.log_sigmoid(mf.reshape(bsz, L, 2, M_HEADS) + lp["m_bf"])
    hm, mst_f, mst_b = _bidir(mlstm_scan,
                              (q, k, v, ig[:, :, 0], lf[:, :, 0]), (q, k, v, ig[:, :, 1], lf[:, :, 1]),
                              (c0[:, 0], n0[:, 0], m0[:, 0]), (c0[:, 1], n0[:, 1], m0[:, 1]))
    y_m = head_rmsnorm(hm, lp["m_norm"]) * jax.nn.sigmoid(mo)

    lb = lp["h_lb"].reshape(2, H_HEADS, H_DK)
    fgate = lb + (1.0 - lb) * jax.nn.sigmoid(hf.reshape(bsz, L, 2, H_HEADS, H_DK))
    logf = jnp.log(fgate)
    kk = 1.0 - fgate
    hq4 = hq.reshape(bsz, L, H_HEADS, H_DK)
    iv = hi.reshape(bsz, L, H_HEADS, H_DV)
    ho, hs_f, hs_b = _bidir(hgrn_scan,
                            (hq4, kk[:, :, 0], iv, logf[:, :, 0]), (hq4, kk[:, :, 1], iv, logf[:, :, 1]),
                            (s0[:, 0],), (s0[:, 1],))
    y_h = head_rmsnorm(ho, lp["h_norm"]) * jax.nn.silu(hg)

    if rows is None:
        u = dwconv(sxbc, lp["s_conv_w"], lp["s_conv_b"])
    else:
        u = dwconv(sxbc.reshape(bsz * rows, GRID_W, CONV_CH), lp["s_conv_w"], lp["s_conv_b"])
        u = u.reshape(bsz, L, CONV_CH)
    u = jax.nn.silu(u)
    xs, bm, cm = jnp.split(u, (S_W, S_W + S_GROUPS * S_N), axis=-1)
    xs = xs.reshape(bsz, L, S_HEADS, S_P)
    bm = bm.reshape(bsz, L, S_GROUPS, S_N)
    cm = cm.reshape(bsz, L, S_GROUPS, S_N)
    dt = jax.nn.softplus(sdt.reshape(bsz, L, 2, S_HEADS) + lp["s_dt_bias"])
    a = -jnp.exp(lp["s_a_log"].astype(f32))
    ys, ss_f, ss_b = _bidir(ssd_scan,
                            (xs, dt[:, :, 0], bm, cm), (xs, dt[:, :, 1], bm, cm),
                            (a[0], h0[:, 0]), (a[1], h0[:, 1]))
    ys = ys + lp["s_d"].astype(f32)[:, None] * xs
    y_s = rmsnorm(ys.reshape(bsz, L, S_W) * jax.nn.silu(sz), lp["s_norm"])

    g = jax.nn.sigmoid(bgate.reshape(bsz, L, 3, D_MODEL))
    merged = (g[:, :, 0] * (y_m @ lp["w_bm"]) + g[:, :, 1] * (y_h @ lp["w_bh"])
              + g[:, :, 2] * (y_s @ lp["w_bs"]))
    out = (merged @ lp["w_out"]).astype(h.dtype)
    if rows is not None:
        return out, None
    states = tuple(jnp.stack([sf, sb], axis=1).astype(h.dtype)
                   for sf, sb in zip(mst_f + hs_f + ss_f, mst_b + hs_b + ss_b))
    return out, states


def trunk_layer(x, cvec, lp, init, rows):
    mod = jax.nn.silu(cvec) @ lp["w_ada"] + lp["b_ada"]
    sh1, sc1, g1, sh2, sc2, g2 = jnp.split(mod[:, None, :], 6, axis=-1)
    h = rmsnorm(x, lp["norm1"]) * (1 + sc1) + sh1
    mix, states = mixer(h, lp, init, rows)
    x = x + g1 * mix
    h = rmsnorm(x, lp["norm2"]) * (1 + sc2) + sh2
    a, b = jnp.split(h @ lp["w_gu"], 2, axis=-1)
    x = x + g2 * ((jax.nn.silu(a) * b) @ lp["w_down"])
    return x, states


def setup_inputs(seed: int = 0) -> dict:
    key = jax.random.key(seed)
    ks = jax.random.split(key, 40)
    f32 = jnp.float32
    D = D_MODEL

    def nrm(k, shape, scale):
        return jax.random.normal(k, shape, f32) * scale

    dt0 = jnp.exp(jax.random.uniform(ks[20], (DEPTH, 2, S_HEADS), f32, math.log(1e-3), math.log(1e-1)))
    return {
        "x_prompt": nrm(ks[0], (BATCH, SEQ, D), 1.0),
        "x_sample": nrm(ks[1], (DEC_BATCH, DEC_SEQ, D), 1.0),
        "state_mlstm_c": nrm(ks[2], (DEC_BATCH, DEPTH, 2, M_HEADS, M_DH, M_DH), 0.1),
        "state_mlstm_n": nrm(ks[3], (DEC_BATCH, DEPTH, 2, M_HEADS, M_DH), 0.1),
        "state_mlstm_m": nrm(ks[4], (DEC_BATCH, DEPTH, 2, M_HEADS), 0.5),
        "state_hgrn": nrm(ks[5], (DEC_BATCH, DEPTH, 2, H_HEADS, H_DK, H_DV), 0.3),
        "state_ssm": nrm(ks[6], (DEC_BATCH, DEPTH, 2, S_HEADS, S_P, S_N), 0.1),
        "c": nrm(ks[7], (DEC_BATCH, D), 1.0),
        "c_ctx": nrm(ks[8], (D,), 1.0),
        "w_ada": nrm(ks[9], (DEPTH, D, 6 * D), 0.5 * D ** -0.5),
        "b_ada": nrm(ks[10], (DEPTH, 6 * D), 0.02),
        "norm1": 1.0 + nrm(ks[11], (DEPTH, D), 0.02),
        "norm2": 1.0 + nrm(ks[12], (DEPTH, D), 0.02),
        "w_in": nrm(ks[13], (DEPTH, D, N_IN), D ** -0.5),
        "m_bi": nrm(ks[14], (DEPTH, 2, M_HEADS), 0.1),
        "m_bf": 3.0 + nrm(ks[15], (DEPTH, 2, M_HEADS), 0.5),
        "m_norm": 1.0 + nrm(ks[16], (DEPTH, M_W), 0.02),
        "h_lb": nrm(ks[17], (2, DEPTH, H_HEADS * H_DK), 1.0),
        "h_norm": 1.0 + nrm(ks[18], (DEPTH, H_W), 0.02),
        "s_conv_w": nrm(ks[19], (DEPTH, CONV_K, CONV_CH), CONV_K ** -0.5),
        "s_conv_b": nrm(ks[21], (DEPTH, CONV_CH), 0.02),
        "s_dt_bias": dt0 + jnp.log(-jnp.expm1(-dt0)),
        "s_a_log": jnp.log(jax.random.uniform(ks[22], (DEPTH, 2, S_HEADS), f32, 1.0, 16.0)),
        "s_d": 1.0 + nrm(ks[23], (DEPTH, S_HEADS), 0.1),
        "s_norm": 1.0 + nrm(ks[24], (DEPTH, S_W), 0.02),
        "w_bm": nrm(ks[25], (DEPTH, M_W, D), M_W ** -0.5),
        "w_bh": nrm(ks[26], (DEPTH, H_W, D), H_W ** -0.5),
        "w_bs": nrm(ks[27], (DEPTH, S_W, D), S_W ** -0.5),
        "w_out": nrm(ks[28], (DEPTH, D, D), D ** -0.5),
        "w_gu": nrm(ks[29], (DEPTH, D, 2 * D_FF), D ** -0.5),
        "w_down": nrm(ks[30], (DEPTH, D_FF, D), D_FF ** -0.5),
        "norm_f": 1.0 + nrm(ks[31], (D,), 0.02),
    }


def reference(x_prompt, x_sample, state_mlstm_c, state_mlstm_n, state_mlstm_m, state_hgrn, state_ssm,
              c, c_ctx, w_ada, b_ada, norm1, norm2, w_in, m_bi, m_bf, m_norm, h_lb, h_norm,
              s_conv_w, s_conv_b, s_dt_bias, s_a_log, s_d, s_norm, w_bm, w_bh, w_bs, w_out,
              w_gu, w_down, norm_f):
    f32 = jnp.float32
    p = jax.nn.softmax(h_lb.astype(f32), axis=1)
    lower = jnp.cumsum(p, axis=1) - p[:, :1]

    def layer_params(i):
        return {"w_ada": w_ada[i], "b_ada": b_ada[i], "norm1": norm1[i], "norm2": norm2[i],
                "w_in": w_in[i], "m_bi": m_bi[i], "m_bf": m_bf[i], "m_norm": m_norm[i],
                "h_lb": lower[:, i], "h_norm": h_norm[i], "s_conv_w": s_conv_w[i], "s_conv_b": s_conv_b[i],
                "s_dt_bias": s_dt_bias[i], "s_a_log": s_a_log[i], "s_d": s_d[i], "s_norm": s_norm[i],
                "w_bm": w_bm[i], "w_bh": w_bh[i], "w_bs": w_bs[i], "w_out": w_out[i],
                "w_gu": w_gu[i], "w_down": w_down[i]}

    bp = x_prompt.shape[0]
    zero_init = (jnp.zeros((bp, 2, M_HEADS, M_DH, M_DH), f32), jnp.zeros((bp, 2, M_HEADS, M_DH), f32),
                 jnp.zeros((bp, 2, M_HEADS), f32), jnp.zeros((bp, 2, H_HEADS, H_DK, H_DV), f32),
                 jnp.zeros((bp, 2, S_HEADS, S_P, S_N), f32))
    xp = x_prompt
    ctx_states = []
    for i in range(DEPTH):
        xp, st = trunk_layer(xp, c_ctx[None, :], layer_params(i), zero_init, None)
        ctx_states.append(st)
    y_prompt = rmsnorm(xp, norm_f)
    new_mlstm_c = jnp.stack([st[0] for st in ctx_states], axis=1)
    new_mlstm_n = jnp.stack([st[1] for st in ctx_states], axis=1)
    new_mlstm_m = jnp.stack([st[2] for st in ctx_states], axis=1)
    new_hgrn = jnp.stack([st[3] for st in ctx_states], axis=1)
    new_ssm = jnp.stack([st[4] for st in ctx_states], axis=1)

    rows = x_sample.shape[1] // GRID_W
    xs = x_sample + grid_pos_embed(rows).astype(x_sample.dtype)
    for i in range(DEPTH):
        cached = (state_mlstm_c[:, i], state_mlstm_n[:, i], state_mlstm_m[:, i], state_hgrn[:, i], state_ssm[:, i])
        xs, _ = trunk_layer(xs, c, layer_params(i), cached, rows)
    y_sample = rmsnorm(xs, norm_f)

    return (y_prompt, y_sample, new_mlstm_c, new_mlstm_n, new_mlstm_m, new_hgrn, new_ssm)
```

```python
import numpy as np
from contextlib import ExitStack
import concourse.bass as bass
import concourse.mybir as mybir
from concourse.bass_utils import run_bass_kernel_spmd

F32 = mybir.dt.float32
BF16 = mybir.dt.bfloat16
ALU = mybir.AluOpType
AF = mybir.ActivationFunctionType
AX = mybir.AxisListType

D = 1024
DEPTH = 4
NT = 5120
TT = 512
NTT = NT // TT
N_IN = 10800
D_FF = 2816
EPS = 1e-6
C_MQ, C_MK, C_MV, C_MO, C_MI, C_MF = 0, 512, 1024, 1536, 2048, 2056
C_HQ, C_HI, C_HG, C_HF = 2064, 2576, 3088, 3600
C_SZ, C_SX, C_SDT, C_BG = 4624, 5648, 7696, 7728
SEQS = [(0, 256, 256, 0), (256, 256, 256, 0), (512, 256, 256, 0), (768, 256, 256, 0), (1024, 4096, 64, 1)]


class TK:
    CE = ("pe", "act", "dve", "pool")

    def __init__(self, nc, es):
        self.nc, self.es = nc, es
        self.ops = {e: [] for e in ("pe", "act", "dve", "pool", "sp")}
        self.lastw, self.rd = {}, {}
        self.known = {e: {} for e in self.ops}
        self.dcount, self.dsem = {}, {}
        self.needed = {e: set() for e in self.CE}
        self.csem = {e: self.es.enter_context(nc.semaphore("c_" + e)) for e in self.CE}
        self.trackmap = {}

    def _deps(self, eng, reads, writes, extra=()):
        deps = {}

        def add(t, v):
            if v > deps.get(t, 0):
                deps[t] = v
        for r in reads:
            lw = self.lastw.get(r)
            if lw:
                add(*lw)
        for w in writes:
            lw = self.lastw.get(w)
            if lw:
                add(*lw)
            for t, v in self.rd.get(w, {}).items():
                add(t, v)
        for t, v in extra:
            add(t, v)
        out = []
        kn = self.known[eng]
        for t, v in deps.items():
            if t == eng and eng == "pe":
                continue
            if kn.get(t, 0) >= v:
                continue
            kn[t] = v
            out.append((t, v))
            if t in self.CE:
                self.needed[t].add(v)
        return out

    def _mark(self, tv, reads, writes):
        for w in writes:
            self.lastw[w] = tv
            self.rd[w] = {}
        for r in reads:
            d = self.rd.setdefault(r, {})
            if tv[1] > d.get(tv[0], 0):
                d[tv[0]] = tv[1]

    @staticmethod
    def _keys(aps):
        ks = []
        for a in aps:
            if a is None or isinstance(a, (int, float)):
                continue
            ks.append(a if isinstance(a, str) else getattr(a, "tensor", a).name)
        return ks

    def op(self, eng, fn, outs, ins):
        writes, reads = self._keys(outs), self._keys(ins)
        waits = self._deps(eng, reads, writes)
        idx = len(self.ops[eng]) + 1
        self.ops[eng].append((waits, fn, None))
        self._mark((eng, idx), reads, writes)

    def dma(self, out, in_, track, outs=None, ins=None, eng="sp", slow=False):
        writes = self._keys(outs if outs is not None else [])
        reads = self._keys(ins if ins is not None else [])
        track = self.trackmap.setdefault(track, f"T{len(self.trackmap) % 64}")
        extra = []
        if self.dcount.get(track, 0):
            extra.append((track, self.dcount[track]))
        waits = self._deps(eng, reads, writes, extra)
        self.dcount[track] = self.dcount.get(track, 0) + 16
        if track not in self.dsem:
            self.dsem[track] = self.es.enter_context(self.nc.semaphore("d_" + track))
        if slow:
            self.ops[eng].append((waits, lambda e: e.dma_start(out=out, in_=in_, allow_slow_non_contiguous=True), track))
        else:
            self.ops[eng].append((waits, lambda e: e.dma_start(out=out, in_=in_), track))
        self._mark((track, self.dcount[track]), reads, writes)

    def barrier(self):
        cur = []
        for e in self.CE:
            if self.ops[e]:
                cur.append((e, len(self.ops[e])))
        for t, c in self.dcount.items():
            cur.append((t, c))
        fixed = []
        for t, v in cur:
            if t in self.CE:
                ops = self.ops[t]
                while v > 0 and (ops[v - 1][1] is None or ops[v - 1][2] is not None):
                    v -= 1
                if v == 0:
                    continue
            fixed.append((t, v))
        for e in self.ops:
            waits = self._deps(e, [], [], fixed)
            if waits:
                self.ops[e].append((waits, None, None))
        self.lastw.clear()
        self.rd.clear()

    def emit(self, block):
        nc = self.nc
        sem = self.csem
        rank = {e: {v: i + 1 for i, v in enumerate(sorted(self.needed[e]))} for e in self.CE}

        def run(name, eng):
            for idx, (waits, fn, track) in enumerate(self.ops[name], 1):
                for t, v in waits:
                    if t in self.CE:
                        eng.wait_ge(sem[t], rank[t][v])
                    else:
                        eng.wait_ge(self.dsem[t], v)
                if fn is None:
                    continue
                ins = fn(eng)
                if track is not None:
                    ins.then_inc(self.dsem[track], 16)
                elif idx in self.needed.get(name, ()):
                    ins.then_inc(sem[name], 1)

        @block.sync
        def _(e):
            run("sp", e)

        @block.tensor
        def _(e):
            run("pe", e)

        @block.scalar
        def _(e):
            run("act", e)

        @block.vector
        def _(e):
            run("dve", e)

        @block.gpsimd
        def _(e):
            run("pool", e)


def build(debug=False, nlayers=DEPTH, stop_after=None, secs="ABCDEPSO"):
    nc = bass.Bass("TRN2", target_bir_lowering=False)
    dbg = {}

    def din(name, shape, dt=F32):
        return nc.dram_tensor(name, list(shape), dt, kind="ExternalInput").ap()

    def dout(name, shape, dt=F32):
        return nc.dram_tensor(name, list(shape), dt, kind="ExternalOutput").ap()

    def dscr(name, shape, dt=F32):
        if debug:
            dbg[name] = (shape, dt)
            return nc.dram_tensor(name, list(shape), dt, kind="ExternalOutput").ap()
        return nc.dram_tensor(name, list(shape), dt, kind="Internal").ap()

    x_in = din("x_in", [NT, D])
    st_c = din("st_c", [DEPTH, 2, 4, 128, 128])
    st_n = din("st_n", [DEPTH, 2, 4, 128])
    st_m = din("st_m", [DEPTH, 2, 4])
    st_h = din("st_h", [DEPTH, 2, 4, 128, 128])
    st_s = din("st_s", [DEPTH, 2, 16, 64, 128])
    cvec = din("cvec", [2, D])
    w_ada = din("w_ada", [DEPTH, D, 6 * D])
    b_ada = din("b_ada", [DEPTH, 6 * D])
    norm1 = din("norm1", [DEPTH, D])
    norm2 = din("norm2", [DEPTH, D])
    w_in = din("w_in", [DEPTH, D, N_IN])
    m_bi = din("m_bi", [DEPTH, 8])
    m_bf = din("m_bf", [DEPTH, 8])
    m_norm = din("m_norm", [DEPTH, 512])
    h_lb = din("h_lb", [2, DEPTH, 512])
    h_norm = din("h_norm", [DEPTH, 512])
    s_conv_w = din("s_conv_w", [DEPTH, 3, 2048])
    s_conv_b = din("s_conv_b", [DEPTH, 2048])
    s_dt_bias = din("s_dt_bias", [DEPTH, 32])
    s_a_log = din("s_a_log", [DEPTH, 32])
    s_d = din("s_d", [DEPTH, 16])
    s_norm = din("s_norm", [DEPTH, 1024])
    w_bm = din("w_bm", [DEPTH, 512, D])
    w_bh = din("w_bh", [DEPTH, 512, D])
    w_bs = din("w_bs", [DEPTH, 1024, D])
    w_out = din("w_out", [DEPTH, D, D])
    w_gu = din("w_gu", [DEPTH, D, 2 * D_FF])
    w_down = din("w_down", [DEPTH, D_FF, D])
    norm_f = din("norm_f", [D])
    cst = din("cst", [128, 768])

    y_out = dout("y_out", [NT, D])
    o_c = dout("o_c", [4, DEPTH, 2, 4, 128, 128])
    o_n = dout("o_n", [4, DEPTH, 2, 4, 128])
    o_m = dout("o_m", [4, DEPTH, 2, 4])
    o_h = dout("o_h", [4, DEPTH, 2, 4, 128, 128])
    o_s = dout("o_s", [4, DEPTH, 2, 16, 64, 128])

    xT = dscr("xT", [8, 128, NT])
    wb = {
        "in": (w_in, nc.dram_tensor("wb_in", [DEPTH, D, N_IN], BF16, kind="Internal").ap(), D, N_IN),
        "bm": (w_bm, nc.dram_tensor("wb_bm", [DEPTH, 512, D], BF16, kind="Internal").ap(), 512, D),
        "bh": (w_bh, nc.dram_tensor("wb_bh", [DEPTH, 512, D], BF16, kind="Internal").ap(), 512, D),
        "bs": (w_bs, nc.dram_tensor("wb_bs", [DEPTH, 1024, D], BF16, kind="Internal").ap(), 1024, D),
        "out": (w_out, nc.dram_tensor("wb_out", [DEPTH, D, D], BF16, kind="Internal").ap(), D, D),
        "gu": (w_gu, nc.dram_tensor("wb_gu", [DEPTH, D, 2 * D_FF], BF16, kind="Internal").ap(), D, 2 * D_FF),
        "down": (w_down, nc.dram_tensor("wb_down", [DEPTH, D_FF, D], BF16, kind="Internal").ap(), D_FF, D),
    }
    S = {}
    for nm, rows in (("qTm", 512), ("kTm", 512), ("hqT", 512), ("kkT", 1024), ("cmT", 512), ("bmT", 512), ("bgT", 3072),
                     ("ymT", 512), ("yhT", 512), ("ysT", 1024)):
        S[nm] = dscr(nm, [rows, NT], BF16)
    for nm, rows in (("gi", 8), ("glf", 8), ("lfT", 1024), ("dtT", 32)):
        S[nm] = dscr(nm, [rows, NT], F32)
    for nm, cols in (("k_tm", 512), ("v_tm", 512), ("go_tm", 512), ("hv_tm", 512), ("hg_tm", 512), ("sz_tm", 1024),
                     ("xs_tm", 1024), ("bm_tm", 512)):
        S[nm] = dscr(nm, [NT, cols], BF16)
    S["yb_all"] = dscr("yb_all", [NT, 2048], F32)

    with ExitStack() as es:
        tk = TK(nc, es)
        _n = [0]

        pes = ExitStack()
        cur_es = [es]

        basename = {}
        usage = {}
        dbg['usage'] = usage

        def sb(shape, dt=F32, name=None):
            _n[0] += 1
            nm = f"{name or 't'}_{_n[0]}"
            basename[nm] = name or nm
            sz = int(np.prod(shape[1:])) * (2 if dt == BF16 else 4)
            usage[id(cur_es[0])] = usage.get(id(cur_es[0]), 0) + sz
            return cur_es[0].enter_context(nc.sbuf_tensor(nm, list(shape), dt))

        def ps(shape, dt=F32, name=None):
            _n[0] += 1
            return es.enter_context(nc.psum_tensor(name or f"p{_n[0]}", list(shape), dt))

        PS = [ps([128, 512], F32, f"psb{i}") for i in range(8)]
        psi = {"i": 0}

        def nps():
            psi["i"] += 1
            return PS[psi["i"] % 7]

        def mm(out, lhsT, rhs, start=True, stop=True):
            tk.op("pe", lambda e: e.matmul(out, lhsT=lhsT, rhs=rhs, start=start, stop=stop), [out], [lhsT, rhs])

        def tr(out, in_, ident):
            tk.op("pe", lambda e: e.transpose(out, in_, ident), [out], [in_, ident])

        def act(out, in_, func, bias=0.0, scale=1.0, eng="act"):
            tk.op(eng, lambda e: e.activation(out=out, in_=in_, func=func, bias=bias, scale=scale), [out], [in_, bias, scale])

        def tt(eng, out, in0, in1, op):
            tk.op(eng, lambda e: e.tensor_tensor(out=out, in0=in0, in1=in1, op=op), [out], [in0, in1])

        def ts(eng, out, in0, s1, s2, op0, op1=None):
            if op1 is None:
                tk.op(eng, lambda e: e.tensor_scalar(out=out, in0=in0, scalar1=s1, scalar2=None, op0=op0), [out], [in0, s1])
            else:
                tk.op(eng, lambda e: e.tensor_scalar(out=out, in0=in0, scalar1=s1, scalar2=s2, op0=op0, op1=op1), [out], [in0, s1, s2])

        def stt(eng, out, in0, scalar, in1, op0, op1):
            tk.op(eng, lambda e: e.scalar_tensor_tensor(out=out, in0=in0, scalar=scalar, in1=in1, op0=op0, op1=op1), [out], [in0, scalar, in1])

        def cp(eng, out, in_):
            if eng == "act":
                tk.op(eng, lambda e: e.copy(out=out, in_=in_), [out], [in_])
            else:
                tk.op(eng, lambda e: e.tensor_copy(out=out, in_=in_), [out], [in_])

        def mset(eng, out, val):
            tk.op(eng, lambda e: e.memset(out, val), [out], [])

        def red(eng, out, in_, op, axis=AX.X):
            tk.op(eng, lambda e: e.tensor_reduce(out=out, in_=in_, axis=axis, op=op), [out], [in_])

        def scan(out, d0, d1, init, op0, op1):
            tk.op("dve", lambda e: e.tensor_tensor_scan(out=out, data0=d0, data1=d1, initial=init, op0=op0, op1=op1), [out], [d0, d1, init])

        def ld(out, in_, eng="sp", slow=False):
            tk.dma(out, in_, track="L_" + basename[out.tensor.name], outs=[out], eng=("pool" if eng == "pool" else "sp"), slow=slow)

        def st(out, in_, eng="pool", key=None, slow=False):
            tk.dma(out, in_, track="S_" + basename[in_.tensor.name], ins=[in_], outs=([key] if key else []), eng=("pool" if eng == "poolq" else "sp"), slow=slow)

        vstage = sb([128, 128], F32, "vstage")

        def vec_to_fm(dst, src2d, n):
            ld(vstage[0:n, :], src2d)
            pt = nps()
            tr(pt[:, 0:n], vstage[0:n, :], ident[0:n, 0:n])
            cp("dve", dst, pt[:, 0:n])

        rr = {"i": 0}

        def anyeng(choices=("dve", "pool", "act")):
            rr["i"] += 1
            return choices[rr["i"] % len(choices)]

        cst_f = sb([128, 768], F32, "cst_f")
        ld(cst_f[:], cst[:, :])
        ident = cst_f[:, 0:128]
        ident_b = sb([128, 128], BF16, "ident_b")
        cp("dve", ident_b[:], ident)
        maskf = sb([128, 128], BF16, "maskf")
        maskb = sb([128, 128], BF16, "maskb")
        cp("dve", maskf[:], cst_f[:, 128:256])
        cp("dve", maskb[:], cst_f[:, 256:384])
        negf = cst_f[:, 384:512]
        negb = cst_f[:, 512:640]
        iota_r = cst_f[:, 640:704]
        pidx = cst_f[:, 704:705]
        ones_b = sb([128, 128], BF16, "ones_b")
        mset("dve", ones_b[:], 1.0)
        ones_f = sb([128, 128], F32, "ones_f")
        mset("dve", ones_f[:], 1.0)
        zeros_f = sb([128, 32], F32, "zeros_f")
        mset("pool", zeros_f[:], 0.0)

        MOD = sb([128, DEPTH, 6, 8, 2], F32, "MOD")
        LB = sb([128, 2, 4, DEPTH], F32, "LB")
        OMLB = sb([128, 2, 4, DEPTH], F32, "OMLB")
        E = sb([128, 4, 64], F32, "Etab")
        cur_es[0] = pes
        cw_in = [sb([128, 2048], F32, f"cw_in{i}") for i in range(4)]
        cw_out = [sb([128, 2048], BF16, f"cw_out{i}") for i in range(4)]
        k = 0
        for nm, (src, dst, rows, cols) in (wb.items() if "A" in secs else []):
            for l in range(nlayers):
                for r0 in range(0, rows, 128):
                    for c0 in range(0, cols, 2048):
                        cc = min(2048, cols - c0)
                        a, b = cw_in[k % 4], cw_out[k % 4]
                        ld(a[:, 0:cc], src[l, r0:r0 + 128, c0:c0 + cc])
                        cp(("dve", "act")[k % 2], b[:, 0:cc], a[:, 0:cc])
                        st(dst[l, r0:r0 + 128, c0:c0 + cc], b[:, 0:cc], eng="poolq")
                        k += 1


        cT = sb([128, 8, 2], F32, "cT")
        sgc = sb([128, 8, 2], F32, "sgc")
        for j in range(2):
            vec_to_fm(cT[:, :, j], cvec[j, :].rearrange("(c p) -> c p", p=128), 8)
        act(sgc[:], cT[:], AF.Silu)
        badaT = sb([128, DEPTH, 48], F32, "badaT")
        n1T = sb([128, DEPTH, 8], F32, "n1T")
        n2T = sb([128, DEPTH, 8], F32, "n2T")
        for l in range(nlayers):
            vec_to_fm(badaT[:, l, :], b_ada[l, :].rearrange("(c p) -> c p", p=128), 48)
            vec_to_fm(n1T[:, l, :], norm1[l, :].rearrange("(c p) -> c p", p=128), 8)
            vec_to_fm(n2T[:, l, :], norm2[l, :].rearrange("(c p) -> c p", p=128), 8)
        wa = [sb([128, 8, 768], F32, f"wa{i}") for i in range(2)]
        k = 0
        for l in range(nlayers if "B" in secs else 0):
            for cb in range(8):
                w = wa[k % 2]
                k += 1
                for dc in range(8):
                    ld(w[:, dc, :], w_ada[l, dc * 128:(dc + 1) * 128, cb * 768:(cb + 1) * 768], eng=("sp" if dc % 2 == 0 else "act"))
                pt = nps()
                for fc in range(6):
                    for dc in range(8):
                        mm(pt[:, fc * 2:fc * 2 + 2], w[:, dc, fc * 128:(fc + 1) * 128], sgc[:, dc, :], start=(dc == 0), stop=(dc == 7))
                for fc in range(6):
                    ch = cb * 6 + fc
                    kind, c8 = ch // 8, ch % 8
                    ts("dve", MOD[:, l, kind, c8, :], pt[:, fc * 2:fc * 2 + 2], badaT[:, l, ch:ch + 1], None, ALU.add)
            for kind, nT in ((1, n1T), (4, n2T)):
                ts("dve", MOD[:, l, kind, :, :], MOD[:, l, kind, :, :], 1.0, None, ALU.add)
                tt("dve", MOD[:, l, kind, :, :], MOD[:, l, kind, :, :], nT[:, l, :].unsqueeze(2).to_broadcast([128, 8, 2]), ALU.mult)

        hlb = sb([128, 2, 4, DEPTH], F32, "hlb")
        for dr in range(2):
            for l in range(DEPTH):
                vec_to_fm(hlb[:, dr, :, l], h_lb[dr, l, :].rearrange("(h p) -> h p", p=128), 4)
        hmx = sb([128, 2, 4], F32, "hmx")
        red("dve", hmx[:], hlb[:], ALU.max)
        tt("dve", hlb[:], hlb[:], hmx[:].unsqueeze(3).to_broadcast([128, 2, 4, DEPTH]), ALU.subtract)
        act(hlb[:], hlb[:], AF.Exp)
        red("dve", hmx[:], hlb[:], ALU.add)
        tk.op("dve", lambda e: e.reciprocal(out=hmx[:], in_=hmx[:]), [hmx], [hmx])
        tt("dve", hlb[:], hlb[:], hmx[:].unsqueeze(3).to_broadcast([128, 2, 4, DEPTH]), ALU.mult)
        mset("dve", LB[:, :, :, 0], 0.0)
        for l in range(1, DEPTH):
            if l == 1:
                cp("dve", LB[:, :, :, 1], hlb[:, :, :, 1])
            else:
                tt("dve", LB[:, :, :, l], LB[:, :, :, l - 1], hlb[:, :, :, l], ALU.add)
        ts("dve", OMLB[:], LB[:], -1.0, 1.0, ALU.mult, ALU.add)

        fr = sb([128, 2], F32, "fr")
        LN1E4 = float(np.log(10000.0))
        act(fr[:, 0:1], pidx, AF.Exp, 0.0, -LN1E4 / 256.0)
        half_b = sb([128, 1], F32, "half_b")
        mset("dve", half_b[:], -LN1E4 * 0.5)
        act(fr[:, 1:2], pidx, AF.Exp, half_b[:, 0:1], -LN1E4 / 256.0)
        s1 = sb([128, 2], F32, "s1")
        c1 = sb([128, 2], F32, "c1")
        hpi = sb([128, 1], F32, "hpi")
        mset("dve", hpi[:], float(np.pi / 2))
        act(s1[:], fr[:], AF.Sin)
        act(c1[:], fr[:], AF.Sin, hpi[:, 0:1], 1.0)
        mset("dve", E[:, 0:2, 0], 0.0)
        mset("dve", E[:, 2:4, 0], 1.0)
        tmpa = sb([128, 2], F32, "tmpa")
        tmpb = sb([128, 2], F32, "tmpb")
        for r in range(1, 64 if "D" in secs else 1):
            tt("dve", tmpa[:], E[:, 2:4, r - 1], s1[:], ALU.mult)
            tt("dve", tmpb[:], E[:, 0:2, r - 1], s1[:], ALU.mult)
            tt("dve", E[:, 0:2, r], E[:, 0:2, r - 1], c1[:], ALU.mult)
            tt("dve", E[:, 0:2, r], E[:, 0:2, r], tmpa[:], ALU.add)
            tt("dve", E[:, 2:4, r], E[:, 2:4, r - 1], c1[:], ALU.mult)
            tt("dve", E[:, 2:4, r], E[:, 2:4, r], tmpb[:], ALU.subtract)

        xin_t = [sb([128, D], F32, f"xin{i}") for i in range(2)]
        xo_t = [sb([128, 8, 128], F32, f"xo{i}") for i in range(2)]
        for rt in range(NT // 128 if "E" in secs else 0):
            a, o = xin_t[rt % 2], xo_t[rt % 2]
            import os
            ld(a[:], x_in[rt * 128:(rt + 1) * 128, :], eng=("sp" if (rt % 2 == 0 or "spld" in os.environ.get("EDBG", "")) else "act"))
            for half in range(2):
                pt = nps()
                for c4 in range(4):
                    c = half * 4 + c4
                    tr(pt[:, c4 * 128:(c4 + 1) * 128], a[:, c * 128:(c + 1) * 128], ident)
                if rt < 8:
                    cp(anyeng(("dve",) if "dvecp" in os.environ.get("EDBG", "") else ("dve", "act")), o[:, half * 4:half * 4 + 4, :], pt[:].rearrange("p (c t) -> p c t", c=4))
                else:
                    r0 = ((rt - 8) * 128) // 64
                    if half == 0:
                        tt("dve", o[:, 0:4, :].rearrange("p c (r q) -> p c r q", r=2),
                           pt[:].rearrange("p (c r q) -> p c r q", c=4, r=2),
                           E[:, :, r0:r0 + 2].unsqueeze(3).to_broadcast([128, 4, 2, 64]), ALU.add)
                    else:
                        tt("dve", o[:, 4:8, :].rearrange("p c (r q) -> p c r q", r=2),
                           pt[:].rearrange("p (c r q) -> p c r q", c=4, r=2),
                           E[:, :, :].unsqueeze(2).to_broadcast([128, 4, 2, 64]), ALU.add)
            import os
            EDBG = os.environ.get("EDBG", "")
            if "nost" in EDBG:
                pass
            elif "spst" in EDBG:
                st(xT[:, :, rt * 128:(rt + 1) * 128].rearrange("c p t -> p c t"), o[:], eng="sp")
            elif "perc" in EDBG:
                for c in range(8):
                    st(xT[c, :, rt * 128:(rt + 1) * 128], o[:, c, :], eng="sp")
            else:
                st(xT[:, :, rt * 128:(rt + 1) * 128].rearrange("c p t -> p c t"), o[:])
        tk.barrier()


        pes.close()
        cur_es[0] = es
        psT = PS[7]
        psTb = psT[:].bitcast(BF16)
        PSF = PS[0:7]

        def npsf():
            psi["i"] += 1
            return PSF[psi["i"] % 7]

        mbiT = sb([8, DEPTH], F32, "mbiT")
        nmbfT = sb([8, DEPTH], F32, "nmbfT")
        dtbT = sb([32, DEPTH], F32, "dtbT")
        negaT = sb([32, DEPTH], F32, "negaT")
        for l in range(nlayers):
            ld(mbiT[0:8, l:l + 1], m_bi[l, :].rearrange("(p o) -> p o", o=1))
            ld(nmbfT[0:8, l:l + 1], m_bf[l, :].rearrange("(p o) -> p o", o=1))
            ld(dtbT[0:32, l:l + 1], s_dt_bias[l, :].rearrange("(p o) -> p o", o=1))
            ld(negaT[0:32, l:l + 1], s_a_log[l, :].rearrange("(p o) -> p o", o=1))
        ts("dve", nmbfT[:], nmbfT[:], -1.0, None, ALU.mult)
        act(negaT[:], negaT[:], AF.Exp)
        ts("dve", negaT[:], negaT[:], -1.0, None, ALU.mult)
        cwT = sb([128, DEPTH, 3, 16], F32, "cwT")
        cbT = sb([128, DEPTH, 16], F32, "cbT")
        for l in range(nlayers):
            for kk in range(3):
                vec_to_fm(cwT[:, l, kk, :], s_conv_w[l, kk, :].rearrange("(c p) -> c p", p=128), 16)
            vec_to_fm(cbT[:, l, :], s_conv_b[l, :].rearrange("(c p) -> c p", p=128), 16)

        B_ = {}

        def alloc_P(full=True):
            B_["xt_b"] = [sb([128, 8, TT], F32, f"xt{i}") for i in range(2)]
            B_["sq_b"] = sb([128, 8, TT], BF16, "sq")
            B_["rstd"] = sb([128, TT], F32, "rstd")
            B_["hn_f"] = sb([128, 8, TT], F32, "hn_f")
            B_["hT"] = sb([128, 8, TT], BF16, "hT")
            B_["wbuf"] = [sb([128, 8, 512], BF16, f"wbuf{i}") for i in range(3)]
            if not full:
                return
            B_["stg_b"] = [sb([128, 512], BF16, f"stgb{i}") for i in range(4)]
            B_["stg_f"] = [sb([128, 512], F32, f"stgf{i}") for i in range(4)]
            B_["acc2"] = [sb([128, 512], F32, f"acc_f{i}") for i in range(2)]
            B_["xs_stage"] = sb([128, 4, 1024], BF16, "xs_stage")
            B_["bm_stage"] = sb([128, 4, 512], BF16, "bm_stage")
            B_["u_b2"] = [sb([128, 512], BF16, f"u_b{i}") for i in range(2)]

        wk = {"i": 0}
        sgi = {"b": 0, "f": 0}

        def nstg(kind):
            sgi[kind] += 1
            return (B_["stg_b"] if kind == "b" else B_["stg_f"])[sgi[kind] % 4]

        def norm_mod(l, ti, kind_gam, kind_sh, xt, hT_out=None):
            cidx = 0 if ti < 2 else 1
            hT_out = hT_out if hT_out is not None else B_["hT"]
            act(B_["sq_b"][:], xt[:], AF.Square)
            pt = npsf()
            for c in range(8):
                mm(pt[:], ones_b[:], B_["sq_b"][:, c, :], start=(c == 0), stop=(c == 7))
            act(B_["rstd"][:], pt[:], AF.Ln, EPS, 1.0 / D)
            act(B_["rstd"][:], B_["rstd"][:], AF.Exp, 0.0, -0.5)
            tt("dve", B_["hn_f"][:], xt[:], B_["rstd"][:].unsqueeze(1).to_broadcast([128, 8, TT]), ALU.mult)
            for c in range(8):
                act(hT_out[:, c, :], B_["hn_f"][:, c, :], AF.Identity, MOD[:, l, kind_sh, c, cidx:cidx + 1], MOD[:, l, kind_gam, c, cidx:cidx + 1])

        def load_w(name, l, r0, nrows, c0, ncols):
            wk["i"] += 1
            w = B_["wbuf"][wk["i"] % 3]
            src = wb[name][1]
            ld(w[:, 0:nrows // 128, 0:ncols], src[l, r0:r0 + nrows, c0:c0 + ncols].rearrange("(dc p) n -> p dc n", p=128), eng="pool")
            return w

        def fm_block(w, j0, m, pt_rows=128):
            pt = npsf()
            for dc in range(8):
                mm(pt[0:m, :], w[:, dc, j0:j0 + m], B_["hT"][:, dc, :], start=(dc == 0), stop=(dc == 7))
            return pt

        def phase_P(l):
            pst = ExitStack()
            cur_es[0] = pst
            alloc_P()
            hTs = [B_["hT"], sb([128, 8, TT], BF16, "hT2")]
            ld(B_["xt_b"][0][:], xT[:, :, 0:TT].rearrange("c p t -> p c t"))
            norm_mod(l, 0, 1, 0, B_["xt_b"][0], hTs[0])
            for ti in range(NTT):
                t0 = ti * TT
                rowlen = 256 if ti < 2 else 64
                nrow = TT // rowlen
                B_["hT"] = hTs[ti % 2]
                if ti + 1 < NTT:
                    ld(B_["xt_b"][(ti + 1) % 2][:], xT[:, :, t0 + TT:t0 + 2 * TT].rearrange("c p t -> p c t"))
                for (c0, nch, dst, func, scale) in ((C_MQ, 4, S["qTm"], AF.Copy, 1.0), (C_MK, 4, S["kTm"], AF.Copy, 128 ** -0.5),
                                                    (C_HQ, 4, S["hqT"], AF.Copy, 1.0),
                                                    (C_BG, 4, S["bgT"], AF.Sigmoid, 1.0), (C_BG + 512, 4, S["bgT"], AF.Sigmoid, 1.0),
                                                    (C_BG + 1024, 4, S["bgT"], AF.Sigmoid, 1.0), (C_BG + 1536, 4, S["bgT"], AF.Sigmoid, 1.0),
                                                    (C_BG + 2048, 4, S["bgT"], AF.Sigmoid, 1.0), (C_BG + 2560, 4, S["bgT"], AF.Sigmoid, 1.0)):
                    w = load_w("in", l, 0, D, c0, nch * 128)
                    rbase = (c0 - C_BG) if c0 >= C_BG else 0
                    for j in range(nch):
                        pt = fm_block(w, j * 128, 128)
                        sg = nstg("b")
                        act(sg[:], pt[:], func, 0.0, scale)
                        st(dst[rbase + j * 128:rbase + (j + 1) * 128, t0:t0 + TT], sg[:])
                for (c0, ncol, dst, func, scale) in ((C_MK, 512, S["k_tm"], AF.Copy, 128 ** -0.5), (C_MV, 512, S["v_tm"], AF.Copy, 1.0),
                                                     (C_MO, 512, S["go_tm"], AF.Sigmoid, 1.0), (C_HI, 512, S["hv_tm"], AF.Copy, 1.0),
                                                     (C_HG, 512, S["hg_tm"], AF.Silu, 1.0), (C_SZ, 512, S["sz_tm"], AF.Silu, 1.0),
                                                     (C_SZ + 512, 512, S["sz_tm"], AF.Silu, 1.0)):
                    w = load_w("in", l, 0, D, c0, 512)
                    cb0 = 512 if c0 == C_SZ + 512 else 0
                    for s4 in range(4):
                        pt = npsf()
                        for dc in range(8):
                            mm(pt[:], B_["hT"][:, dc, s4 * 128:(s4 + 1) * 128], w[:, dc, 0:512], start=(dc == 0), stop=(dc == 7))
                        sg = nstg("b")
                        act(sg[:], pt[:], func, 0.0, scale)
                        st(dst[t0 + s4 * 128:t0 + (s4 + 1) * 128, cb0:cb0 + 512], sg[:])
                if ti + 1 < NTT:
                    norm_mod(l, ti + 1, 1, 0, B_["xt_b"][(ti + 1) % 2], hTs[(ti + 1) % 2])
                w = load_w("in", l, 0, D, C_MI, 16)
                pt = fm_block(w, 0, 8)
                sg = nstg("f")
                ts("dve", sg[0:8, :], pt[0:8, :], mbiT[0:8, l:l + 1], None, ALU.add)
                st(S["gi"][:, t0:t0 + TT], sg[0:8, :])
                pt = fm_block(w, 8, 8)
                sg = nstg("f")
                act(sg[0:8, :], pt[0:8, :], AF.Exp, nmbfT[0:8, l:l + 1], -1.0)
                act(sg[0:8, :], sg[0:8, :], AF.Ln, 1.0, 1.0)
                ts("dve", sg[0:8, :], sg[0:8, :], -1.0, None, ALU.mult)
                st(S["glf"][:, t0:t0 + TT], sg[0:8, :])
                w = load_w("in", l, 0, D, C_SDT, 32)
                pt = fm_block(w, 0, 32)
                sg = nstg("f")
                act(sg[0:32, :], pt[0:32, :], AF.Exp, dtbT[0:32, l:l + 1], 1.0)
                act(sg[0:32, :], sg[0:32, :], AF.Ln, 1.0, 1.0)
                st(S["dtT"][:, t0:t0 + TT], sg[0:32, :])
                for half in range(2):
                    w = load_w("in", l, 0, D, C_HF + half * 512, 512)
                    for j in range(4):
                        pt = fm_block(w, j * 128, 128)
                        sg = nstg("f")
                        act(sg[:], pt[:], AF.Sigmoid)
                        ts("dve", sg[:], sg[:], OMLB[:, half, j, l:l + 1], LB[:, half, j, l:l + 1], ALU.mult, ALU.add)
                        sgk = nstg("b")
                        ts("pool", sgk[:], sg[:], -1.0, 1.0, ALU.mult, ALU.add)
                        st(S["kkT"][half * 512 + j * 128:half * 512 + (j + 1) * 128, t0:t0 + TT], sgk[:])
                        sgl = nstg("f")
                        act(sgl[:], sg[:], AF.Ln)
                        st(S["lfT"][half * 512 + j * 128:half * 512 + (j + 1) * 128, t0:t0 + TT], sgl[:])
                pend = [None]

                def make_tail(ch, ub):
                    def tail():
                        for s4 in range(4):
                            tr(psTb[:, s4 * 128:(s4 + 1) * 128], ub[:, s4 * 128:(s4 + 1) * 128], ident_b[:])
                        if ch < 8:
                            cp("dve", B_["xs_stage"][:, :, ch * 128:(ch + 1) * 128], psTb[:, 0:512].rearrange("p (s f) -> p s f", s=4))
                        else:
                            cp("dve", B_["bm_stage"][:, :, (ch - 8) * 128:(ch - 7) * 128], psTb[:, 0:512].rearrange("p (s f) -> p s f", s=4))
                    return tail
                for q in range(4):
                    w = load_w("in", l, 0, D, C_SX + q * 512, 512)
                    for j in range(4):
                        ch = q * 4 + j
                        pt = fm_block(w, j * 128, 128)
                        if pend[0] is not None:
                            pend[0]()
                            pend[0] = None
                        acc = B_["acc2"][ch % 2]
                        p3 = pt[:].rearrange("p (r q) -> p r q", q=rowlen)
                        a3 = acc[:].rearrange("p (r q) -> p r q", q=rowlen)
                        ts("dve", acc[:], pt[:], cwT[:, l, 1, ch:ch + 1], None, ALU.mult)
                        stt("dve", a3[:, :, 1:rowlen], p3[:, :, 0:rowlen - 1], cwT[:, l, 0, ch:ch + 1], a3[:, :, 1:rowlen], ALU.mult, ALU.add)
                        stt("dve", a3[:, :, 0:rowlen - 1], p3[:, :, 1:rowlen], cwT[:, l, 2, ch:ch + 1], a3[:, :, 0:rowlen - 1], ALU.mult, ALU.add)
                        if ch < 12:
                            ub = B_["u_b2"][ch % 2]
                            act(ub[:], acc[:], AF.Silu, cbT[:, l, ch:ch + 1], 1.0)
                            if ch >= 8:
                                st(S["bmT"][(ch - 8) * 128:(ch - 7) * 128, t0:t0 + TT], ub[:])
                            pend[0] = make_tail(ch, ub)
                        else:
                            sg = nstg("b")
                            act(sg[:], acc[:], AF.Silu, cbT[:, l, ch:ch + 1], 1.0)
                            st(S["cmT"][(ch - 12) * 128:(ch - 11) * 128, t0:t0 + TT], sg[:])
                if pend[0] is not None:
                    pend[0]()
                    pend[0] = None
                st(S["xs_tm"][t0:t0 + TT, :].rearrange("(s p) f -> p s f", p=128), B_["xs_stage"][:])
                st(S["bm_tm"][t0:t0 + TT, :].rearrange("(s p) f -> p s f", p=128), B_["bm_stage"][:])
            tk.barrier()
            pst.close()
            cur_es[0] = es


        def row_bcast(dst, src_row, n):
            r1 = B_["r1"]
            for c0 in range(0, n, 256):
                cc = min(256, n - c0)
                ld(r1[0:1, 0:cc], src_row[:, c0:c0 + cc])
                pt = npsf()
                mm(pt[:, 0:cc], ones_f[0:1, :], r1[0:1, 0:cc])
                cp("dve", dst[:, c0:c0 + cc], pt[:, 0:cc])

        def rms_rows(eng, out_rst, x3, nh, dsz, tmp):
            tt(eng, tmp, x3, x3, ALU.mult)
            red("dve", out_rst, tmp, ALU.add)
            act(out_rst, out_rst, AF.Ln, EPS, 1.0 / dsz)
            act(out_rst, out_rst, AF.Exp, 0.0, -0.5)

        def run_gens(gens):
            gens = list(gens)
            while gens:
                for g_ in list(gens):
                    try:
                        next(g_)
                    except StopIteration:
                        gens.remove(g_)

        def phase_S(l):
            pst = ExitStack()
            cur_es[0] = pst
            LM = 4096
            GCH = 256
            B_["r1"] = sb([1, 256], F32, "r1")
            GA = sb([36, LM], F32, "GA")
            GB = sb([36, LM], F32, "GB")
            GC = sb([36, GCH], F32, "GC")
            gtot = sb([36, 1], F32, "gtot")
            gcar = sb([36, 1], F32, "gcar")
            cmx = sb([36, 32], F32, "cmx")
            Gt = sb([36, 32], F32, "Gt")
            Gp = sb([36, 32], F32, "Gp")
            fs = sb([36, 32], F32, "fs")
            fsm = sb([36, 32], F32, "fsm")
            m0t = sb([36, 1], F32, "m0t")
            mfin = sb([36, 1], F32, "mfin")
            fsb = sb([128, 8, 32], F32, "fsb")
            gsc = sb([128, 16], F32, "gsc")
            mnb = sb([128, 512], F32, "mnb")
            hnb = sb([128, 512], F32, "hnb")
            snb = sb([128, 1024], F32, "snb")
            sdb = sb([128, 16], F32, "sdb")
            row_bcast(mnb, m_norm[l:l + 1, :], 512)
            row_bcast(hnb, h_norm[l:l + 1, :], 512)
            row_bcast(snb, s_norm[l:l + 1, :], 1024)
            row_bcast(sdb, s_d[l:l + 1, :], 16)
            nega2 = sb([16, 2], F32, "nega2")
            for dr in range(2):
                ld(nega2[0:16, dr:dr + 1], s_a_log[l, dr * 16:(dr + 1) * 16].rearrange("(p o) -> p o", o=1))
            act(nega2[:], nega2[:], AF.Exp)
            ts("dve", nega2[:], nega2[:], -1.0, None, ALU.mult)
            PM0, PM1 = PS[0], PS[1]
            PH0, PH1, PH2 = PS[2], PS[3], PS[4]
            PSS = [PS[5], PS[6]]
            ssi = {"i": 0}

            def nb_s():
                ssi["i"] += 1
                return PSS[ssi["i"] % 2]

            qT_t = [sb([128, 4, 128], BF16, f"qT_t{i}") for i in range(2)]
            kT_t = [sb([128, 4, 128], BF16, f"kT_t{i}") for i in range(2)]
            ktm_t = [sb([128, 512], BF16, f"ktm_t{i}") for i in range(2)]
            vtm_t = [sb([128, 512], BF16, f"vtm_t{i}") for i in range(2)]
            gate_m = sb([128, 512], BF16, "gate_m")
            vaug = sb([128, 4, 129], BF16, "vaug")
            PTm = sb([128, 4, 128], BF16, "PTm")
            C32 = [[sb([128, 129], F32, f"C32_{dr}{h}") for h in range(4)] for dr in range(2)]
            Cb = [[sb([128, 129], BF16, f"Cb_{dr}{h}") for h in range(4)] for dr in range(2)]
            dn = sb([128, 4], F32, "dn")
            yblm = sb([128, 512], F32, "yblm")
            ycm = sb([128, 512], F32, "ycm")
            tmpm = yblm
            ybfm = sb([128, 512], BF16, "ybfm")
            yTm = sb([128, 4, 128], BF16, "yTm")
            rstm = sb([128, 4], F32, "rstm")
            hq_t = [sb([128, 4, 128], BF16, f"hq_t{i}") for i in range(2)]
            kk_t = [sb([128, 4, 128], BF16, f"kk_t{i}") for i in range(2)]
            lf_t = [sb([128, 4, 128], F32, f"lf_t{i}") for i in range(2)]
            hvc = [[sb([32, 512], BF16, f"hvc{i}_{c}") for c in range(4)] for i in range(2)]
            gate_h = sb([128, 512], BF16, "gate_h")
            cp_t = sb([128, 4, 128], F32, "cp_t")
            dd_t = sb([128, 4, 128], F32, "dd_t")
            ep_t = sb([128, 4, 128], F32, "ep_t")
            em_t = sb([128, 4, 128], F32, "em_t")
            qt_t = sb([128, 4, 128], BF16, "qt_t")
            kt_t = sb([128, 4, 128], BF16, "kt_t")
            kt2 = sb([128, 4, 128], BF16, "kt2")
            qt2p = [sb([128, 4, 128], BF16, f"qt2p{c}") for c in range(4)]
            refg = sb([128, 4, 4], F32, "refg")
            egm = sb([128, 4, 4], F32, "egm")
            egd = sb([128, 4, 4], F32, "egd")
            egt = sb([128, 4, 4], F32, "egt")
            PThp = sb([32, 16, 128], BF16, "PThp")
            kttm = sb([32, 16, 128], BF16, "kttm")
            SA = [sb([128, 4, 128], F32, f"SA{dr}") for dr in range(2)]
            SB = sb([128, 4, 128], F32, "SBh")
            Sbb = [sb([128, 4, 128], BF16, f"Sbb{c}") for c in range(2)]
            yblh = sb([128, 512], F32, "yblh")
            ych = sb([128, 512], F32, "ych")
            tmph = yblh
            ybfh = sb([128, 512], BF16, "ybfh")
            yTh = sb([128, 4, 128], BF16, "yTh")
            rsth = sb([128, 4], F32, "rsth")
            mset("pool", PThp[:], 0.0)
            zb = sb([1, 512], BF16, "zb")
            mset("pool", zb[:], 0.0)
            for c in range(4):
                mset("pool", qt2p[c][:], 0.0)
            dta = [sb([16, 128], F32, f"dta{i}") for i in range(2)]
            cm_t = [sb([128, 4, 128], BF16, f"cm_t{i}") for i in range(2)]
            bmT_t = [sb([128, 4, 128], BF16, f"bmT_t{i}") for i in range(2)]
            bmtm_t = [sb([128, 512], BF16, f"bmtm_t{i}") for i in range(2)]
            xs_t = [sb([128, 1024], BF16, f"xs_t{i}") for i in range(2)]
            gate_s = sb([128, 1024], BF16, "gate_s")
            da = sb([16, 128], F32, "da")
            lcs = sb([16, 128], F32, "lcs")
            fm4 = sb([16, 4, 128], F32, "fm4")
            gs2 = [sb([128, 4, 16], F32, f"gs{i}") for i in range(2)]
            ngla = sb([128, 16], F32, "ngla")
            labd = [sb([16, 4, 128], F32, f"labd{i}") for i in range(2)]
            elae = sb([16, 1], F32, "elae")
            ebd = sb([16, 16], F32, "ebd")
            eb2 = [sb([128, 16], F32, f"eb{i}") for i in range(2)]
            cb_all = sb([128, 4, 128], F32, "cb_all")
            ddm = [sb([128, 4, 128], F32, f"ddm{i}") for i in range(2)]
            PTs2 = [[sb([128, 4, 128], BF16, f"PTs{p}_{i}") for i in range(4)] for p in range(2)]
            xp2 = [sb([128, 16, 64], BF16, f"xp{i}") for i in range(2)]
            xpp2 = [sb([128, 1024], BF16, f"xpp{i}") for i in range(2)]
            pis = [sb([128, 256], F32, f"pis{i}") for i in range(2)]
            hT32 = [[sb([128, 4, 64], F32, f"hT32_{dr}{g}") for g in range(4)] for dr in range(2)]
            hTb = [[sb([128, 256], BF16, f"hTb_{dr}{g}") for g in range(4)] for dr in range(2)]
            tmps = sb([128, 4, 64], F32, "tmps")
            sst = sb([64, 128], F32, "sst")
            ybls = sb([128, 1024], F32, "ybls")
            ycs = sb([128, 1024], F32, "ycs")
            tmpS = ybls
            ybfs = sb([128, 1024], BF16, "ybfs")
            yTs = sb([128, 8, 128], BF16, "yTs")
            rsts = sb([128, 1], F32, "rsts")

            NC, CS = 4, 32

            def loads_m(dr, tb, par):
                ld(qT_t[par][:], S["qTm"][:, tb:tb + 128].rearrange("(h d) t -> d h t", d=128))
                ld(kT_t[par][:], S["kTm"][:, tb:tb + 128].rearrange("(h d) t -> d h t", d=128))
                ld(ktm_t[par][:], S["k_tm"][tb:tb + 128, :])
                ld(vtm_t[par][:], S["v_tm"][tb:tb + 128, :])

            def loads_h(dr, tb, par):
                ld(lf_t[par][:], S["lfT"][dr * 512:(dr + 1) * 512, tb:tb + 128].rearrange("(h d) t -> d h t", d=128))
                ld(kk_t[par][:], S["kkT"][dr * 512:(dr + 1) * 512, tb:tb + 128].rearrange("(h d) t -> d h t", d=128))
                ld(hq_t[par][:], S["hqT"][:, tb:tb + 128].rearrange("(h d) t -> d h t", d=128))
                for c2 in range(NC):
                    ld(hvc[par][c2][:], S["hv_tm"][tb + c2 * CS:tb + (c2 + 1) * CS, :])

            def loads_s(dr, tb, par):
                ld(dta[par][:], S["dtT"][dr * 16:(dr + 1) * 16, tb:tb + 128])
                ld(cm_t[par][:], S["cmT"][:, tb:tb + 128].rearrange("(g n) t -> n g t", n=128))
                ld(bmT_t[par][:], S["bmT"][:, tb:tb + 128].rearrange("(g n) t -> n g t", n=128))
                ld(bmtm_t[par][:], S["bm_tm"][tb:tb + 128, :])
                ld(xs_t[par][:], S["xs_tm"][tb:tb + 128, :])

            for sq, (s0, L, rowlen, cidx) in enumerate(SEQS):
                nb = L // 128
                is_p = cidx == 0
                for dr in range(2):
                    p0 = 32 * dr
                    P = slice(p0, p0 + 4)
                    ld(GA[P, 0:L], S["gi"][dr * 4:dr * 4 + 4, s0:s0 + L])
                    ld(GB[P, 0:L], S["glf"][dr * 4:dr * 4 + 4, s0:s0 + L])
                    if is_p:
                        mset("dve", m0t[P, :], 0.0)
                    else:
                        ld(m0t[P, 0:1], st_m[l, dr, :].rearrange("(p o) -> p o", o=1))
                    if dr == 1:
                        red("dve", gtot[P, :], GB[P, 0:L], ALU.add)
                    for c0 in range(0, L, GCH):
                        cc = min(GCH, L - c0)
                        init = 0.0 if c0 == 0 else gcar[P, 0:1]
                        scan(GC[P, 0:cc], ones_f[P, 0:1].to_broadcast([4, cc]), GB[P, c0:c0 + cc], init, ALU.mult, ALU.add)
                        if c0 + cc < L:
                            cp("dve", gcar[P, :], GC[P, cc - 1:cc])
                        if dr == 0:
                            cp("dve", GB[P, c0:c0 + cc], GC[P, 0:cc])
                        else:
                            ts("dve", GB[P, c0:c0 + cc], GB[P, c0:c0 + cc], gtot[P, 0:1], None, ALU.add)
                            tt("dve", GB[P, c0:c0 + cc], GB[P, c0:c0 + cc], GC[P, 0:cc], ALU.subtract)
                    tt("dve", GA[P, 0:L], GA[P, 0:L], GB[P, 0:L], ALU.subtract)
                    red("dve", cmx[P, 0:nb], GA[P, 0:L].rearrange("p (c t) -> p c t", t=128), ALU.max)
                    if dr == 0:
                        scan(Gt[P, 0:nb], zeros_f[P, 0:nb], cmx[P, 0:nb], m0t[P, 0:1], ALU.add, ALU.max)
                        cp("dve", Gp[P, 0:1], m0t[P, 0:1])
                        if nb > 1:
                            cp("dve", Gp[P, 1:nb], Gt[P, 0:nb - 1])
                    else:
                        for c in range(nb - 1, -1, -1):
                            prev = m0t[P, 0:1] if c == nb - 1 else Gt[P, c + 1:c + 2]
                            tt("dve", Gt[P, c:c + 1], prev, cmx[P, c:c + 1], ALU.max)
                        cp("dve", Gp[P, nb - 1:nb], m0t[P, 0:1])
                        if nb > 1:
                            cp("dve", Gp[P, 0:nb - 1], Gt[P, 1:nb])
                    tt("dve", fs[P, 0:nb], Gp[P, 0:nb], Gt[P, 0:nb], ALU.subtract)
                    act(fs[P, 0:nb], fs[P, 0:nb], AF.Exp)
                    if dr == 0:
                        tt("dve", mfin[P, :], GB[P, L - 1:L], Gt[P, nb - 1:nb], ALU.add)
                    else:
                        tt("dve", mfin[P, :], GB[P, 0:1], Gt[P, 0:1], ALU.add)
                    if is_p:
                        st(o_m[sq, l, dr, :].rearrange("(p o) -> p o", o=1), mfin[P, :])
                    gb3 = Gt[P, 0:nb].unsqueeze(2).to_broadcast([4, nb, 128])
                    tt("dve", GA[P, 0:L].rearrange("p (c t) -> p c t", t=128), GA[P, 0:L].rearrange("p (c t) -> p c t", t=128), gb3, ALU.subtract)
                    act(GA[P, 0:L], GA[P, 0:L], AF.Exp)
                    tt("dve", GB[P, 0:L].rearrange("p (c t) -> p c t", t=128), GB[P, 0:L].rearrange("p (c t) -> p c t", t=128), gb3, ALU.add)
                    act(GB[P, 0:L], GB[P, 0:L], AF.Exp, 0.0, -1.0)
                    for h in range(4):
                        ts("dve", fsm[P, 0:nb], fs[P, 0:nb], ident[P, p0 + h:p0 + h + 1], None, ALU.mult)
                        pt = PSS[h % 2]
                        mm(pt[:, 0:nb], ones_f[P, :], fsm[P, 0:nb])
                        cp("dve", fsb[:, dr * 4 + h, 0:nb], pt[:, 0:nb])
                for dr in range(2):
                    for h in range(4):
                        if is_p:
                            mset("pool", C32[dr][h][:], 0.0)
                        else:
                            ld(C32[dr][h][:, 0:128], st_c[l, dr, h, :, :])
                            ld(C32[dr][h][:, 128:129], st_n[l, dr, h, :].rearrange("(p o) -> p o", o=1))
                        b0 = 0 if dr == 0 else nb - 1
                        ts("dve", C32[dr][h][:], C32[dr][h][:], fsb[:, dr * 4 + h, b0:b0 + 1], None, ALU.mult)
                        cp("pool", Cb[dr][h][:], C32[dr][h][:])
                    if is_p:
                        mset("pool", SA[dr][:], 0.0)
                    else:
                        for h in range(4):
                            ld(SA[dr][:, h, :], st_h[l, dr, h, :, :])
                    for g in range(4):
                        if is_p:
                            mset("pool", hT32[dr][g][:], 0.0)
                        else:
                            for hh in range(4):
                                ld(sst[:], st_s[l, dr, g * 4 + hh, :, :])
                                pt = PSS[hh % 2]
                                tr(pt[:, 0:64], sst[:], ident[0:64, 0:64])
                                cp("dve", hT32[dr][g][:, hh, :], pt[:, 0:64])
                        cp("pool", hTb[dr][g][:].rearrange("p (h e) -> p h e", h=4), hT32[dr][g][:])
                iters = [(1, b) for b in range(nb - 1, -1, -1)] + [(0, b) for b in range(nb)]

                def gen_m(dr, b, par):
                    p0 = 32 * dr
                    P = slice(p0, p0 + 4)
                    mask = maskf if dr == 0 else maskb
                    tb = s0 + b * 128
                    last = (b == nb - 1) if dr == 0 else (b == 0)
                    bnext = b + 1 if dr == 0 else b - 1
                    ykey = f"yb_{sq}_{b}"
                    qT, kT, ktm, vtm = qT_t[par], kT_t[par], ktm_t[par], vtm_t[par]
                    if dr == 0:
                        tk.dma(yblm[:], S["yb_all"][tb:tb + 128, 0:512], track="L_yblm", outs=[yblm], ins=[ykey + "m"])
                        ld(gate_m[:], S["go_tm"][tb:tb + 128, :])
                    tr(PM1[:, 0:4], GA[P, b * 128:(b + 1) * 128], ident[P, p0:p0 + 4])
                    tr(PM1[:, 4:8], GB[P, b * 128:(b + 1) * 128], ident[P, p0:p0 + 4])
                    cp("dve", gsc[:, 0:8], PM1[:, 0:8])
                    yield
                    tt("pool", vaug[:, :, 0:128], vtm[:].rearrange("p (h e) -> p h e", h=4), gsc[:, 0:4].unsqueeze(2).to_broadcast([128, 4, 128]), ALU.mult)
                    cp("pool", vaug[:, :, 128], gsc[:, 0:4])
                    for h in range(4):
                        mm(PM0[:, h * 128:(h + 1) * 128], kT[:, h, :], qT[:, h, :])
                    yield
                    tt("dve", PTm[:], PM0[:].rearrange("p (h t) -> p h t", h=4), mask[:].unsqueeze(1).to_broadcast([128, 4, 128]), ALU.mult)
                    yield

                    def pov(h, n):
                        return (PM1 if h < 2 else PM0)[:, (h % 2) * 256:(h % 2) * 256 + n]
                    for h in range(4):
                        mm(pov(h, 129), PTm[:, h, :], vaug[:, h, :], start=True, stop=False)
                        mm(pov(h, 129), qT[:, h, :], Cb[dr][h][:], start=False, stop=True)
                    yield
                    for h in range(4):
                        act(dn[:, h:h + 1], pov(h, 129)[:, 128:129], AF.Abs)
                    yield
                    tt("dve", dn[:], dn[:], gsc[:, 4:8], ALU.max)
                    tk.op("dve", lambda e: e.reciprocal(out=dn[:], in_=dn[:]), [dn], [dn])
                    yield
                    for h in range(4):
                        if dr == 1:
                            act(ycm[:, h * 128:(h + 1) * 128], pov(h, 128), AF.Copy, 0.0, dn[:, h:h + 1])
                        else:
                            stt("dve", ycm[:, h * 128:(h + 1) * 128], pov(h, 128), dn[:, h:h + 1], yblm[:, h * 128:(h + 1) * 128], ALU.mult, ALU.add)
                    yield
                    for h in range(4):
                        pd = (PM1 if h < 2 else PM0)[:, (h % 2) * 256:(h % 2) * 256 + 129]
                        mm(pd, ktm[:, h * 128:(h + 1) * 128], vaug[:, h, :])
                    yield
                    for h in range(4):
                        pd = (PM1 if h < 2 else PM0)[:, (h % 2) * 256:(h % 2) * 256 + 129]
                        tt("dve", C32[dr][h][:], C32[dr][h][:], pd, ALU.add)
                        if not last:
                            ts("pool", C32[dr][h][:], C32[dr][h][:], fsb[:, dr * 4 + h, bnext:bnext + 1], None, ALU.mult)
                            cp("pool", Cb[dr][h][:], C32[dr][h][:])
                        elif is_p:
                            st(o_c[sq, l, dr, h, :, :], C32[dr][h][:, 0:128])
                            st(o_n[sq, l, dr, h, :].rearrange("(p o) -> p o", o=1), C32[dr][h][:, 128:129])
                    yield
                    if dr == 1:
                        tk.dma(S["yb_all"][tb:tb + 128, 0:512], ycm[:], track="S_ycm", ins=[ycm], outs=[ykey + "m"])
                    else:
                        h3 = ycm[:].rearrange("p (h e) -> p h e", h=4)
                        rms_rows("pool", rstm[:, 0:4], h3, 4, 128, tmpm[:].rearrange("p (h e) -> p h e", h=4))
                        yield
                        tt("dve", h3, h3, rstm[:, 0:4].unsqueeze(2).to_broadcast([128, 4, 128]), ALU.mult)
                        tt("pool", ycm[:], ycm[:], mnb[:], ALU.mult)
                        yield
                        tt("dve", ybfm[:], ycm[:], gate_m[:], ALU.mult)
                        yield
                        for h in range(4):
                            tr(psTb[:, h * 128:(h + 1) * 128], ybfm[:, h * 128:(h + 1) * 128], ident_b[:])
                        cp("dve", yTm[:], psTb[:, 0:512].rearrange("p (h t) -> p h t", h=4))
                        st(S["ymT"][:, tb:tb + 128].rearrange("(h f) t -> f h t", f=128), yTm[:])

                def front_h(dr, b, par):
                    hq, kk, lf = hq_t[par], kk_t[par], lf_t[par]
                    for h in range(4):
                        scan(cp_t[:, h, :], ones_f[:, 0:128], lf[:, h, :], 0.0, ALU.mult, ALU.add)
                    c4 = cp_t[:].rearrange("p h (c t) -> p h c t", c=NC)
                    if dr == 0:
                        src4 = c4
                        cp("dve", refg[:], c4[:, :, :, 15])
                        cp("dve", egm[:, :, 0], c4[:, :, 0, 15])
                        tt("dve", egm[:, :, 1:NC], c4[:, :, 1:NC, 15], c4[:, :, 0:NC - 1, CS - 1], ALU.subtract)
                        tt("dve", egd[:], c4[:, :, :, CS - 1], c4[:, :, :, 15], ALU.subtract)
                        sgn = 1.0
                    else:
                        tt("dve", dd_t[:], cp_t[:], lf[:], ALU.subtract)
                        e4 = dd_t[:].rearrange("p h (c t) -> p h c t", c=NC)
                        src4 = e4
                        cp("dve", refg[:], e4[:, :, :, 16])
                        tt("dve", egm[:], c4[:, :, :, CS - 1], e4[:, :, :, 16], ALU.subtract)
                        tt("dve", egd[:], e4[:, :, :, 16], e4[:, :, :, 0], ALU.subtract)
                        sgn = -1.0
                    tt("dve", ep_t[:].rearrange("p h (c t) -> p h c t", c=NC), src4, refg[:].unsqueeze(3).to_broadcast([128, 4, NC, CS]), ALU.subtract)
                    ts("dve", ep_t[:], ep_t[:], -43.0, 43.0, ALU.max, ALU.min)
                    yield
                    act(egm[:], egm[:], AF.Exp)
                    act(egd[:], egd[:], AF.Exp)
                    act(em_t[:], ep_t[:], AF.Exp, 0.0, -sgn)
                    act(ep_t[:], ep_t[:], AF.Exp, 0.0, sgn)
                    yield
                    em2, ep2 = dd_t, cp_t
                    tt("dve", em2[:].rearrange("p h (c t) -> p h c t", c=NC), em_t[:].rearrange("p h (c t) -> p h c t", c=NC),
                       egd[:].unsqueeze(3).to_broadcast([128, 4, NC, CS]), ALU.mult)
                    tt("pool", kt_t[:], kk[:], em_t[:], ALU.mult)
                    tt("pool", qt_t[:], hq[:], ep_t[:], ALU.mult)
                    yield
                    tt("dve", kt2[:], kk[:], em2[:], ALU.mult)
                    tt("dve", ep2[:].rearrange("p h (c t) -> p h c t", c=NC), ep_t[:].rearrange("p h (c t) -> p h c t", c=NC),
                       egm[:].unsqueeze(3).to_broadcast([128, 4, NC, CS]), ALU.mult)

                def back_h(dr, b, par):
                    mask = maskf if dr == 0 else maskb
                    tb = s0 + b * 128
                    last = (b == nb - 1) if dr == 0 else (b == 0)
                    ykey = f"yb_{sq}_{b}"
                    hq, hv = hq_t[par], hvc[par]
                    ep2 = cp_t
                    order = list(range(NC)) if dr == 0 else list(range(NC - 1, -1, -1))
                    if dr == 0:
                        tk.dma(yblh[:], S["yb_all"][tb:tb + 128, 512:1024], track="L_yblh", outs=[yblh], ins=[ykey + "h"])
                        ld(gate_h[:], S["hg_tm"][tb:tb + 128, :])
                    tt("dve", egt[:], egm[:], egd[:], ALU.mult)
                    for c2 in range(NC):
                        tt("pool", qt2p[c2][:, :, c2 * CS:(c2 + 1) * CS], hq[:, :, c2 * CS:(c2 + 1) * CS], ep2[:, :, c2 * CS:(c2 + 1) * CS], ALU.mult)
                    for half in range(2):
                        for cc in range(2):
                            c2 = half * 2 + cc
                            for h in range(4):
                                j = c2 * 4 + h
                                mm(PH0[0:CS, j * CS:(j + 1) * CS], kt_t[:, h, c2 * CS:(c2 + 1) * CS], qt_t[:, h, c2 * CS:(c2 + 1) * CS])
                                jj = cc * 4 + h
                                tr(psTb[0:CS, jj * 128:(jj + 1) * 128], kt2[:, h, c2 * CS:(c2 + 1) * CS], ident_b[:])
                        cp("act", kttm[:, half * 8:(half + 1) * 8, :], psTb[0:CS, :].rearrange("p (j k) -> p j k", j=8))
                    yield
                    PT4 = PThp[:].rearrange("p (c h) t -> p c h t", c=NC)
                    pS4 = PH0[0:CS, :].rearrange("p (c h t) -> p c h t", c=NC, h=4)
                    for c2 in range(NC):
                        tt("dve", PT4[:, c2, :, c2 * CS:(c2 + 1) * CS], pS4[:, c2, :, :], mask[0:CS, 0:CS].unsqueeze(1).to_broadcast([CS, 4, CS]), ALU.mult)
                    PD = [PH1, PH2]

                    def emit_pd(i):
                        c2 = order[i]
                        for h in range(4):
                            mm(PD[i % 2][:, h * 128:(h + 1) * 128], kttm[:, c2 * 4 + h, :], hv[c2][:, h * 128:(h + 1) * 128])
                    emit_pd(0)
                    emit_pd(1)
                    yield
                    Scur, Soth = SA[dr], SB
                    mm(PH0[:, :], zb[0:1, 0:128], zb[0:1, 0:512], start=True, stop=False)
                    for i, c2 in enumerate(order):
                        cp("act", Sbb[i % 2][:], Scur[:])
                        for h in range(4):
                            j = c2 * 4 + h
                            mm(PH0[:, h * 128:(h + 1) * 128], PThp[:, j, :], hv[c2][:, h * 128:(h + 1) * 128], start=False, stop=False)
                            mm(PH0[:, h * 128:(h + 1) * 128], qt2p[c2][:, h, :], Sbb[i % 2][:, h, :], start=False, stop=(i == NC - 1))
                        tt("dve", Soth[:], Scur[:], egt[:, :, c2].unsqueeze(2).to_broadcast([128, 4, 128]), ALU.mult)
                        tt("dve", Soth[:], Soth[:], PD[i % 2][:].rearrange("p (h e) -> p h e", h=4), ALU.add)
                        Scur, Soth = Soth, Scur
                        if i + 2 < NC:
                            emit_pd(i + 2)
                        yield
                    assert Scur is SA[dr]
                    if last and is_p:
                        for h in range(4):
                            st(o_h[sq, l, dr, h, :, :], SA[dr][:, h, :])
                    if dr == 1:
                        cp("act", ych[:], PH0[:])
                        tk.dma(S["yb_all"][tb:tb + 128, 512:1024], ych[:], track="S_ych", ins=[ych], outs=[ykey + "h"])
                    else:
                        tt("dve", ych[:], PH0[:], yblh[:], ALU.add)
                        h3 = ych[:].rearrange("p (h e) -> p h e", h=4)
                        rms_rows("pool", rsth[:, 0:4], h3, 4, 128, tmph[:].rearrange("p (h e) -> p h e", h=4))
                        yield
                        tt("dve", h3, h3, rsth[:, 0:4].unsqueeze(2).to_broadcast([128, 4, 128]), ALU.mult)
                        tt("pool", ych[:], ych[:], hnb[:], ALU.mult)
                        yield
                        tt("dve", ybfh[:], ych[:], gate_h[:], ALU.mult)
                        yield
                        for h in range(4):
                            tr(psTb[:, h * 128:(h + 1) * 128], ybfh[:, h * 128:(h + 1) * 128], ident_b[:])
                        cp("dve", yTh[:], psTb[:, 0:512].rearrange("p (h t) -> p h t", h=4))
                        st(S["yhT"][:, tb:tb + 128].rearrange("(h f) t -> f h t", f=128), yTh[:])

                def front_s(dr, b, par):
                    negm = negf if dr == 0 else negb
                    dt_, cm, bmT, xs = dta[par], cm_t[par], bmT_t[par], xs_t[par]
                    gs, eb, xp, xpp, PTs = gs2[par], eb2[par], xp2[par], xpp2[par], PTs2[par]
                    ts("dve", da[:], dt_[:], nega2[:, dr:dr + 1], None, ALU.mult)
                    scan(lcs[:], ones_f[0:16, 0:128], da[:], 0.0, ALU.mult, ALU.add)
                    if dr == 0:
                        cp("dve", fm4[:, 0, :], lcs[:])
                        lae = fm4[:, 0, 127:128]
                    else:
                        stt("dve", fm4[:, 0, :], da[:], lcs[:, 127:128], lcs[:], ALU.add, ALU.subtract)
                        lae = fm4[:, 0, 0:1]
                    cp("dve", fm4[:, 1, :], dt_[:])
                    ts("dve", fm4[:, 3, :], fm4[:, 0, :], lae, -1.0, ALU.subtract, ALU.mult)
                    yield
                    act(fm4[:, 2, :], fm4[:, 0, :], AF.Exp)
                    act(elae[:], lae, AF.Exp)
                    act(fm4[:, 3, :], fm4[:, 3, :], AF.Exp)
                    yield
                    tt("dve", fm4[:, 3, :], fm4[:, 3, :], dt_[:], ALU.mult)
                    ts("dve", ebd[:], ident[0:16, 0:16], elae[:, 0:1], None, ALU.mult)
                    yield
                    pt = nb_s()
                    for q in range(4):
                        tr(pt[:, q * 16:(q + 1) * 16], fm4[:, q, :], ident[0:16, 0:16])
                    mm(pt[:, 64:80], ones_f[0:16, :], ebd[:])
                    cp("dve", gs[:], pt[:, 0:64].rearrange("p (q h) -> p q h", q=4))
                    cp("dve", eb[:], pt[:, 64:80])
                    ts("dve", ngla[:], gs[:, 0, :], -1.0, None, ALU.mult)
                    pcb = nb_s()
                    for g in range(4):
                        mm(pcb[:, g * 128:(g + 1) * 128], bmT[:, g, :], cm[:, g, :])
                    cp("act", cb_all[:], pcb[:].rearrange("p (g t) -> p g t", g=4))
                    yield
                    tt("pool", xp[:], xs[:].rearrange("p (h e) -> p h e", h=16), gs[:, 1, :].unsqueeze(2).to_broadcast([128, 16, 64]), ALU.mult)
                    tt("pool", xpp[:].rearrange("p (h e) -> p h e", h=16), xs[:].rearrange("p (h e) -> p h e", h=16), gs[:, 3, :].unsqueeze(2).to_broadcast([128, 16, 64]), ALU.mult)
                    for g in range(4):
                        lg = labd[g % 2]
                        tt("dve", lg[:], fm4[:, 0, :].unsqueeze(1).to_broadcast([16, 4, 128]), ident[0:16, g * 4:(g + 1) * 4].unsqueeze(2).to_broadcast([16, 4, 128]), ALU.mult)
                        pb = nb_s()
                        mm(pb[:], ones_f[0:16, :], lg[:])
                        dg = ddm[g % 2]
                        tt("dve", dg[:], pb[:].rearrange("p (h t) -> p h t", h=4), negm.unsqueeze(1).to_broadcast([128, 4, 128]), ALU.add)
                        yield
                        for hh in range(4):
                            act(dg[:, hh, :], dg[:, hh, :], AF.Exp, ngla[:, g * 4 + hh:g * 4 + hh + 1], 1.0)
                        yield
                        tt("dve", PTs[g][:], dg[:], cb_all[:, g, :].unsqueeze(1).to_broadcast([128, 4, 128]), ALU.mult)

                def back_s(dr, b, par):
                    tb = s0 + b * 128
                    last = (b == nb - 1) if dr == 0 else (b == 0)
                    ykey = f"yb_{sq}_{b}"
                    cm, bmtm, xs = cm_t[par], bmtm_t[par], xs_t[par]
                    gs, eb, xp, xpp, PTs = gs2[par], eb2[par], xp2[par], xpp2[par], PTs2[par]
                    if dr == 0:
                        tk.dma(ybls[:], S["yb_all"][tb:tb + 128, 1024:2048], track="L_ybls", outs=[ybls], ins=[ykey + "s"])
                        ld(gate_s[:], S["sz_tm"][tb:tb + 128, :])
                    for g in range(4):
                        pyi = nb_s()
                        for hh in range(4):
                            hd = g * 4 + hh
                            mm(pyi[:, hh * 64:(hh + 1) * 64], PTs[g][:, hh, :], xp[:, hd, :])
                        mm(pyi[:, 256:512], cm[:, g, :], hTb[dr][g][:])
                        pg = pis[g % 2]
                        tt("dve", pg[:].rearrange("p (h e) -> p h e", h=4), pyi[:, 256:512].rearrange("p (h e) -> p h e", h=4), gs[:, 2, g * 4:(g + 1) * 4].unsqueeze(2).to_broadcast([128, 4, 64]), ALU.mult)
                        yo = ycs[:, g * 256:(g + 1) * 256]
                        tt("dve", yo, pg[:], pyi[:, 0:256], ALU.add)
                        if g % 2 == 1:
                            yield
                    if dr == 0:
                        tt("pool", ycs[:], ycs[:], ybls[:], ALU.add)
                    for g in range(4):
                        pd = nb_s()
                        mm(pd[:, 0:256], bmtm[:, g * 128:(g + 1) * 128], xpp[:, g * 256:(g + 1) * 256])
                        tt("dve", tmps[:], hT32[dr][g][:], eb[:, g * 4:(g + 1) * 4].unsqueeze(2).to_broadcast([128, 4, 64]), ALU.mult)
                        tt("dve", hT32[dr][g][:], tmps[:], pd[:, 0:256].rearrange("p (h e) -> p h e", h=4), ALU.add)
                        if not last:
                            cp("pool", hTb[dr][g][:].rearrange("p (h e) -> p h e", h=4), hT32[dr][g][:])
                        elif is_p:
                            for hh in range(4):
                                pq = nb_s()
                                tr(pq[0:64, 0:128], hT32[dr][g][:, hh, :], ident[:, :])
                                cp("dve", sst[:], pq[0:64, 0:128])
                                st(o_s[sq, l, dr, g * 4 + hh, :, :], sst[:])
                        if g % 2 == 1:
                            yield
                    if dr == 1:
                        tk.dma(S["yb_all"][tb:tb + 128, 1024:2048], ycs[:], track="S_ycs", ins=[ycs], outs=[ykey + "s"])
                    else:
                        tt("pool", tmpS[:].rearrange("p (h e) -> p h e", h=16), xs[:].rearrange("p (h e) -> p h e", h=16), sdb[:].unsqueeze(2).to_broadcast([128, 16, 64]), ALU.mult)
                        yield
                        tt("dve", ycs[:], ycs[:], tmpS[:], ALU.add)
                        tt("dve", ycs[:], ycs[:], gate_s[:], ALU.mult)
                        rms_rows("pool", rsts[:, 0:1], ycs[:].unsqueeze(1), 1, 1024, tmpS[:].unsqueeze(1))
                        yield
                        ts("dve", ycs[:], ycs[:], rsts[:, 0:1], None, ALU.mult)
                        tt("dve", ybfs[:], ycs[:], snb[:], ALU.mult)
                        yield
                        for q in range(2):
                            for h in range(4):
                                j = q * 4 + h
                                tr(psTb[:, h * 128:(h + 1) * 128], ybfs[:, j * 128:(j + 1) * 128], ident_b[:])
                            cp("dve", yTs[:, q * 4:q * 4 + 4, :], psTb[:, 0:512].rearrange("p (h t) -> p h t", h=4))
                        st(S["ysT"][:, tb:tb + 128].rearrange("(h f) t -> f h t", f=128), yTs[:])

                def emit_loads(k):
                    dr_, b_ = iters[k]
                    tb_ = s0 + b_ * 128
                    loads_h(dr_, tb_, k % 2)
                    loads_s(dr_, tb_, k % 2)
                    loads_m(dr_, tb_, k % 2)

                emit_loads(0)
                run_gens([front_h(iters[0][0], iters[0][1], 0), front_s(iters[0][0], iters[0][1], 0)])
                for k, (dr, b) in enumerate(iters):
                    gl = [back_h(dr, b, k % 2), back_s(dr, b, k % 2), gen_m(dr, b, k % 2)]
                    if k + 1 < len(iters):
                        emit_loads(k + 1)
                        dn_, bn_ = iters[k + 1]
                        gl += [front_h(dn_, bn_, (k + 1) % 2), front_s(dn_, bn_, (k + 1) % 2)]
                    run_gens(gl)
            tk.barrier()
            pst.close()
            cur_es[0] = es

        def phase_O(l, final):
            pst = ExitStack()
            cur_es[0] = pst
            alloc_P(full=False)
            xt_b, hT = B_["xt_b"], B_["hT"]
            yT_t = sb([128, 16, TT], BF16, "yT_t")
            bg_t = sb([128, 8, TT], BF16, "bg_t")
            mrg = sb([128, 8, TT], BF16, "mrg")
            t1 = sb([128, TT], F32, "t1")
            t2 = sb([128, TT], F32, "t2")
            actT = sb([128, 22, TT], BF16, "actT")
            sa = sb([128, TT], F32, "sa")
            wdn = [sb([128, 22, 128], BF16, f"wdn{i}") for i in range(3)]
            wdi = {"i": 0}
            nfT = sb([128, 8], F32, "nfT")
            otm = sb([128, D], F32, "otm")
            if final:
                vec_to_fm(nfT[:, :], norm_f.rearrange("(c p) -> c p", p=128), 8)
            def ld_yT(t0_):
                ld(yT_t[:, 0:4, :], S["ymT"][:, t0_:t0_ + TT].rearrange("(c p) t -> p c t", p=128))
                ld(yT_t[:, 4:8, :], S["yhT"][:, t0_:t0_ + TT].rearrange("(c p) t -> p c t", p=128))
                ld(yT_t[:, 8:16, :], S["ysT"][:, t0_:t0_ + TT].rearrange("(c p) t -> p c t", p=128))

            for ti in range(NTT):
                t0 = ti * TT
                cidx = 0 if ti < 2 else 1
                xt = xt_b[ti % 2]
                if ti == 0:
                    ld(xt[:], xT[:, :, t0:t0 + TT].rearrange("c p t -> p c t"))
                    ld_yT(0)
                if ti + 1 < NTT:
                    ld(xt_b[(ti + 1) % 2][:], xT[:, :, t0 + TT:t0 + 2 * TT].rearrange("c p t -> p c t"))
                for bi, (wn, kc0, nk) in enumerate((("bm", 0, 4), ("bh", 4, 4), ("bs", 8, 8))):
                    ld(bg_t[:], S["bgT"][bi * 1024:(bi + 1) * 1024, t0:t0 + TT].rearrange("(c p) t -> p c t", p=128))
                    for half in range(2):
                        w = load_w(wn, l, 0, nk * 128, half * 512, 512)
                        for j in range(4):
                            fc = half * 4 + j
                            pt = npsf()
                            for kc in range(nk):
                                mm(pt[:], w[:, kc, j * 128:(j + 1) * 128], yT_t[:, kc0 + kc, :], start=(kc == 0), stop=(kc == nk - 1))
                            if bi == 0:
                                tt("dve", mrg[:, fc, :], pt[:], bg_t[:, fc, :], ALU.mult)
                            else:
                                tt("dve", t1[:], pt[:], bg_t[:, fc, :], ALU.mult)
                                tt("pool", mrg[:, fc, :], mrg[:, fc, :], t1[:], ALU.add)
                if ti + 1 < NTT:
                    ld_yT(t0 + TT)
                for half in range(2):
                    w = load_w("out", l, 0, D, half * 512, 512)
                    for j in range(4):
                        fc = half * 4 + j
                        pt = npsf()
                        for kc in range(8):
                            mm(pt[:], w[:, kc, j * 128:(j + 1) * 128], mrg[:, kc, :], start=(kc == 0), stop=(kc == 7))
                        stt("dve", xt[:, fc, :], pt[:], MOD[:, l, 2, fc, cidx:cidx + 1], xt[:, fc, :], ALU.mult, ALU.add)
                norm_mod(l, ti, 4, 3, xt)
                for fb in range(0, 22, 4):
                    nf = min(4, 22 - fb)
                    wa_ = load_w("gu", l, 0, D, fb * 128, nf * 128)
                    wb_ = load_w("gu", l, 0, D, D_FF + fb * 128, nf * 128)
                    for j in range(nf):
                        pa = npsf()
                        for kc in range(8):
                            mm(pa[:], wa_[:, kc, j * 128:(j + 1) * 128], hT[:, kc, :], start=(kc == 0), stop=(kc == 7))
                        pb = npsf()
                        for kc in range(8):
                            mm(pb[:], wb_[:, kc, j * 128:(j + 1) * 128], hT[:, kc, :], start=(kc == 0), stop=(kc == 7))
                        act(sa[:], pa[:], AF.Silu)
                        tt("dve", actT[:, fb + j, :], sa[:], pb[:], ALU.mult)
                for fc in range(8):
                    wdi["i"] += 1
                    wd = wdn[wdi["i"] % 3]
                    ld(wd[:], wb["down"][1][l, :, fc * 128:(fc + 1) * 128].rearrange("(kc p) n -> p kc n", p=128))
                    pt = npsf()
                    for kc in range(22):
                        mm(pt[:], wd[:, kc, :], actT[:, kc, :], start=(kc == 0), stop=(kc == 21))
                    stt("dve", xt[:, fc, :], pt[:], MOD[:, l, 5, fc, cidx:cidx + 1], xt[:, fc, :], ALU.mult, ALU.add)
                if not final:
                    st(xT[:, :, t0:t0 + TT].rearrange("c p t -> p c t"), xt[:])
                else:
                    act(B_["sq_b"][:], xt[:], AF.Square)
                    pt = npsf()
                    for c in range(8):
                        mm(pt[:], ones_b[:], B_["sq_b"][:, c, :], start=(c == 0), stop=(c == 7))
                    act(B_["rstd"][:], pt[:], AF.Ln, EPS, 1.0 / D)
                    act(B_["rstd"][:], B_["rstd"][:], AF.Exp, 0.0, -0.5)
                    tt("dve", B_["hn_f"][:], xt[:], B_["rstd"][:].unsqueeze(1).to_broadcast([128, 8, TT]), ALU.mult)
                    tt("dve", B_["hn_f"][:], B_["hn_f"][:], nfT[:].unsqueeze(2).to_broadcast([128, 8, TT]), ALU.mult)
                    for s4 in range(4):
                        for half in range(2):
                            pt = npsf()
                            for c4 in range(4):
                                c = half * 4 + c4
                                tr(pt[:, c4 * 128:(c4 + 1) * 128], B_["hn_f"][:, c, s4 * 128:(s4 + 1) * 128], ident)
                            cp("act" if half else "dve", otm[:, half * 512:(half + 1) * 512], pt[:])
                        st(y_out[t0 + s4 * 128:t0 + (s4 + 1) * 128, :], otm[:])
            tk.barrier()
            pst.close()
            cur_es[0] = es

        for l in range(nlayers):
            if "P" in secs:
                phase_P(l)
            if "S" in secs:
                phase_S(l)
            if "O" in secs:
                phase_O(l, l == nlayers - 1)

        tk.barrier()
        block = es.enter_context(nc.Block())
        tk.emit(block)
    return nc, dbg


def make_consts():
    c = np.zeros((128, 768), np.float32)
    c[:, 0:128] = np.eye(128)
    s = np.arange(128)[:, None]
    t = np.arange(128)[None, :]
    c[:, 128:256] = (s <= t)
    c[:, 256:384] = (s >= t)
    c[:, 384:512] = np.where(s <= t, 0.0, -1e30)
    c[:, 512:640] = np.where(s >= t, 0.0, -1e30)
    c[:, 640:704] = np.arange(64)[None, :]
    c[:, 704] = np.arange(128)
    return c


def make_in_maps(inputs):
    f = lambda a: np.ascontiguousarray(a, dtype=np.float32)
    shared = {k: f(inputs[k]) for k in ("w_ada", "b_ada", "norm1", "norm2", "w_in", "m_norm", "h_lb", "h_norm", "s_conv_w",
                                        "s_conv_b", "s_d", "s_norm", "w_bm", "w_bh", "w_bs", "w_out", "w_gu", "w_down", "norm_f")}
    shared["m_bi"] = f(inputs["m_bi"]).reshape(DEPTH, 8)
    shared["m_bf"] = f(inputs["m_bf"]).reshape(DEPTH, 8)
    shared["s_dt_bias"] = f(inputs["s_dt_bias"]).reshape(DEPTH, 32)
    shared["s_a_log"] = f(inputs["s_a_log"]).reshape(DEPTH, 32)
    shared["cst"] = make_consts()
    maps = []
    for i in range(8):
        m = dict(shared)
        m["x_in"] = np.concatenate([f(inputs["x_prompt"][4 * i:4 * i + 4]).reshape(1024, D), f(inputs["x_sample"][i])], 0)
        m["st_c"] = f(inputs["state_mlstm_c"][i])
        m["st_n"] = f(inputs["state_mlstm_n"][i])
        m["st_m"] = f(inputs["state_mlstm_m"][i])
        m["st_h"] = f(inputs["state_hgrn"][i])
        m["st_s"] = f(inputs["state_ssm"][i])
        m["cvec"] = np.stack([f(inputs["c_ctx"]), f(inputs["c"][i])], 0)
        maps.append(m)
    return maps


def kernel(**inputs):
    nc, _ = build()
    maps = make_in_maps(inputs)
    res = run_bass_kernel_spmd(nc, maps, core_ids=list(range(8)))
    R = res.results
    y_prompt = np.concatenate([r["y_out"][0:1024].reshape(4, 256, D) for r in R], 0)
    y_sample = np.stack([r["y_out"][1024:] for r in R], 0)
    outs = [y_prompt, y_sample]
    for nm in ("o_c", "o_n", "o_m", "o_h", "o_s"):
        outs.append(np.concatenate([r[nm] for r in R], 0))
    return tuple(np.ascontiguousarray(o, dtype=np.float32) for o in outs)
```

```python
import numpy as np
from contextlib import ExitStack
import concourse.bass as bass
import concourse.mybir as mybir
from concourse.bass_utils import run_bass_kernel_spmd

F32 = mybir.dt.float32
BF16 = mybir.dt.bfloat16
ALU = mybir.AluOpType
AF = mybir.ActivationFunctionType
AX = mybir.AxisListType

D = 1024
DEPTH = 4
NT = 5120
TT = 512
NTT = NT // TT
N_IN = 10800
D_FF = 2816
EPS = 1e-6
C_MQ, C_MK, C_MV, C_MO, C_MI, C_MF = 0, 512, 1024, 1536, 2048, 2056
C_HQ, C_HI, C_HG, C_HF = 2064, 2576, 3088, 3600
C_SZ, C_SX, C_SDT, C_BG = 4624, 5648, 7696, 7728
SEQS = [(0, 256, 256, 0), (256, 256, 256, 0), (512, 256, 256, 0), (768, 256, 256, 0), (1024, 4096, 64, 1)]


class TK:
    CE = ("pe", "act", "dve", "pool")

    def __init__(self, nc, es):
        self.nc, self.es = nc, es
        self.ops = {e: [] for e in ("pe", "act", "dve", "pool", "sp")}
        self.lastw, self.rd = {}, {}
        self.known = {e: {} for e in self.ops}
        self.dcount, self.dsem = {}, {}
        self.needed = {e: set() for e in self.CE}
        self.csem = {e: self.es.enter_context(nc.semaphore("c_" + e)) for e in self.CE}
        self.trackmap = {}

    def _deps(self, eng, reads, writes, extra=()):
        deps = {}

        def add(t, v):
            if v > deps.get(t, 0):
                deps[t] = v
        for r in reads:
            lw = self.lastw.get(r)
            if lw:
                add(*lw)
        for w in writes:
            lw = self.lastw.get(w)
            if lw:
                add(*lw)
            for t, v in self.rd.get(w, {}).items():
                add(t, v)
        for t, v in extra:
            add(t, v)
        out = []
        kn = self.known[eng]
        for t, v in deps.items():
            if t == eng and eng == "pe":
                continue
            if kn.get(t, 0) >= v:
                continue
            kn[t] = v
            out.append((t, v))
            if t in self.CE:
                self.needed[t].add(v)
        return out

    def _mark(self, tv, reads, writes):
        for w in writes:
            self.lastw[w] = tv
            self.rd[w] = {}
        for r in reads:
            d = self.rd.setdefault(r, {})
            if tv[1] > d.get(tv[0], 0):
                d[tv[0]] = tv[1]

    @staticmethod
    def _keys(aps):
        ks = []
        for a in aps:
            if a is None or isinstance(a, (int, float)):
                continue
            ks.append(a if isinstance(a, str) else getattr(a, "tensor", a).name)
        return ks

    def op(self, eng, fn, outs, ins):
        writes, reads = self._keys(outs), self._keys(ins)
        waits = self._deps(eng, reads, writes)
        idx = len(self.ops[eng]) + 1
        self.ops[eng].append((waits, fn, None))
        self._mark((eng, idx), reads, writes)

    def dma(self, out, in_, track, outs=None, ins=None, eng="sp", slow=False):
        writes = self._keys(outs if outs is not None else [])
        reads = self._keys(ins if ins is not None else [])
        track = self.trackmap.setdefault(track, f"T{len(self.trackmap) % 64}")
        extra = []
        if self.dcount.get(track, 0):
            extra.append((track, self.dcount[track]))
        waits = self._deps(eng, reads, writes, extra)
        self.dcount[track] = self.dcount.get(track, 0) + 16
        if track not in self.dsem:
            self.dsem[track] = self.es.enter_context(self.nc.semaphore("d_" + track))
        if slow:
            self.ops[eng].append((waits, lambda e: e.dma_start(out=out, in_=in_, allow_slow_non_contiguous=True), track))
        else:
            self.ops[eng].append((waits, lambda e: e.dma_start(out=out, in_=in_), track))
        self._mark((track, self.dcount[track]), reads, writes)

    def barrier(self):
        cur = []
        for e in self.CE:
            if self.ops[e]:
                cur.append((e, len(self.ops[e])))
        for t, c in self.dcount.items():
            cur.append((t, c))
        fixed = []
        for t, v in cur:
            if t in self.CE:
                ops = self.ops[t]
                while v > 0 and (ops[v - 1][1] is None or ops[v - 1][2] is not None):
                    v -= 1
                if v == 0:
                    continue
            fixed.append((t, v))
        for e in self.ops:
            waits = self._deps(e, [], [], fixed)
            if waits:
                self.ops[e].append((waits, None, None))
        self.lastw.clear()
        self.rd.clear()

    def emit(self, block):
        nc = self.nc
        sem = self.csem
        rank = {e: {v: i + 1 for i, v in enumerate(sorted(self.needed[e]))} for e in self.CE}

        def run(name, eng):
            for idx, (waits, fn, track) in enumerate(self.ops[name], 1):
                for t, v in waits:
                    if t in self.CE:
                        eng.wait_ge(sem[t], rank[t][v])
                    else:
                        eng.wait_ge(self.dsem[t], v)
                if fn is None:
                    continue
                ins = fn(eng)
                if track is not None:
                    ins.then_inc(self.dsem[track], 16)
                elif idx in self.needed.get(name, ()):
                    ins.then_inc(sem[name], 1)

        @block.sync
        def _(e):
            run("sp", e)

        @block.tensor
        def _(e):
            run("pe", e)

        @block.scalar
        def _(e):
            run("act", e)

        @block.vector
        def _(e):
            run("dve", e)

        @block.gpsimd
        def _(e):
            run("pool", e)


def build(debug=False, nlayers=DEPTH, stop_after=None, secs="ABCDEPSO"):
    nc = bass.Bass("TRN2", target_bir_lowering=False)
    dbg = {}

    def din(name, shape, dt=F32):
        return nc.dram_tensor(name, list(shape), dt, kind="ExternalInput").ap()

    def dout(name, shape, dt=F32):
        return nc.dram_tensor(name, list(shape), dt, kind="ExternalOutput").ap()

    def dscr(name, shape, dt=F32):
        if debug:
            dbg[name] = (shape, dt)
            return nc.dram_tensor(name, list(shape), dt, kind="ExternalOutput").ap()
        return nc.dram_tensor(name, list(shape), dt, kind="Internal").ap()

    x_in = din("x_in", [NT, D])
    st_c = din("st_c", [DEPTH, 2, 4, 128, 128])
    st_n = din("st_n", [DEPTH, 2, 4, 128])
    st_m = din("st_m", [DEPTH, 2, 4])
    st_h = din("st_h", [DEPTH, 2, 4, 128, 128])
    st_s = din("st_s", [DEPTH, 2, 16, 64, 128])
    cvec = din("cvec", [2, D])
    w_ada = din("w_ada", [DEPTH, D, 6 * D])
    b_ada = din("b_ada", [DEPTH, 6 * D])
    norm1 = din("norm1", [DEPTH, D])
    norm2 = din("norm2", [DEPTH, D])
    w_in = din("w_in", [DEPTH, D, N_IN])
    m_bi = din("m_bi", [DEPTH, 8])
    m_bf = din("m_bf", [DEPTH, 8])
    m_norm = din("m_norm", [DEPTH, 512])
    h_lb = din("h_lb", [2, DEPTH, 512])
    h_norm = din("h_norm", [DEPTH, 512])
    s_conv_w = din("s_conv_w", [DEPTH, 3, 2048])
    s_conv_b = din("s_conv_b", [DEPTH, 2048])
    s_dt_bias = din("s_dt_bias", [DEPTH, 32])
    s_a_log = din("s_a_log", [DEPTH, 32])
    s_d = din("s_d", [DEPTH, 16])
    s_norm = din("s_norm", [DEPTH, 1024])
    w_bm = din("w_bm", [DEPTH, 512, D])
    w_bh = din("w_bh", [DEPTH, 512, D])
    w_bs = din("w_bs", [DEPTH, 1024, D])
    w_out = din("w_out", [DEPTH, D, D])
    w_gu = din("w_gu", [DEPTH, D, 2 * D_FF])
    w_down = din("w_down", [DEPTH, D_FF, D])
    norm_f = din("norm_f", [D])
    cst = din("cst", [128, 768])

    y_out = dout("y_out", [NT, D])
    o_c = dout("o_c", [4, DEPTH, 2, 4, 128, 128])
    o_n = dout("o_n", [4, DEPTH, 2, 4, 128])
    o_m = dout("o_m", [4, DEPTH, 2, 4])
    o_h = dout("o_h", [4, DEPTH, 2, 4, 128, 128])
    o_s = dout("o_s", [4, DEPTH, 2, 16, 64, 128])

    xT = dscr("xT", [8, 128, NT])
    wb = {
        "in": (w_in, nc.dram_tensor("wb_in", [DEPTH, D, N_IN], BF16, kind="Internal").ap(), D, N_IN),
        "bm": (w_bm, nc.dram_tensor("wb_bm", [DEPTH, 512, D], BF16, kind="Internal").ap(), 512, D),
        "bh": (w_bh, nc.dram_tensor("wb_bh", [DEPTH, 512, D], BF16, kind="Internal").ap(), 512, D),
        "bs": (w_bs, nc.dram_tensor("wb_bs", [DEPTH, 1024, D], BF16, kind="Internal").ap(), 1024, D),
        "out": (w_out, nc.dram_tensor("wb_out", [DEPTH, D, D], BF16, kind="Internal").ap(), D, D),
        "gu": (w_gu, nc.dram_tensor("wb_gu", [DEPTH, D, 2 * D_FF], BF16, kind="Internal").ap(), D, 2 * D_FF),
        "down": (w_down, nc.dram_tensor("wb_down", [DEPTH, D_FF, D], BF16, kind="Internal").ap(), D_FF, D),
    }
    S = {}
    for nm, rows in (("qTm", 512), ("kTm", 512), ("hqT", 512), ("kkT", 1024), ("cmT", 512), ("bmT", 512), ("bgT", 3072),
                     ("ymT", 512), ("yhT", 512), ("ysT", 1024)):
        S[nm] = dscr(nm, [rows, NT], BF16)
    for nm, rows in (("gi", 8), ("glf", 8), ("lfT", 1024), ("dtT", 32)):
        S[nm] = dscr(nm, [rows, NT], F32)
    for nm, cols in (("k_tm", 512), ("v_tm", 512), ("go_tm", 512), ("hv_tm", 512), ("hg_tm", 512), ("sz_tm", 1024),
                     ("xs_tm", 1024), ("bm_tm", 512)):
        S[nm] = dscr(nm, [NT, cols], BF16)
    S["yb_all"] = dscr("yb_all", [NT, 2048], F32)

    with ExitStack() as es:
        tk = TK(nc, es)
        _n = [0]

        pes = ExitStack()
        cur_es = [es]

        basename = {}
        usage = {}
        dbg['usage'] = usage

        def sb(shape, dt=F32, name=None):
            _n[0] += 1
            nm = f"{name or 't'}_{_n[0]}"
            basename[nm] = name or nm
            sz = int(np.prod(shape[1:])) * (2 if dt == BF16 else 4)
            usage[id(cur_es[0])] = usage.get(id(cur_es[0]), 0) + sz
            return cur_es[0].enter_context(nc.sbuf_tensor(nm, list(shape), dt))

        def ps(shape, dt=F32, name=None):
            _n[0] += 1
            return es.enter_context(nc.psum_tensor(name or f"p{_n[0]}", list(shape), dt))

        PS = [ps([128, 512], F32, f"psb{i}") for i in range(8)]
        psi = {"i": 0}

        def nps():
            psi["i"] += 1
            return PS[psi["i"] % 7]

        def mm(out, lhsT, rhs, start=True, stop=True):
            tk.op("pe", lambda e: e.matmul(out, lhsT=lhsT, rhs=rhs, start=start, stop=stop), [out], [lhsT, rhs])

        def tr(out, in_, ident):
            tk.op("pe", lambda e: e.transpose(out, in_, ident), [out], [in_, ident])

        def act(out, in_, func, bias=0.0, scale=1.0, eng="act"):
            tk.op(eng, lambda e: e.activation(out=out, in_=in_, func=func, bias=bias, scale=scale), [out], [in_, bias, scale])

        def tt(eng, out, in0, in1, op):
            tk.op(eng, lambda e: e.tensor_tensor(out=out, in0=in0, in1=in1, op=op), [out], [in0, in1])

        def ts(eng, out, in0, s1, s2, op0, op1=None):
            if op1 is None:
                tk.op(eng, lambda e: e.tensor_scalar(out=out, in0=in0, scalar1=s1, scalar2=None, op0=op0), [out], [in0, s1])
            else:
                tk.op(eng, lambda e: e.tensor_scalar(out=out, in0=in0, scalar1=s1, scalar2=s2, op0=op0, op1=op1), [out], [in0, s1, s2])

        def stt(eng, out, in0, scalar, in1, op0, op1):
            tk.op(eng, lambda e: e.scalar_tensor_tensor(out=out, in0=in0, scalar=scalar, in1=in1, op0=op0, op1=op1), [out], [in0, scalar, in1])

        def cp(eng, out, in_):
            if eng == "act":
                tk.op(eng, lambda e: e.copy(out=out, in_=in_), [out], [in_])
            else:
                tk.op(eng, lambda e: e.tensor_copy(out=out, in_=in_), [out], [in_])

        def mset(eng, out, val):
            tk.op(eng, lambda e: e.memset(out, val), [out], [])

        def red(eng, out, in_, op, axis=AX.X):
            tk.op(eng, lambda e: e.tensor_reduce(out=out, in_=in_, axis=axis, op=op), [out], [in_])

        def scan(out, d0, d1, init, op0, op1):
            tk.op("dve", lambda e: e.tensor_tensor_scan(out=out, data0=d0, data1=d1, initial=init, op0=op0, op1=op1), [out], [d0, d1, init])

        def ld(out, in_, eng="sp", slow=False):
            tk.dma(out, in_, track="L_" + basename[out.tensor.name], outs=[out], eng=("pool" if eng == "pool" else "sp"), slow=slow)

        def st(out, in_, eng="pool", key=None, slow=False):
            tk.dma(out, in_, track="S_" + basename[in_.tensor.name], ins=[in_], outs=([key] if key else []), eng=("pool" if eng == "poolq" else "sp"), slow=slow)

        vstage = sb([128, 128], F32, "vstage")

        def vec_to_fm(dst, src2d, n):
            ld(vstage[0:n, :], src2d)
            pt = nps()
            tr(pt[:, 0:n], vstage[0:n, :], ident[0:n, 0:n])
            cp("dve", dst, pt[:, 0:n])

        rr = {"i": 0}

        def anyeng(choices=("dve", "pool", "act")):
            rr["i"] += 1
            return choices[rr["i"] % len(choices)]

        cst_f = sb([128, 768], F32, "cst_f")
        ld(cst_f[:], cst[:, :])
        ident = cst_f[:, 0:128]
        ident_b = sb([128, 128], BF16, "ident_b")
        cp("dve", ident_b[:], ident)
        maskf = sb([128, 128], BF16, "maskf")
        maskb = sb([128, 128], BF16, "maskb")
        cp("dve", maskf[:], cst_f[:, 128:256])
        cp("dve", maskb[:], cst_f[:, 256:384])
        negf = cst_f[:, 384:512]
        negb = cst_f[:, 512:640]
        iota_r = cst_f[:, 640:704]
        pidx = cst_f[:, 704:705]
        ones_b = sb([128, 128], BF16, "ones_b")
        mset("dve", ones_b[:], 1.0)
        ones_f = sb([128, 128], F32, "ones_f")
        mset("dve", ones_f[:], 1.0)
        zeros_f = sb([128, 32], F32, "zeros_f")
        mset("pool", zeros_f[:], 0.0)

        MOD = sb([128, DEPTH, 6, 8, 2], F32, "MOD")
        LB = sb([128, 2, 4, DEPTH], F32, "LB")
        OMLB = sb([128, 2, 4, DEPTH], F32, "OMLB")
        E = sb([128, 4, 64], F32, "Etab")
        cur_es[0] = pes
        cw_in = [sb([128, 2048], F32, f"cw_in{i}") for i in range(4)]
        cw_out = [sb([128, 2048], BF16, f"cw_out{i}") for i in range(4)]
        k = 0
        for nm, (src, dst, rows, cols) in (wb.items() if "A" in secs else []):
            for l in range(nlayers):
                for r0 in range(0, rows, 128):
                    for c0 in range(0, cols, 2048):
                        cc = min(2048, cols - c0)
                        a, b = cw_in[k % 4], cw_out[k % 4]
                        ld(a[:, 0:cc], src[l, r0:r0 + 128, c0:c0 + cc])
                        cp(("dve", "act")[k % 2], b[:, 0:cc], a[:, 0:cc])
                        st(dst[l, r0:r0 + 128, c0:c0 + cc], b[:, 0:cc], eng="poolq")
                        k += 1


        cT = sb([128, 8, 2], F32, "cT")
        sgc = sb([128, 8, 2], F32, "sgc")
        for j in range(2):
            vec_to_fm(cT[:, :, j], cvec[j, :].rearrange("(c p) -> c p", p=128), 8)
        act(sgc[:], cT[:], AF.Silu)
        badaT = sb([128, DEPTH, 48], F32, "badaT")
        n1T = sb([128, DEPTH, 8], F32, "n1T")
        n2T = sb([128, DEPTH, 8], F32, "n2T")
        for l in range(nlayers):
            vec_to_fm(badaT[:, l, :], b_ada[l, :].rearrange("(c p) -> c p", p=128), 48)
            vec_to_fm(n1T[:, l, :], norm1[l, :].rearrange("(c p) -> c p", p=128), 8)
            vec_to_fm(n2T[:, l, :], norm2[l, :].rearrange("(c p) -> c p", p=128), 8)
        wa = [sb([128, 8, 768], F32, f"wa{i}") for i in range(2)]
        k = 0
        for l in range(nlayers if "B" in secs else 0):
            for cb in range(8):
                w = wa[k % 2]
                k += 1
                for dc in range(8):
                    ld(w[:, dc, :], w_ada[l, dc * 128:(dc + 1) * 128, cb * 768:(cb + 1) * 768], eng=("sp" if dc % 2 == 0 else "act"))
                pt = nps()
                for fc in range(6):
                    for dc in range(8):
                        mm(pt[:, fc * 2:fc * 2 + 2], w[:, dc, fc * 128:(fc + 1) * 128], sgc[:, dc, :], start=(dc == 0), stop=(dc == 7))
                for fc in range(6):
                    ch = cb * 6 + fc
                    kind, c8 = ch // 8, ch % 8
                    ts("dve", MOD[:, l, kind, c8, :], pt[:, fc * 2:fc * 2 + 2], badaT[:, l, ch:ch + 1], None, ALU.add)
            for kind, nT in ((1, n1T), (4, n2T)):
                ts("dve", MOD[:, l, kind, :, :], MOD[:, l, kind, :, :], 1.0, None, ALU.add)
                tt("dve", MOD[:, l, kind, :, :], MOD[:, l, kind, :, :], nT[:, l, :].unsqueeze(2).to_broadcast([128, 8, 2]), ALU.mult)

        hlb = sb([128, 2, 4, DEPTH], F32, "hlb")
        for dr in range(2):
            for l in range(DEPTH):
                vec_to_fm(hlb[:, dr, :, l], h_lb[dr, l, :].rearrange("(h p) -> h p", p=128), 4)
        hmx = sb([128, 2, 4], F32, "hmx")
        red("dve", hmx[:], hlb[:], ALU.max)
        tt("dve", hlb[:], hlb[:], hmx[:].unsqueeze(3).to_broadcast([128, 2, 4, DEPTH]), ALU.subtract)
        act(hlb[:], hlb[:], AF.Exp)
        red("dve", hmx[:], hlb[:], ALU.add)
        tk.op("dve", lambda e: e.reciprocal(out=hmx[:], in_=hmx[:]), [hmx], [hmx])
        tt("dve", hlb[:], hlb[:], hmx[:].unsqueeze(3).to_broadcast([128, 2, 4, DEPTH]), ALU.mult)
        mset("dve", LB[:, :, :, 0], 0.0)
        for l in range(1, DEPTH):
            if l == 1:
                cp("dve", LB[:, :, :, 1], hlb[:, :, :, 1])
            else:
                tt("dve", LB[:, :, :, l], LB[:, :, :, l - 1], hlb[:, :, :, l], ALU.add)
        ts("dve", OMLB[:], LB[:], -1.0, 1.0, ALU.mult, ALU.add)

        fr = sb([128, 2], F32, "fr")
        LN1E4 = float(np.log(10000.0))
        act(fr[:, 0:1], pidx, AF.Exp, 0.0, -LN1E4 / 256.0)
        half_b = sb([128, 1], F32, "half_b")
        mset("dve", half_b[:], -LN1E4 * 0.5)
        act(fr[:, 1:2], pidx, AF.Exp, half_b[:, 0:1], -LN1E4 / 256.0)
        s1 = sb([128, 2], F32, "s1")
        c1 = sb([128, 2], F32, "c1")
        hpi = sb([128, 1], F32, "hpi")
        mset("dve", hpi[:], float(np.pi / 2))
        act(s1[:], fr[:], AF.Sin)
        act(c1[:], fr[:], AF.Sin, hpi[:, 0:1], 1.0)
        mset("dve", E[:, 0:2, 0], 0.0)
        mset("dve", E[:, 2:4, 0], 1.0)
        tmpa = sb([128, 2], F32, "tmpa")
        tmpb = sb([128, 2], F32, "tmpb")
        for r in range(1, 64 if "D" in secs else 1):
            tt("dve", tmpa[:], E[:, 2:4, r - 1], s1[:], ALU.mult)
            tt("dve", tmpb[:], E[:, 0:2, r - 1], s1[:], ALU.mult)
            tt("dve", E[:, 0:2, r], E[:, 0:2, r - 1], c1[:], ALU.mult)
            tt("dve", E[:, 0:2, r], E[:, 0:2, r], tmpa[:], ALU.add)
            tt("dve", E[:, 2:4, r], E[:, 2:4, r - 1], c1[:], ALU.mult)
            tt("dve", E[:, 2:4, r], E[:, 2:4, r], tmpb[:], ALU.subtract)

        xin_t = [sb([128, D], F32, f"xin{i}") for i in range(2)]
        xo_t = [sb([128, 8, 128], F32, f"xo{i}") for i in range(2)]
        for rt in range(NT // 128 if "E" in secs else 0):
            a, o = xin_t[rt % 2], xo_t[rt % 2]
            import os
            ld(a[:], x_in[rt * 128:(rt + 1) * 128, :], eng=("sp" if (rt % 2 == 0 or "spld" in os.environ.get("EDBG", "")) else "act"))
            for half in range(2):
                pt = nps()
                for c4 in range(4):
                    c = half * 4 + c4
                    tr(pt[:, c4 * 128:(c4 + 1) * 128], a[:, c * 128:(c + 1) * 128], ident)
                if rt < 8:
                    cp(anyeng(("dve",) if "dvecp" in os.environ.get("EDBG", "") else ("dve", "act")), o[:, half * 4:half * 4 + 4, :], pt[:].rearrange("p (c t) -> p c t", c=4))
                else:
                    r0 = ((rt - 8) * 128) // 64
                    if half == 0:
                        tt("dve", o[:, 0:4, :].rearrange("p c (r q) -> p c r q", r=2),
                           pt[:].rearrange("p (c r q) -> p c r q", c=4, r=2),
                           E[:, :, r0:r0 + 2].unsqueeze(3).to_broadcast([128, 4, 2, 64]), ALU.add)
                    else:
                        tt("dve", o[:, 4:8, :].rearrange("p c (r q) -> p c r q", r=2),
                           pt[:].rearrange("p (c r q) -> p c r q", c=4, r=2),
                           E[:, :, :].unsqueeze(2).to_broadcast([128, 4, 2, 64]), ALU.add)
            import os
            EDBG = os.environ.get("EDBG", "")
            if "nost" in EDBG:
                pass
            elif "spst" in EDBG:
                st(xT[:, :, rt * 128:(rt + 1) * 128].rearrange("c p t -> p c t"), o[:], eng="sp")
            elif "perc" in EDBG:
                for c in range(8):
                    st(xT[c, :, rt * 128:(rt + 1) * 128], o[:, c, :], eng="sp")
            else:
                st(xT[:, :, rt * 128:(rt + 1) * 128].rearrange("c p t -> p c t"), o[:])
        tk.barrier()


        pes.close()
        cur_es[0] = es
        psT = PS[7]
        psTb = psT[:].bitcast(BF16)
        PSF = PS[0:7]

        def npsf():
            psi["i"] += 1
            return PSF[psi["i"] % 7]

        mbiT = sb([8, DEPTH], F32, "mbiT")
        nmbfT = sb([8, DEPTH], F32, "nmbfT")
        dtbT = sb([32, DEPTH], F32, "dtbT")
        negaT = sb([32, DEPTH], F32, "negaT")
        for l in range(nlayers):
            ld(mbiT[0:8, l:l + 1], m_bi[l, :].rearrange("(p o) -> p o", o=1))
            ld(nmbfT[0:8, l:l + 1], m_bf[l, :].rearrange("(p o) -> p o", o=1))
            ld(dtbT[0:32, l:l + 1], s_dt_bias[l, :].rearrange("(p o) -> p o", o=1))
            ld(negaT[0:32, l:l + 1], s_a_log[l, :].rearrange("(p o) -> p o", o=1))
        ts("dve", nmbfT[:], nmbfT[:], -1.0, None, ALU.mult)
        act(negaT[:], negaT[:], AF.Exp)
        ts("dve", negaT[:], negaT[:], -1.0, None, ALU.mult)
        cwT = sb([128, DEPTH, 3, 16], F32, "cwT")
        cbT = sb([128, DEPTH, 16], F32, "cbT")
        for l in range(nlayers):
            for kk in range(3):
                vec_to_fm(cwT[:, l, kk, :], s_conv_w[l, kk, :].rearrange("(c p) -> c p", p=128), 16)
            vec_to_fm(cbT[:, l, :], s_conv_b[l, :].rearrange("(c p) -> c p", p=128), 16)

        B_ = {}

        def alloc_P(full=True):
            B_["xt_b"] = [sb([128, 8, TT], F32, f"xt{i}") for i in range(2)]
            B_["sq_b"] = sb([128, 8, TT], BF16, "sq")
            B_["rstd"] = sb([128, TT], F32, "rstd")
            B_["hn_f"] = sb([128, 8, TT], F32, "hn_f")
            B_["hT"] = sb([128, 8, TT], BF16, "hT")
            B_["wbuf"] = [sb([128, 8, 512], BF16, f"wbuf{i}") for i in range(3)]
            if not full:
                return
            B_["stg_b"] = [sb([128, 512], BF16, f"stgb{i}") for i in range(4)]
            B_["stg_f"] = [sb([128, 512], F32, f"stgf{i}") for i in range(4)]
            B_["acc2"] = [sb([128, 512], F32, f"acc_f{i}") for i in range(2)]
            B_["xs_stage"] = sb([128, 4, 1024], BF16, "xs_stage")
            B_["bm_stage"] = sb([128, 4, 512], BF16, "bm_stage")
            B_["u_b2"] = [sb([128, 512], BF16, f"u_b{i}") for i in range(2)]

        wk = {"i": 0}
        sgi = {"b": 0, "f": 0}

        def nstg(kind):
            sgi[kind] += 1
            return (B_["stg_b"] if kind == "b" else B_["stg_f"])[sgi[kind] % 4]

        def norm_mod(l, ti, kind_gam, kind_sh, xt, hT_out=None):
            cidx = 0 if ti < 2 else 1
            hT_out = hT_out if hT_out is not None else B_["hT"]
            act(B_["sq_b"][:], xt[:], AF.Square)
            pt = npsf()
            for c in range(8):
                mm(pt[:], ones_b[:], B_["sq_b"][:, c, :], start=(c == 0), stop=(c == 7))
            act(B_["rstd"][:], pt[:], AF.Ln, EPS, 1.0 / D)
            act(B_["rstd"][:], B_["rstd"][:], AF.Exp, 0.0, -0.5)
            tt("dve", B_["hn_f"][:], xt[:], B_["rstd"][:].unsqueeze(1).to_broadcast([128, 8, TT]), ALU.mult)
            for c in range(8):
                act(hT_out[:, c, :], B_["hn_f"][:, c, :], AF.Identity, MOD[:, l, kind_sh, c, cidx:cidx + 1], MOD[:, l, kind_gam, c, cidx:cidx + 1])

        def load_w(name, l, r0, nrows, c0, ncols):
            wk["i"] += 1
            w = B_["wbuf"][wk["i"] % 3]
            src = wb[name][1]
            ld(w[:, 0:nrows // 128, 0:ncols], src[l, r0:r0 + nrows, c0:c0 + ncols].rearrange("(dc p) n -> p dc n", p=128), eng="pool")
            return w

        def fm_block(w, j0, m, pt_rows=128):
            pt = npsf()
            for dc in range(8):
                mm(pt[0:m, :], w[:, dc, j0:j0 + m], B_["hT"][:, dc, :], start=(dc == 0), stop=(dc == 7))
            return pt

        def phase_P(l):
            pst = ExitStack()
            cur_es[0] = pst
            alloc_P()
            hTs = [B_["hT"], sb([128, 8, TT], BF16, "hT2")]
            ld(B_["xt_b"][0][:], xT[:, :, 0:TT].rearrange("c p t -> p c t"))
            norm_mod(l, 0, 1, 0, B_["xt_b"][0], hTs[0])
            for ti in range(NTT):
                t0 = ti * TT
                rowlen = 256 if ti < 2 else 64
                nrow = TT // rowlen
                B_["hT"] = hTs[ti % 2]
                if ti + 1 < NTT:
                    ld(B_["xt_b"][(ti + 1) % 2][:], xT[:, :, t0 + TT:t0 + 2 * TT].rearrange("c p t -> p c t"))
                for (c0, nch, dst, func, scale) in ((C_MQ, 4, S["qTm"], AF.Copy, 1.0), (C_MK, 4, S["kTm"], AF.Copy, 128 ** -0.5),
                                                    (C_HQ, 4, S["hqT"], AF.Copy, 1.0),
                                                    (C_BG, 4, S["bgT"], AF.Sigmoid, 1.0), (C_BG + 512, 4, S["bgT"], AF.Sigmoid, 1.0),
                                                    (C_BG + 1024, 4, S["bgT"], AF.Sigmoid, 1.0), (C_BG + 1536, 4, S["bgT"], AF.Sigmoid, 1.0),
                                                    (C_BG + 2048, 4, S["bgT"], AF.Sigmoid, 1.0), (C_BG + 2560, 4, S["bgT"], AF.Sigmoid, 1.0)):
                    w = load_w("in", l, 0, D, c0, nch * 128)
                    rbase = (c0 - C_BG) if c0 >= C_BG else 0
                    for j in range(nch):
                        pt = fm_block(w, j * 128, 128)
                        sg = nstg("b")
                        act(sg[:], pt[:], func, 0.0, scale)
                        st(dst[rbase + j * 128:rbase + (j + 1) * 128, t0:t0 + TT], sg[:])
                for (c0, ncol, dst, func, scale) in ((C_MK, 512, S["k_tm"], AF.Copy, 128 ** -0.5), (C_MV, 512, S["v_tm"], AF.Copy, 1.0),
                                                     (C_MO, 512, S["go_tm"], AF.Sigmoid, 1.0), (C_HI, 512, S["hv_tm"], AF.Copy, 1.0),
                                                     (C_HG, 512, S["hg_tm"], AF.Silu, 1.0), (C_SZ, 512, S["sz_tm"], AF.Silu, 1.0),
                                                     (C_SZ + 512, 512, S["sz_tm"], AF.Silu, 1.0)):
                    w = load_w("in", l, 0, D, c0, 512)
                    cb0 = 512 if c0 == C_SZ + 512 else 0
                    for s4 in range(4):
                        pt = npsf()
                        for dc in range(8):
                            mm(pt[:], B_["hT"][:, dc, s4 * 128:(s4 + 1) * 128], w[:, dc, 0:512], start=(dc == 0), stop=(dc == 7))
                        sg = nstg("b")
                        act(sg[:], pt[:], func, 0.0, scale)
                        st(dst[t0 + s4 * 128:t0 + (s4 + 1) * 128, cb0:cb0 + 512], sg[:])
                if ti + 1 < NTT:
                    norm_mod(l, ti + 1, 1, 0, B_["xt_b"][(ti + 1) % 2], hTs[(ti + 1) % 2])
                w = load_w("in", l, 0, D, C_MI, 16)
                pt = fm_block(w, 0, 8)
                sg = nstg("f")
                ts("dve", sg[0:8, :], pt[0:8, :], mbiT[0:8, l:l + 1], None, ALU.add)
                st(S["gi"][:, t0:t0 + TT], sg[0:8, :])
                pt = fm_block(w, 8, 8)
                sg = nstg("f")
                act(sg[0:8, :], pt[0:8, :], AF.Exp, nmbfT[0:8, l:l + 1], -1.0)
                act(sg[0:8, :], sg[0:8, :], AF.Ln, 1.0, 1.0)
                ts("dve", sg[0:8, :], sg[0:8, :], -1.0, None, ALU.mult)
                st(S["glf"][:, t0:t0 + TT], sg[0:8, :])
                w = load_w("in", l, 0, D, C_SDT, 32)
                pt = fm_block(w, 0, 32)
                sg = nstg("f")
                act(sg[0:32, :], pt[0:32, :], AF.Exp, dtbT[0:32, l:l + 1], 1.0)
                act(sg[0:32, :], sg[0:32, :], AF.Ln, 1.0, 1.0)
                st(S["dtT"][:, t0:t0 + TT], sg[0:32, :])
                for half in range(2):
                    w = load_w("in", l, 0, D, C_HF + half * 512, 512)
                    for j in range(4):
                        pt = fm_block(w, j * 128, 128)
                        sg = nstg("f")
                        act(sg[:], pt[:], AF.Sigmoid)
                        ts("dve", sg[:], sg[:], OMLB[:, half, j, l:l + 1], LB[:, half, j, l:l + 1], ALU.mult, ALU.add)
                        sgk = nstg("b")
                        ts("pool", sgk[:], sg[:], -1.0, 1.0, ALU.mult, ALU.add)
                        st(S["kkT"][half * 512 + j * 128:half * 512 + (j + 1) * 128, t0:t0 + TT], sgk[:])
                        sgl = nstg("f")
                        act(sgl[:], sg[:], AF.Ln)
                        st(S["lfT"][half * 512 + j * 128:half * 512 + (j + 1) * 128, t0:t0 + TT], sgl[:])
                pend = [None]

                def make_tail(ch, ub):
                    def tail():
                        for s4 in range(4):
                            tr(psTb[:, s4 * 128:(s4 + 1) * 128], ub[:, s4 * 128:(s4 + 1) * 128], ident_b[:])
                        if ch < 8:
                            cp("dve", B_["xs_stage"][:, :, ch * 128:(ch + 1) * 128], psTb[:, 0:512].rearrange("p (s f) -> p s f", s=4))
                        else:
                            cp("dve", B_["bm_stage"][:, :, (ch - 8) * 128:(ch - 7) * 128], psTb[:, 0:512].rearrange("p (s f) -> p s f", s=4))
                    return tail
                for q in range(4):
                    w = load_w("in", l, 0, D, C_SX + q * 512, 512)
                    for j in range(4):
                        ch = q * 4 + j
                        pt = fm_block(w, j * 128, 128)
                        if pend[0] is not None:
                            pend[0]()
                            pend[0] = None
                        acc = B_["acc2"][ch % 2]
                        p3 = pt[:].rearrange("p (r q) -> p r q", q=rowlen)
                        a3 = acc[:].rearrange("p (r q) -> p r q", q=rowlen)
                        ts("dve", acc[:], pt[:], cwT[:, l, 1, ch:ch + 1], None, ALU.mult)
                        stt("dve", a3[:, :, 1:rowlen], p3[:, :, 0:rowlen - 1], cwT[:, l, 0, ch:ch + 1], a3[:, :, 1:rowlen], ALU.mult, ALU.add)
                        stt("dve", a3[:, :, 0:rowlen - 1], p3[:, :, 1:rowlen], cwT[:, l, 2, ch:ch + 1], a3[:, :, 0:rowlen - 1], ALU.mult, ALU.add)
                        if ch < 12:
                            ub = B_["u_b2"][ch % 2]
                            act(ub[:], acc[:], AF.Silu, cbT[:, l, ch:ch + 1], 1.0)
                            if ch >= 8:
                                st(S["bmT"][(ch - 8) * 128:(ch - 7) * 128, t0:t0 + TT], ub[:])
                            pend[0] = make_tail(ch, ub)
                        else:
                            sg = nstg("b")
                            act(sg[:], acc[:], AF.Silu, cbT[:, l, ch:ch + 1], 1.0)
                            st(S["cmT"][(ch - 12) * 128:(ch - 11) * 128, t0:t0 + TT], sg[:])
                if pend[0] is not None:
                    pend[0]()
                    pend[0] = None
                st(S["xs_tm"][t0:t0 + TT, :].rearrange("(s p) f -> p s f", p=128), B_["xs_stage"][:])
                st(S["bm_tm"][t0:t0 + TT, :].rearrange("(s p) f -> p s f", p=128), B_["bm_stage"][:])
            tk.barrier()
            pst.close()
            cur_es[0] = es


        def row_bcast(dst, src_row, n):
            r1 = B_["r1"]
            for c0 in range(0, n, 256):
                cc = min(256, n - c0)
                ld(r1[0:1, 0:cc], src_row[:, c0:c0 + cc])
                pt = npsf()
                mm(pt[:, 0:cc], ones_f[0:1, :], r1[0:1, 0:cc])
                cp("dve", dst[:, c0:c0 + cc], pt[:, 0:cc])

        def rms_rows(eng, out_rst, x3, nh, dsz, tmp):
            if eng == "act":
                act(tmp, x3, AF.Square)
            else:
                tt(eng, tmp, x3, x3, ALU.mult)
            red("dve", out_rst, tmp, ALU.add)
            act(out_rst, out_rst, AF.Ln, EPS, 1.0 / dsz)
            act(out_rst, out_rst, AF.Exp, 0.0, -0.5)

        def run_gens(gens):
            gens = list(gens)
            while gens:
                for g_ in list(gens):
                    try:
                        next(g_)
                    except StopIteration:
                        gens.remove(g_)

        def phase_S(l):
            pst = ExitStack()
            cur_es[0] = pst
            LM = 4096
            GCH = 512
            B_["r1"] = sb([1, 256], F32, "r1")
            GA = sb([36, LM], F32, "GA")
            GB = sb([36, LM], F32, "GB")
            GC = sb([36, GCH], F32, "GC")
            gtot = sb([36, 1], F32, "gtot")
            gcar = sb([36, 1], F32, "gcar")
            cmx = sb([36, 32], F32, "cmx")
            Gt = sb([36, 32], F32, "Gt")
            Gp = sb([36, 32], F32, "Gp")
            fs = sb([36, 32], F32, "fs")
            fsm = sb([36, 32], F32, "fsm")
            m0t = sb([36, 1], F32, "m0t")
            mfin = sb([36, 1], F32, "mfin")
            fsb = sb([128, 8, 32], F32, "fsb")
            gsc = sb([128, 16], F32, "gsc")
            mnb = sb([128, 512], F32, "mnb")
            hnb = sb([128, 512], F32, "hnb")
            snb = sb([128, 1024], F32, "snb")
            sdb = sb([128, 16], F32, "sdb")
            row_bcast(mnb, m_norm[l:l + 1, :], 512)
            row_bcast(hnb, h_norm[l:l + 1, :], 512)
            row_bcast(snb, s_norm[l:l + 1, :], 1024)
            row_bcast(sdb, s_d[l:l + 1, :], 16)
            nega2 = sb([16, 2], F32, "nega2")
            for dr in range(2):
                ld(nega2[0:16, dr:dr + 1], s_a_log[l, dr * 16:(dr + 1) * 16].rearrange("(p o) -> p o", o=1))
            act(nega2[:], nega2[:], AF.Exp)
            ts("dve", nega2[:], nega2[:], -1.0, None, ALU.mult)
            PM0, PM1 = PS[0], PS[1]
            PH0, PH1, PH2 = PS[2], PS[3], PS[4]
            PSS = [PS[5], PS[6]]
            ssi = {"i": 0}

            def nb_s():
                ssi["i"] += 1
                return PSS[ssi["i"] % 2]

            qT_t = [sb([128, 4, 128], BF16, f"qT_t{i}") for i in range(2)]
            kT_t = [sb([128, 4, 128], BF16, f"kT_t{i}") for i in range(2)]
            ktm_t = [sb([128, 512], BF16, f"ktm_t{i}") for i in range(2)]
            vtm_t = [sb([128, 512], BF16, f"vtm_t{i}") for i in range(2)]
            gate_m = sb([128, 512], BF16, "gate_m")
            vaug = sb([128, 4, 129], BF16, "vaug")
            PTm = sb([128, 4, 128], BF16, "PTm")
            C32 = [[sb([128, 129], F32, f"C32_{dr}{h}") for h in range(4)] for dr in range(2)]
            Cb = [[sb([128, 129], BF16, f"Cb_{dr}{h}") for h in range(4)] for dr in range(2)]
            dn = sb([128, 4], F32, "dn")
            yblm = sb([128, 512], F32, "yblm")
            ycm = sb([128, 512], F32, "ycm")
            tmpm = yblm
            ybfm = sb([128, 512], BF16, "ybfm")
            yTm = sb([128, 4, 128], BF16, "yTm")
            rstm = sb([128, 4], F32, "rstm")
            hq_t = [sb([128, 4, 128], BF16, f"hq_t{i}") for i in range(2)]
            kk_t = [sb([128, 4, 128], BF16, f"kk_t{i}") for i in range(2)]
            lf_t = [sb([128, 4, 128], F32, f"lf_t{i}") for i in range(2)]
            hvc = [[sb([32, 512], BF16, f"hvc{i}_{c}") for c in range(4)] for i in range(2)]
            gate_h = sb([128, 512], BF16, "gate_h")
            cp_t = sb([128, 4, 128], F32, "cp_t")
            dd_t = sb([128, 4, 128], F32, "dd_t")
            ep_t = sb([128, 4, 128], F32, "ep_t")
            em_t = sb([128, 4, 128], F32, "em_t")
            qt_t = sb([128, 4, 128], BF16, "qt_t")
            kt_t = sb([128, 4, 128], BF16, "kt_t")
            kt2 = sb([128, 4, 128], BF16, "kt2")
            qt2p = [sb([128, 4, 128], BF16, f"qt2p{c}") for c in range(4)]
            refg = sb([128, 4, 4], F32, "refg")
            egm = sb([128, 4, 4], F32, "egm")
            egd = sb([128, 4, 4], F32, "egd")
            egt = sb([128, 4, 4], F32, "egt")
            PThp = sb([32, 16, 128], BF16, "PThp")
            kttm = sb([32, 16, 128], BF16, "kttm")
            SA = [sb([128, 4, 128], F32, f"SA{dr}") for dr in range(2)]
            SB = sb([128, 4, 128], F32, "SBh")
            Sbb = [sb([128, 4, 128], BF16, f"Sbb{c}") for c in range(2)]
            yblh = sb([128, 512], F32, "yblh")
            ych = sb([128, 512], F32, "ych")
            tmph = yblh
            ybfh = sb([128, 512], BF16, "ybfh")
            yTh = sb([128, 4, 128], BF16, "yTh")
            rsth = sb([128, 4], F32, "rsth")
            mset("pool", PThp[:], 0.0)
            zb = sb([1, 512], BF16, "zb")
            mset("pool", zb[:], 0.0)
            for c in range(4):
                mset("pool", qt2p[c][:], 0.0)
            dta = [sb([16, 128], F32, f"dta{i}") for i in range(2)]
            cm_t = [sb([128, 4, 128], BF16, f"cm_t{i}") for i in range(2)]
            bmT_t = [sb([128, 4, 128], BF16, f"bmT_t{i}") for i in range(2)]
            bmtm_t = [sb([128, 512], BF16, f"bmtm_t{i}") for i in range(2)]
            xs_t = [sb([128, 1024], BF16, f"xs_t{i}") for i in range(2)]
            gate_s = sb([128, 1024], BF16, "gate_s")
            da = sb([16, 128], F32, "da")
            lcs = sb([16, 128], F32, "lcs")
            fm4 = sb([16, 4, 128], F32, "fm4")
            gs2 = [sb([128, 4, 16], F32, f"gs{i}") for i in range(2)]
            ngla = sb([128, 16], F32, "ngla")
            labd = sb([16, 16, 128], F32, "labd")
            elae = sb([16, 1], F32, "elae")
            ebd = sb([16, 16], F32, "ebd")
            eb2 = [sb([128, 16], F32, f"eb{i}") for i in range(2)]
            cb_all = sb([128, 4, 128], F32, "cb_all")
            ddm = [sb([128, 4, 128], F32, f"ddm{i}") for i in range(2)]
            PTs2 = [[sb([128, 4, 128], BF16, f"PTs{p}_{i}") for i in range(4)] for p in range(2)]
            xp2 = [sb([128, 16, 64], BF16, f"xp{i}") for i in range(2)]
            xpp2 = [sb([128, 1024], BF16, f"xpp{i}") for i in range(2)]
            pis = [sb([128, 256], F32, f"pis{i}") for i in range(2)]
            hT32 = [[sb([128, 4, 64], F32, f"hT32_{dr}{g}") for g in range(4)] for dr in range(2)]
            hTb = [[sb([128, 256], BF16, f"hTb_{dr}{g}") for g in range(4)] for dr in range(2)]
            tmps = sb([128, 4, 64], F32, "tmps")
            sst = sb([64, 128], F32, "sst")
            ybls = sb([128, 1024], F32, "ybls")
            ycs = sb([128, 1024], F32, "ycs")
            tmpS = ybls
            ybfs = sb([128, 1024], BF16, "ybfs")
            yTs = sb([128, 8, 128], BF16, "yTs")
            rsts = sb([128, 1], F32, "rsts")

            NC, CS = 4, 32

            def loads_m(dr, tb, par):
                ld(qT_t[par][:], S["qTm"][:, tb:tb + 128].rearrange("(h d) t -> d h t", d=128))
                ld(kT_t[par][:], S["kTm"][:, tb:tb + 128].rearrange("(h d) t -> d h t", d=128))
                ld(ktm_t[par][:], S["k_tm"][tb:tb + 128, :])
                ld(vtm_t[par][:], S["v_tm"][tb:tb + 128, :])

            def loads_h(dr, tb, par):
                ld(lf_t[par][:], S["lfT"][dr * 512:(dr + 1) * 512, tb:tb + 128].rearrange("(h d) t -> d h t", d=128))
                ld(kk_t[par][:], S["kkT"][dr * 512:(dr + 1) * 512, tb:tb + 128].rearrange("(h d) t -> d h t", d=128))
                ld(hq_t[par][:], S["hqT"][:, tb:tb + 128].rearrange("(h d) t -> d h t", d=128))
                for c2 in range(NC):
                    ld(hvc[par][c2][:], S["hv_tm"][tb + c2 * CS:tb + (c2 + 1) * CS, :])

            def loads_s(dr, tb, par):
                ld(dta[par][:], S["dtT"][dr * 16:(dr + 1) * 16, tb:tb + 128])
                ld(cm_t[par][:], S["cmT"][:, tb:tb + 128].rearrange("(g n) t -> n g t", n=128))
                ld(bmT_t[par][:], S["bmT"][:, tb:tb + 128].rearrange("(g n) t -> n g t", n=128))
                ld(bmtm_t[par][:], S["bm_tm"][tb:tb + 128, :])
                ld(xs_t[par][:], S["xs_tm"][tb:tb + 128, :])

            for sq, (s0, L, rowlen, cidx) in enumerate(SEQS):
                nb = L // 128
                is_p = cidx == 0
                for dr in range(2):
                    p0 = 32 * dr
                    P = slice(p0, p0 + 4)
                    ld(GA[P, 0:L], S["gi"][dr * 4:dr * 4 + 4, s0:s0 + L])
                    ld(GB[P, 0:L], S["glf"][dr * 4:dr * 4 + 4, s0:s0 + L])
                    if is_p:
                        mset("dve", m0t[P, :], 0.0)
                    else:
                        ld(m0t[P, 0:1], st_m[l, dr, :].rearrange("(p o) -> p o", o=1))
                    if dr == 1:
                        red("dve", gtot[P, :], GB[P, 0:L], ALU.add)
                    for c0 in range(0, L, GCH):
                        cc = min(GCH, L - c0)
                        init = 0.0 if c0 == 0 else gcar[P, 0:1]
                        scan(GC[P, 0:cc], ones_f[P, 0:1].to_broadcast([4, cc]), GB[P, c0:c0 + cc], init, ALU.mult, ALU.add)
                        if c0 + cc < L:
                            cp("dve", gcar[P, :], GC[P, cc - 1:cc])
                        if dr == 0:
                            cp("dve", GB[P, c0:c0 + cc], GC[P, 0:cc])
                        else:
                            ts("dve", GB[P, c0:c0 + cc], GB[P, c0:c0 + cc], gtot[P, 0:1], None, ALU.add)
                            tt("dve", GB[P, c0:c0 + cc], GB[P, c0:c0 + cc], GC[P, 0:cc], ALU.subtract)
                    tt("dve", GA[P, 0:L], GA[P, 0:L], GB[P, 0:L], ALU.subtract)
                    red("dve", cmx[P, 0:nb], GA[P, 0:L].rearrange("p (c t) -> p c t", t=128), ALU.max)
                    if dr == 0:
                        scan(Gt[P, 0:nb], zeros_f[P, 0:nb], cmx[P, 0:nb], m0t[P, 0:1], ALU.add, ALU.max)
                        cp("dve", Gp[P, 0:1], m0t[P, 0:1])
                        if nb > 1:
                            cp("dve", Gp[P, 1:nb], Gt[P, 0:nb - 1])
                    else:
                        for c in range(nb - 1, -1, -1):
                            prev = m0t[P, 0:1] if c == nb - 1 else Gt[P, c + 1:c + 2]
                            tt("dve", Gt[P, c:c + 1], prev, cmx[P, c:c + 1], ALU.max)
                        cp("dve", Gp[P, nb - 1:nb], m0t[P, 0:1])
                        if nb > 1:
                            cp("dve", Gp[P, 0:nb - 1], Gt[P, 1:nb])
                    tt("dve", fs[P, 0:nb], Gp[P, 0:nb], Gt[P, 0:nb], ALU.subtract)
                    act(fs[P, 0:nb], fs[P, 0:nb], AF.Exp)
                    if dr == 0:
                        tt("dve", mfin[P, :], GB[P, L - 1:L], Gt[P, nb - 1:nb], ALU.add)
                    else:
                        tt("dve", mfin[P, :], GB[P, 0:1], Gt[P, 0:1], ALU.add)
                    if is_p:
                        st(o_m[sq, l, dr, :].rearrange("(p o) -> p o", o=1), mfin[P, :])
                    gb3 = Gt[P, 0:nb].unsqueeze(2).to_broadcast([4, nb, 128])
                    tt("dve", GA[P, 0:L].rearrange("p (c t) -> p c t", t=128), GA[P, 0:L].rearrange("p (c t) -> p c t", t=128), gb3, ALU.subtract)
                    act(GA[P, 0:L], GA[P, 0:L], AF.Exp)
                    tt("dve", GB[P, 0:L].rearrange("p (c t) -> p c t", t=128), GB[P, 0:L].rearrange("p (c t) -> p c t", t=128), gb3, ALU.add)
                    act(GB[P, 0:L], GB[P, 0:L], AF.Exp, 0.0, -1.0)
                    for h in range(4):
                        ts("dve", fsm[P, 0:nb], fs[P, 0:nb], ident[P, p0 + h:p0 + h + 1], None, ALU.mult)
                        pt = PSS[h % 2]
                        mm(pt[:, 0:nb], ones_f[P, :], fsm[P, 0:nb])
                        cp("dve", fsb[:, dr * 4 + h, 0:nb], pt[:, 0:nb])
                for dr in range(2):
                    for h in range(4):
                        if is_p:
                            mset("pool", C32[dr][h][:], 0.0)
                        else:
                            ld(C32[dr][h][:, 0:128], st_c[l, dr, h, :, :])
                            ld(C32[dr][h][:, 128:129], st_n[l, dr, h, :].rearrange("(p o) -> p o", o=1))
                        b0 = 0 if dr == 0 else nb - 1
                        ts("dve", C32[dr][h][:], C32[dr][h][:], fsb[:, dr * 4 + h, b0:b0 + 1], None, ALU.mult)
                        cp("pool", Cb[dr][h][:], C32[dr][h][:])
                    if is_p:
                        mset("pool", SA[dr][:], 0.0)
                    else:
                        for h in range(4):
                            ld(SA[dr][:, h, :], st_h[l, dr, h, :, :])
                    for g in range(4):
                        if is_p:
                            mset("pool", hT32[dr][g][:], 0.0)
                        else:
                            for hh in range(4):
                                ld(sst[:], st_s[l, dr, g * 4 + hh, :, :])
                                pt = PSS[hh % 2]
                                tr(pt[:, 0:64], sst[:], ident[0:64, 0:64])
                                cp("dve", hT32[dr][g][:, hh, :], pt[:, 0:64])
                        cp("pool", hTb[dr][g][:].rearrange("p (h e) -> p h e", h=4), hT32[dr][g][:])
                iters = [(1, b) for b in range(nb - 1, -1, -1)] + [(0, b) for b in range(nb)]

                def gen_m(dr, b, par):
                    p0 = 32 * dr
                    P = slice(p0, p0 + 4)
                    mask = maskf if dr == 0 else maskb
                    tb = s0 + b * 128
                    last = (b == nb - 1) if dr == 0 else (b == 0)
                    bnext = b + 1 if dr == 0 else b - 1
                    ykey = f"yb_{sq}_{b}"
                    qT, kT, ktm, vtm = qT_t[par], kT_t[par], ktm_t[par], vtm_t[par]
                    if dr == 0:
                        tk.dma(yblm[:], S["yb_all"][tb:tb + 128, 0:512], track="L_yblm", outs=[yblm], ins=[ykey + "m"])
                        ld(gate_m[:], S["go_tm"][tb:tb + 128, :])
                    tr(PM1[:, 0:4], GA[P, b * 128:(b + 1) * 128], ident[P, p0:p0 + 4])
                    tr(PM1[:, 4:8], GB[P, b * 128:(b + 1) * 128], ident[P, p0:p0 + 4])
                    cp("dve", gsc[:, 0:8], PM1[:, 0:8])
                    yield
                    tt("pool", vaug[:, :, 0:128], vtm[:].rearrange("p (h e) -> p h e", h=4), gsc[:, 0:4].unsqueeze(2).to_broadcast([128, 4, 128]), ALU.mult)
                    cp("pool", vaug[:, :, 128], gsc[:, 0:4])
                    for h in range(4):
                        mm(PM0[:, h * 128:(h + 1) * 128], kT[:, h, :], qT[:, h, :])
                    yield
                    tt("dve", PTm[:], PM0[:].rearrange("p (h t) -> p h t", h=4), mask[:].unsqueeze(1).to_broadcast([128, 4, 128]), ALU.mult)
                    yield

                    def pov(h, n):
                        return (PM1 if h < 2 else PM0)[:, (h % 2) * 256:(h % 2) * 256 + n]
                    for h in range(4):
                        mm(pov(h, 129), PTm[:, h, :], vaug[:, h, :], start=True, stop=False)
                        mm(pov(h, 129), qT[:, h, :], Cb[dr][h][:], start=False, stop=True)
                    yield
                    for h in range(4):
                        act(dn[:, h:h + 1], pov(h, 129)[:, 128:129], AF.Abs)
                    yield
                    tt("dve", dn[:], dn[:], gsc[:, 4:8], ALU.max)
                    tk.op("dve", lambda e: e.reciprocal(out=dn[:], in_=dn[:]), [dn], [dn])
                    yield
                    for h in range(4):
                        if dr == 1:
                            act(ycm[:, h * 128:(h + 1) * 128], pov(h, 128), AF.Copy, 0.0, dn[:, h:h + 1])
                        else:
                            stt("dve", ycm[:, h * 128:(h + 1) * 128], pov(h, 128), dn[:, h:h + 1], yblm[:, h * 128:(h + 1) * 128], ALU.mult, ALU.add)
                    yield
                    for h in range(4):
                        pd = (PM1 if h < 2 else PM0)[:, (h % 2) * 256:(h % 2) * 256 + 129]
                        mm(pd, ktm[:, h * 128:(h + 1) * 128], vaug[:, h, :])
                    yield
                    for h in range(4):
                        pd = (PM1 if h < 2 else PM0)[:, (h % 2) * 256:(h % 2) * 256 + 129]
                        tt("dve", C32[dr][h][:], C32[dr][h][:], pd, ALU.add)
                        if not last:
                            ts("pool", C32[dr][h][:], C32[dr][h][:], fsb[:, dr * 4 + h, bnext:bnext + 1], None, ALU.mult)
                            cp("pool", Cb[dr][h][:], C32[dr][h][:])
                        elif is_p:
                            st(o_c[sq, l, dr, h, :, :], C32[dr][h][:, 0:128])
                            st(o_n[sq, l, dr, h, :].rearrange("(p o) -> p o", o=1), C32[dr][h][:, 128:129])
                    yield
                    if dr == 1:
                        tk.dma(S["yb_all"][tb:tb + 128, 0:512], ycm[:], track="S_ycm", ins=[ycm], outs=[ykey + "m"])
                    else:
                        h3 = ycm[:].rearrange("p (h e) -> p h e", h=4)
                        rms_rows("act", rstm[:, 0:4], h3, 4, 128, tmpm[:].rearrange("p (h e) -> p h e", h=4))
                        yield
                        for h in range(4):
                            stt("dve", ycm[:, h * 128:(h + 1) * 128], ycm[:, h * 128:(h + 1) * 128], rstm[:, h:h + 1], mnb[:, h * 128:(h + 1) * 128], ALU.mult, ALU.mult)
                        yield
                        tt("dve", ybfm[:], ycm[:], gate_m[:], ALU.mult)
                        yield
                        for h in range(4):
                            tr(psTb[:, h * 128:(h + 1) * 128], ybfm[:, h * 128:(h + 1) * 128], ident_b[:])
                        cp("act", yTm[:], psTb[:, 0:512].rearrange("p (h t) -> p h t", h=4))
                        st(S["ymT"][:, tb:tb + 128].rearrange("(h f) t -> f h t", f=128), yTm[:])

                def front_h(dr, b, par):
                    hq, kk, lf = hq_t[par], kk_t[par], lf_t[par]
                    for h in range(4):
                        scan(cp_t[:, h, :], ones_f[:, 0:128], lf[:, h, :], 0.0, ALU.mult, ALU.add)
                    c4 = cp_t[:].rearrange("p h (c t) -> p h c t", c=NC)
                    if dr == 0:
                        src4 = c4
                        cp("dve", refg[:], c4[:, :, :, 15])
                        cp("dve", egm[:, :, 0], c4[:, :, 0, 15])
                        tt("dve", egm[:, :, 1:NC], c4[:, :, 1:NC, 15], c4[:, :, 0:NC - 1, CS - 1], ALU.subtract)
                        tt("dve", egd[:], c4[:, :, :, CS - 1], c4[:, :, :, 15], ALU.subtract)
                        sgn = 1.0
                    else:
                        tt("dve", dd_t[:], cp_t[:], lf[:], ALU.subtract)
                        e4 = dd_t[:].rearrange("p h (c t) -> p h c t", c=NC)
                        src4 = e4
                        cp("dve", refg[:], e4[:, :, :, 16])
                        tt("dve", egm[:], c4[:, :, :, CS - 1], e4[:, :, :, 16], ALU.subtract)
                        tt("dve", egd[:], e4[:, :, :, 16], e4[:, :, :, 0], ALU.subtract)
                        sgn = -1.0
                    tt("dve", ep_t[:].rearrange("p h (c t) -> p h c t", c=NC), src4, refg[:].unsqueeze(3).to_broadcast([128, 4, NC, CS]), ALU.subtract)
                    ts("dve", ep_t[:], ep_t[:], -43.0, 43.0, ALU.max, ALU.min)
                    yield
                    act(egm[:], egm[:], AF.Exp)
                    act(egd[:], egd[:], AF.Exp)
                    act(em_t[:], ep_t[:], AF.Exp, 0.0, -sgn)
                    act(ep_t[:], ep_t[:], AF.Exp, 0.0, sgn)
                    yield
                    em2, ep2 = dd_t, cp_t
                    tt("pool", em2[:].rearrange("p h (c t) -> p h c t", c=NC), em_t[:].rearrange("p h (c t) -> p h c t", c=NC),
                       egd[:].unsqueeze(3).to_broadcast([128, 4, NC, CS]), ALU.mult)
                    tt("pool", kt_t[:], kk[:], em_t[:], ALU.mult)
                    tt("pool", qt_t[:], hq[:], ep_t[:], ALU.mult)
                    yield
                    tt("pool", kt2[:], kk[:], em2[:], ALU.mult)
                    tt("pool", ep2[:].rearrange("p h (c t) -> p h c t", c=NC), ep_t[:].rearrange("p h (c t) -> p h c t", c=NC),
                       egm[:].unsqueeze(3).to_broadcast([128, 4, NC, CS]), ALU.mult)

                def back_h(dr, b, par):
                    mask = maskf if dr == 0 else maskb
                    tb = s0 + b * 128
                    last = (b == nb - 1) if dr == 0 else (b == 0)
                    ykey = f"yb_{sq}_{b}"
                    hq, hv = hq_t[par], hvc[par]
                    ep2 = cp_t
                    order = list(range(NC)) if dr == 0 else list(range(NC - 1, -1, -1))
                    if dr == 0:
                        tk.dma(yblh[:], S["yb_all"][tb:tb + 128, 512:1024], track="L_yblh", outs=[yblh], ins=[ykey + "h"])
                        ld(gate_h[:], S["hg_tm"][tb:tb + 128, :])
                    tt("dve", egt[:], egm[:], egd[:], ALU.mult)
                    for c2 in range(NC):
                        tt("pool", qt2p[c2][:, :, c2 * CS:(c2 + 1) * CS], hq[:, :, c2 * CS:(c2 + 1) * CS], ep2[:, :, c2 * CS:(c2 + 1) * CS], ALU.mult)
                    for half in range(2):
                        for cc in range(2):
                            c2 = half * 2 + cc
                            for h in range(4):
                                j = c2 * 4 + h
                                mm(PH0[0:CS, j * CS:(j + 1) * CS], kt_t[:, h, c2 * CS:(c2 + 1) * CS], qt_t[:, h, c2 * CS:(c2 + 1) * CS])
                                jj = cc * 4 + h
                                tr(psTb[0:CS, jj * 128:(jj + 1) * 128], kt2[:, h, c2 * CS:(c2 + 1) * CS], ident_b[:])
                        cp("act", kttm[:, half * 8:(half + 1) * 8, :], psTb[0:CS, :].rearrange("p (j k) -> p j k", j=8))
                    yield
                    PT4 = PThp[:].rearrange("p (c h) t -> p c h t", c=NC)
                    pS4 = PH0[0:CS, :].rearrange("p (c h t) -> p c h t", c=NC, h=4)
                    for c2 in range(NC):
                        tt("dve", PT4[:, c2, :, c2 * CS:(c2 + 1) * CS], pS4[:, c2, :, :], mask[0:CS, 0:CS].unsqueeze(1).to_broadcast([CS, 4, CS]), ALU.mult)
                    PD = [PH1, PH2]

                    def emit_pd(i):
                        c2 = order[i]
                        for h in range(4):
                            mm(PD[i % 2][:, h * 128:(h + 1) * 128], kttm[:, c2 * 4 + h, :], hv[c2][:, h * 128:(h + 1) * 128])
                    emit_pd(0)
                    emit_pd(1)
                    yield
                    Scur, Soth = SA[dr], SB
                    mm(PH0[:, :], zb[0:1, 0:128], zb[0:1, 0:512], start=True, stop=False)
                    for i, c2 in enumerate(order):
                        cp("act", Sbb[i % 2][:], Scur[:])
                        for h in range(4):
                            j = c2 * 4 + h
                            mm(PH0[:, h * 128:(h + 1) * 128], PThp[:, j, :], hv[c2][:, h * 128:(h + 1) * 128], start=False, stop=False)
                            mm(PH0[:, h * 128:(h + 1) * 128], qt2p[c2][:, h, :], Sbb[i % 2][:, h, :], start=False, stop=(i == NC - 1))
                        tt("dve", Soth[:], Scur[:], egt[:, :, c2].unsqueeze(2).to_broadcast([128, 4, 128]), ALU.mult)
                        tt("dve", Soth[:], Soth[:], PD[i % 2][:].rearrange("p (h e) -> p h e", h=4), ALU.add)
                        Scur, Soth = Soth, Scur
                        if i + 2 < NC:
                            emit_pd(i + 2)
                        yield
                    assert Scur is SA[dr]
                    if last and is_p:
                        for h in range(4):
                            st(o_h[sq, l, dr, h, :, :], SA[dr][:, h, :])
                    if dr == 1:
                        cp("act", ych[:], PH0[:])
                        tk.dma(S["yb_all"][tb:tb + 128, 512:1024], ych[:], track="S_ych", ins=[ych], outs=[ykey + "h"])
                    else:
                        tt("dve", ych[:], PH0[:], yblh[:], ALU.add)
                        h3 = ych[:].rearrange("p (h e) -> p h e", h=4)
                        rms_rows("act", rsth[:, 0:4], h3, 4, 128, tmph[:].rearrange("p (h e) -> p h e", h=4))
                        yield
                        for h in range(4):
                            stt("dve", ych[:, h * 128:(h + 1) * 128], ych[:, h * 128:(h + 1) * 128], rsth[:, h:h + 1], hnb[:, h * 128:(h + 1) * 128], ALU.mult, ALU.mult)
                        yield
                        tt("dve", ybfh[:], ych[:], gate_h[:], ALU.mult)
                        yield
                        for h in range(4):
                            tr(psTb[:, h * 128:(h + 1) * 128], ybfh[:, h * 128:(h + 1) * 128], ident_b[:])
                        cp("act", yTh[:], psTb[:, 0:512].rearrange("p (h t) -> p h t", h=4))
                        st(S["yhT"][:, tb:tb + 128].rearrange("(h f) t -> f h t", f=128), yTh[:])

                def front_s(dr, b, par):
                    negm = negf if dr == 0 else negb
                    dt_, cm, bmT, xs = dta[par], cm_t[par], bmT_t[par], xs_t[par]
                    gs, eb, xp, xpp, PTs = gs2[par], eb2[par], xp2[par], xpp2[par], PTs2[par]
                    ts("dve", da[:], dt_[:], nega2[:, dr:dr + 1], None, ALU.mult)
                    scan(lcs[:], ones_f[0:16, 0:128], da[:], 0.0, ALU.mult, ALU.add)
                    if dr == 0:
                        cp("dve", fm4[:, 0, :], lcs[:])
                        lae = fm4[:, 0, 127:128]
                    else:
                        stt("dve", fm4[:, 0, :], da[:], lcs[:, 127:128], lcs[:], ALU.add, ALU.subtract)
                        lae = fm4[:, 0, 0:1]
                    cp("dve", fm4[:, 1, :], dt_[:])
                    ts("dve", fm4[:, 3, :], fm4[:, 0, :], lae, -1.0, ALU.subtract, ALU.mult)
                    yield
                    act(fm4[:, 2, :], fm4[:, 0, :], AF.Exp)
                    act(elae[:], lae, AF.Exp)
                    act(fm4[:, 3, :], fm4[:, 3, :], AF.Exp)
                    yield
                    tt("dve", fm4[:, 3, :], fm4[:, 3, :], dt_[:], ALU.mult)
                    ts("dve", ebd[:], ident[0:16, 0:16], elae[:, 0:1], None, ALU.mult)
                    tt("pool", labd[:], fm4[:, 0, :].unsqueeze(1).to_broadcast([16, 16, 128]), ident[0:16, 0:16].unsqueeze(2).to_broadcast([16, 16, 128]), ALU.mult)
                    yield
                    pt = nb_s()
                    for q in range(4):
                        tr(pt[:, q * 16:(q + 1) * 16], fm4[:, q, :], ident[0:16, 0:16])
                    mm(pt[:, 64:80], ones_f[0:16, :], ebd[:])
                    cp("dve", gs[:], pt[:, 0:64].rearrange("p (q h) -> p q h", q=4))
                    cp("dve", eb[:], pt[:, 64:80])
                    ts("dve", ngla[:], gs[:, 0, :], -1.0, None, ALU.mult)
                    pcb = nb_s()
                    for g in range(4):
                        mm(pcb[:, g * 128:(g + 1) * 128], bmT[:, g, :], cm[:, g, :])
                    cp("act", cb_all[:], pcb[:].rearrange("p (g t) -> p g t", g=4))
                    yield
                    tt("pool", xp[:], xs[:].rearrange("p (h e) -> p h e", h=16), gs[:, 1, :].unsqueeze(2).to_broadcast([128, 16, 64]), ALU.mult)
                    tt("pool", xpp[:].rearrange("p (h e) -> p h e", h=16), xs[:].rearrange("p (h e) -> p h e", h=16), gs[:, 3, :].unsqueeze(2).to_broadcast([128, 16, 64]), ALU.mult)
                    for g in range(4):
                        pb = nb_s()
                        mm(pb[:], ones_f[0:16, :], labd[:, g * 4:(g + 1) * 4, :])
                        dg = ddm[g % 2]
                        tt("dve", dg[:], pb[:].rearrange("p (h t) -> p h t", h=4), negm.unsqueeze(1).to_broadcast([128, 4, 128]), ALU.add)
                        yield
                        for hh in range(4):
                            act(dg[:, hh, :], dg[:, hh, :], AF.Exp, ngla[:, g * 4 + hh:g * 4 + hh + 1], 1.0)
                        yield
                        tt("pool", PTs[g][:], dg[:], cb_all[:, g, :].unsqueeze(1).to_broadcast([128, 4, 128]), ALU.mult)

                def back_s(dr, b, par):
                    tb = s0 + b * 128
                    last = (b == nb - 1) if dr == 0 else (b == 0)
                    ykey = f"yb_{sq}_{b}"
                    cm, bmtm, xs = cm_t[par], bmtm_t[par], xs_t[par]
                    gs, eb, xp, xpp, PTs = gs2[par], eb2[par], xp2[par], xpp2[par], PTs2[par]
                    if dr == 0:
                        tk.dma(ybls[:], S["yb_all"][tb:tb + 128, 1024:2048], track="L_ybls", outs=[ybls], ins=[ykey + "s"])
                        ld(gate_s[:], S["sz_tm"][tb:tb + 128, :])
                    for g in range(4):
                        pyi = nb_s()
                        for hh in range(4):
                            hd = g * 4 + hh
                            mm(pyi[:, hh * 64:(hh + 1) * 64], PTs[g][:, hh, :], xp[:, hd, :])
                        mm(pyi[:, 256:512], cm[:, g, :], hTb[dr][g][:])
                        pg = pis[g % 2]
                        tt("dve", pg[:].rearrange("p (h e) -> p h e", h=4), pyi[:, 256:512].rearrange("p (h e) -> p h e", h=4), gs[:, 2, g * 4:(g + 1) * 4].unsqueeze(2).to_broadcast([128, 4, 64]), ALU.mult)
                        yo = ycs[:, g * 256:(g + 1) * 256]
                        tt("dve", yo, pg[:], pyi[:, 0:256], ALU.add)
                        if g % 2 == 1:
                            yield
                    if dr == 0:
                        tt("pool", ycs[:], ycs[:], ybls[:], ALU.add)
                    for g in range(4):
                        pd = nb_s()
                        mm(pd[:, 0:256], bmtm[:, g * 128:(g + 1) * 128], xpp[:, g * 256:(g + 1) * 256])
                        tt("dve", tmps[:], hT32[dr][g][:], eb[:, g * 4:(g + 1) * 4].unsqueeze(2).to_broadcast([128, 4, 64]), ALU.mult)
                        tt("dve", hT32[dr][g][:], tmps[:], pd[:, 0:256].rearrange("p (h e) -> p h e", h=4), ALU.add)
                        if not last:
                            cp("pool", hTb[dr][g][:].rearrange("p (h e) -> p h e", h=4), hT32[dr][g][:])
                        elif is_p:
                            for hh in range(4):
                                pq = nb_s()
                                tr(pq[0:64, 0:128], hT32[dr][g][:, hh, :], ident[:, :])
                                cp("dve", sst[:], pq[0:64, 0:128])
                                st(o_s[sq, l, dr, g * 4 + hh, :, :], sst[:])
                        if g % 2 == 1:
                            yield
                    if dr == 1:
                        tk.dma(S["yb_all"][tb:tb + 128, 1024:2048], ycs[:], track="S_ycs", ins=[ycs], outs=[ykey + "s"])
                    else:
                        tt("pool", tmpS[:].rearrange("p (h e) -> p h e", h=16), xs[:].rearrange("p (h e) -> p h e", h=16), sdb[:].unsqueeze(2).to_broadcast([128, 16, 64]), ALU.mult)
                        yield
                        tt("dve", ycs[:], ycs[:], tmpS[:], ALU.add)
                        tt("dve", ycs[:], ycs[:], gate_s[:], ALU.mult)
                        rms_rows("act", rsts[:, 0:1], ycs[:].unsqueeze(1), 1, 1024, tmpS[:].unsqueeze(1))
                        yield
                        stt("dve", ybfs[:], ycs[:], rsts[:, 0:1], snb[:], ALU.mult, ALU.mult)
                        yield
                        for q in range(2):
                            for h in range(4):
                                j = q * 4 + h
                                tr(psTb[:, h * 128:(h + 1) * 128], ybfs[:, j * 128:(j + 1) * 128], ident_b[:])
                            cp("act", yTs[:, q * 4:q * 4 + 4, :], psTb[:, 0:512].rearrange("p (h t) -> p h t", h=4))
                        st(S["ysT"][:, tb:tb + 128].rearrange("(h f) t -> f h t", f=128), yTs[:])

                def emit_loads(k):
                    dr_, b_ = iters[k]
                    tb_ = s0 + b_ * 128
                    loads_h(dr_, tb_, k % 2)
                    loads_s(dr_, tb_, k % 2)
                    loads_m(dr_, tb_, k % 2)

                emit_loads(0)
                run_gens([front_h(iters[0][0], iters[0][1], 0), front_s(iters[0][0], iters[0][1], 0)])
                for k, (dr, b) in enumerate(iters):
                    gl = [back_h(dr, b, k % 2), back_s(dr, b, k % 2), gen_m(dr, b, k % 2)]
                    if k + 1 < len(iters):
                        emit_loads(k + 1)
                        dn_, bn_ = iters[k + 1]
                        gl += [front_h(dn_, bn_, (k + 1) % 2), front_s(dn_, bn_, (k + 1) % 2)]
                    run_gens(gl)
            tk.barrier()
            pst.close()
            cur_es[0] = es

        def phase_O(l, final):
            pst = ExitStack()
            cur_es[0] = pst
            alloc_P(full=False)
            xt_b, hT = B_["xt_b"], B_["hT"]
            yT_t = sb([128, 16, TT], BF16, "yT_t")
            bg_t = sb([128, 8, TT], BF16, "bg_t")
            mrg = sb([128, 8, TT], BF16, "mrg")
            t1 = sb([128, TT], F32, "t1")
            t2 = sb([128, TT], F32, "t2")
            actT = sb([128, 22, TT], BF16, "actT")
            sa = sb([128, TT], F32, "sa")
            wdn = [sb([128, 22, 128], BF16, f"wdn{i}") for i in range(3)]
            wdi = {"i": 0}
            nfT = sb([128, 8], F32, "nfT")
            otm = sb([128, D], F32, "otm")
            if final:
                vec_to_fm(nfT[:, :], norm_f.rearrange("(c p) -> c p", p=128), 8)
            def ld_yT(t0_):
                ld(yT_t[:, 0:4, :], S["ymT"][:, t0_:t0_ + TT].rearrange("(c p) t -> p c t", p=128))
                ld(yT_t[:, 4:8, :], S["yhT"][:, t0_:t0_ + TT].rearrange("(c p) t -> p c t", p=128))
                ld(yT_t[:, 8:16, :], S["ysT"][:, t0_:t0_ + TT].rearrange("(c p) t -> p c t", p=128))

            for ti in range(NTT):
                t0 = ti * TT
                cidx = 0 if ti < 2 else 1
                xt = xt_b[ti % 2]
                if ti == 0:
                    ld(xt[:], xT[:, :, t0:t0 + TT].rearrange("c p t -> p c t"))
                    ld_yT(0)
                if ti + 1 < NTT:
                    ld(xt_b[(ti + 1) % 2][:], xT[:, :, t0 + TT:t0 + 2 * TT].rearrange("c p t -> p c t"))
                for bi, (wn, kc0, nk) in enumerate((("bm", 0, 4), ("bh", 4, 4), ("bs", 8, 8))):
                    ld(bg_t[:], S["bgT"][bi * 1024:(bi + 1) * 1024, t0:t0 + TT].rearrange("(c p) t -> p c t", p=128))
                    for half in range(2):
                        w = load_w(wn, l, 0, nk * 128, half * 512, 512)
                        for j in range(4):
                            fc = half * 4 + j
                            pt = npsf()
                            for kc in range(nk):
                                mm(pt[:], w[:, kc, j * 128:(j + 1) * 128], yT_t[:, kc0 + kc, :], start=(kc == 0), stop=(kc == nk - 1))
                            if bi == 0:
                                tt("dve", mrg[:, fc, :], pt[:], bg_t[:, fc, :], ALU.mult)
                            else:
                                tt("dve", t1[:], pt[:], bg_t[:, fc, :], ALU.mult)
                                tt("pool", mrg[:, fc, :], mrg[:, fc, :], t1[:], ALU.add)
                if ti + 1 < NTT:
                    ld_yT(t0 + TT)
                for half in range(2):
                    w = load_w("out", l, 0, D, half * 512, 512)
                    for j in range(4):
                        fc = half * 4 + j
                        pt = npsf()
                        for kc in range(8):
                            mm(pt[:], w[:, kc, j * 128:(j + 1) * 128], mrg[:, kc, :], start=(kc == 0), stop=(kc == 7))
                        stt("dve", xt[:, fc, :], pt[:], MOD[:, l, 2, fc, cidx:cidx + 1], xt[:, fc, :], ALU.mult, ALU.add)
                norm_mod(l, ti, 4, 3, xt)
                for fb in range(0, 22, 4):
                    nf = min(4, 22 - fb)
                    wa_ = load_w("gu", l, 0, D, fb * 128, nf * 128)
                    wb_ = load_w("gu", l, 0, D, D_FF + fb * 128, nf * 128)
                    for j in range(nf):
                        pa = npsf()
                        for kc in range(8):
                            mm(pa[:], wa_[:, kc, j * 128:(j + 1) * 128], hT[:, kc, :], start=(kc == 0), stop=(kc == 7))
                        pb = npsf()
                        for kc in range(8):
                            mm(pb[:], wb_[:, kc, j * 128:(j + 1) * 128], hT[:, kc, :], start=(kc == 0), stop=(kc == 7))
                        act(sa[:], pa[:], AF.Silu)
                        tt("dve", actT[:, fb + j, :], sa[:], pb[:], ALU.mult)
                for fc in range(8):
                    wdi["i"] += 1
                    wd = wdn[wdi["i"] % 3]
                    ld(wd[:], wb["down"][1][l, :, fc * 128:(fc + 1) * 128].rearrange("(kc p) n -> p kc n", p=128))
                    pt = npsf()
                    for kc in range(22):
                        mm(pt[:], wd[:, kc, :], actT[:, kc, :], start=(kc == 0), stop=(kc == 21))
                    stt("dve", xt[:, fc, :], pt[:], MOD[:, l, 5, fc, cidx:cidx + 1], xt[:, fc, :], ALU.mult, ALU.add)
                if not final:
                    st(xT[:, :, t0:t0 + TT].rearrange("c p t -> p c t"), xt[:])
                else:
                    act(B_["sq_b"][:], xt[:], AF.Square)
                    pt = npsf()
                    for c in range(8):
                        mm(pt[:], ones_b[:], B_["sq_b"][:, c, :], start=(c == 0), stop=(c == 7))
                    act(B_["rstd"][:], pt[:], AF.Ln, EPS, 1.0 / D)
                    act(B_["rstd"][:], B_["rstd"][:], AF.Exp, 0.0, -0.5)
                    tt("dve", B_["hn_f"][:], xt[:], B_["rstd"][:].unsqueeze(1).to_broadcast([128, 8, TT]), ALU.mult)
                    tt("dve", B_["hn_f"][:], B_["hn_f"][:], nfT[:].unsqueeze(2).to_broadcast([128, 8, TT]), ALU.mult)
                    for s4 in range(4):
                        for half in range(2):
                            pt = npsf()
                            for c4 in range(4):
                                c = half * 4 + c4
                                tr(pt[:, c4 * 128:(c4 + 1) * 128], B_["hn_f"][:, c, s4 * 128:(s4 + 1) * 128], ident)
                            cp("act" if half else "dve", otm[:, half * 512:(half + 1) * 512], pt[:])
                        st(y_out[t0 + s4 * 128:t0 + (s4 + 1) * 128, :], otm[:])
            tk.barrier()
            pst.close()
            cur_es[0] = es

        for l in range(nlayers):
            if "P" in secs:
                phase_P(l)
            if "S" in secs:
                phase_S(l)
            if "O" in secs:
                phase_O(l, l == nlayers - 1)

        tk.barrier()
        block = es.enter_context(nc.Block())
        tk.emit(block)
    return nc, dbg


def make_consts():
    c = np.zeros((128, 768), np.float32)
    c[:, 0:128] = np.eye(128)
    s = np.arange(128)[:, None]
    t = np.arange(128)[None, :]
    c[:, 128:256] = (s <= t)
    c[:, 256:384] = (s >= t)
    c[:, 384:512] = np.where(s <= t, 0.0, -1e30)
    c[:, 512:640] = np.where(s >= t, 0.0, -1e30)
    c[:, 640:704] = np.arange(64)[None, :]
    c[:, 704] = np.arange(128)
    return c


def make_in_maps(inputs):
    f = lambda a: np.ascontiguousarray(a, dtype=np.float32)
    shared = {k: f(inputs[k]) for k in ("w_ada", "b_ada", "norm1", "norm2", "w_in", "m_norm", "h_lb", "h_norm", "s_conv_w",
                                        "s_conv_b", "s_d", "s_norm", "w_bm", "w_bh", "w_bs", "w_out", "w_gu", "w_down", "norm_f")}
    shared["m_bi"] = f(inputs["m_bi"]).reshape(DEPTH, 8)
    shared["m_bf"] = f(inputs["m_bf"]).reshape(DEPTH, 8)
    shared["s_dt_bias"] = f(inputs["s_dt_bias"]).reshape(DEPTH, 32)
    shared["s_a_log"] = f(inputs["s_a_log"]).reshape(DEPTH, 32)
    shared["cst"] = make_consts()
    maps = []
    for i in range(8):
        m = dict(shared)
        m["x_in"] = np.concatenate([f(inputs["x_prompt"][4 * i:4 * i + 4]).reshape(1024, D), f(inputs["x_sample"][i])], 0)
        m["st_c"] = f(inputs["state_mlstm_c"][i])
        m["st_n"] = f(inputs["state_mlstm_n"][i])
        m["st_m"] = f(inputs["state_mlstm_m"][i])
        m["st_h"] = f(inputs["state_hgrn"][i])
        m["st_s"] = f(inputs["state_ssm"][i])
        m["cvec"] = np.stack([f(inputs["c_ctx"]), f(inputs["c"][i])], 0)
        maps.append(m)
    return maps


def kernel(**inputs):
    nc, _ = build()
    maps = make_in_maps(inputs)
    res = run_bass_kernel_spmd(nc, maps, core_ids=list(range(8)))
    R = res.results
    y_prompt = np.concatenate([r["y_out"][0:1024].reshape(4, 256, D) for r in R], 0)
    y_sample = np.stack([r["y_out"][1024:] for r in R], 0)
    outs = [y_prompt, y_sample]
    for nm in ("o_c", "o_n", "o_m", "o_h", "o_s"):
        outs.append(np.concatenate([r[nm] for r in R], 0))
    return tuple(np.ascontiguousarray(o, dtype=np.float32) for o in outs)
```

```python
import numpy as np
from contextlib import ExitStack
import concourse.bass as bass
import concourse.mybir as mybir
from concourse.bass_utils import run_bass_kernel_spmd

F32 = mybir.dt.float32
BF16 = mybir.dt.bfloat16
ALU = mybir.AluOpType
AF = mybir.ActivationFunctionType
AX = mybir.AxisListType

D = 1024
DEPTH = 4
NT = 5120
TT = 512
NTT = NT // TT
N_IN = 10800
D_FF = 2816
EPS = 1e-6
C_MQ, C_MK, C_MV, C_MO, C_MI, C_MF = 0, 512, 1024, 1536, 2048, 2056
C_HQ, C_HI, C_HG, C_HF = 2064, 2576, 3088, 3600
C_SZ, C_SX, C_SDT, C_BG = 4624, 5648, 7696, 7728
SEQS = [(0, 256, 256, 0), (256, 256, 256, 0), (512, 256, 256, 0), (768, 256, 256, 0), (1024, 4096, 64, 1)]


class TK:
    CE = ("pe", "act", "dve", "pool")

    def __init__(self, nc, es):
        self.nc, self.es = nc, es
        self.ops = {e: [] for e in ("pe", "act", "dve", "pool", "sp")}
        self.lastw, self.rd = {}, {}
        self.known = {e: {} for e in self.ops}
        self.dcount, self.dsem = {}, {}
        self.needed = {e: set() for e in self.CE}
        self.csem = {e: self.es.enter_context(nc.semaphore("c_" + e)) for e in self.CE}
        self.trackmap = {}

    def _deps(self, eng, reads, writes, extra=()):
        deps = {}

        def add(t, v):
            if v > deps.get(t, 0):
                deps[t] = v
        for r in reads:
            lw = self.lastw.get(r)
            if lw:
                add(*lw)
        for w in writes:
            lw = self.lastw.get(w)
            if lw:
                add(*lw)
            for t, v in self.rd.get(w, {}).items():
                add(t, v)
        for t, v in extra:
            add(t, v)
        out = []
        kn = self.known[eng]
        for t, v in deps.items():
            if t == eng and eng == "pe":
                continue
            if kn.get(t, 0) >= v:
                continue
            kn[t] = v
            out.append((t, v))
            if t in self.CE:
                self.needed[t].add(v)
        return out

    def _mark(self, tv, reads, writes):
        for w in writes:
            self.lastw[w] = tv
            self.rd[w] = {}
        for r in reads:
            d = self.rd.setdefault(r, {})
            if tv[1] > d.get(tv[0], 0):
                d[tv[0]] = tv[1]

    @staticmethod
    def _keys(aps):
        ks = []
        for a in aps:
            if a is None or isinstance(a, (int, float)):
                continue
            ks.append(a if isinstance(a, str) else getattr(a, "tensor", a).name)
        return ks

    def op(self, eng, fn, outs, ins):
        writes, reads = self._keys(outs), self._keys(ins)
        waits = self._deps(eng, reads, writes)
        idx = len(self.ops[eng]) + 1
        self.ops[eng].append((waits, fn, None))
        self._mark((eng, idx), reads, writes)

    def dma(self, out, in_, track, outs=None, ins=None, eng="sp", slow=False):
        writes = self._keys(outs if outs is not None else [])
        reads = self._keys(ins if ins is not None else [])
        track = self.trackmap.setdefault(track, f"T{len(self.trackmap) % 64}")
        extra = []
        if self.dcount.get(track, 0):
            extra.append((track, self.dcount[track]))
        waits = self._deps(eng, reads, writes, extra)
        self.dcount[track] = self.dcount.get(track, 0) + 16
        if track not in self.dsem:
            self.dsem[track] = self.es.enter_context(self.nc.semaphore("d_" + track))
        if slow:
            self.ops[eng].append((waits, lambda e: e.dma_start(out=out, in_=in_, allow_slow_non_contiguous=True), track))
        else:
            self.ops[eng].append((waits, lambda e: e.dma_start(out=out, in_=in_), track))
        self._mark((track, self.dcount[track]), reads, writes)

    def barrier(self):
        cur = []
        for e in self.CE:
            if self.ops[e]:
                cur.append((e, len(self.ops[e])))
        for t, c in self.dcount.items():
            cur.append((t, c))
        fixed = []
        for t, v in cur:
            if t in self.CE:
                ops = self.ops[t]
                while v > 0 and (ops[v - 1][1] is None or ops[v - 1][2] is not None):
                    v -= 1
                if v == 0:
                    continue
            fixed.append((t, v))
        for e in self.ops:
            waits = self._deps(e, [], [], fixed)
            if waits:
                self.ops[e].append((waits, None, None))
        self.lastw.clear()
        self.rd.clear()

    def emit(self, block):
        nc = self.nc
        sem = self.csem
        rank = {e: {v: i + 1 for i, v in enumerate(sorted(self.needed[e]))} for e in self.CE}

        def run(name, eng):
            for idx, (waits, fn, track) in enumerate(self.ops[name], 1):
                for t, v in waits:
                    if t in self.CE:
                        eng.wait_ge(sem[t], rank[t][v])
                    else:
                        eng.wait_ge(self.dsem[t], v)
                if fn is None:
                    continue
                ins = fn(eng)
                if track is not None:
                    ins.then_inc(self.dsem[track], 16)
                elif idx in self.needed.get(name, ()):
                    ins.then_inc(sem[name], 1)

        @block.sync
        def _(e):
            run("sp", e)

        @block.tensor
        def _(e):
            run("pe", e)

        @block.scalar
        def _(e):
            run("act", e)

        @block.vector
        def _(e):
            run("dve", e)

        @block.gpsimd
        def _(e):
            run("pool", e)


def build(debug=False, nlayers=DEPTH, stop_after=None, secs="ABCDEPSO"):
    nc = bass.Bass("TRN2", target_bir_lowering=False)
    dbg = {}

    def din(name, shape, dt=F32):
        return nc.dram_tensor(name, list(shape), dt, kind="ExternalInput").ap()

    def dout(name, shape, dt=F32):
        return nc.dram_tensor(name, list(shape), dt, kind="ExternalOutput").ap()

    def dscr(name, shape, dt=F32):
        if debug:
            dbg[name] = (shape, dt)
            return nc.dram_tensor(name, list(shape), dt, kind="ExternalOutput").ap()
        return nc.dram_tensor(name, list(shape), dt, kind="Internal").ap()

    x_in = din("x_in", [NT, D])
    st_c = din("st_c", [DEPTH, 2, 4, 128, 128])
    st_n = din("st_n", [DEPTH, 2, 4, 128])
    st_m = din("st_m", [DEPTH, 2, 4])
    st_h = din("st_h", [DEPTH, 2, 4, 128, 128])
    st_s = din("st_s", [DEPTH, 2, 16, 64, 128])
    cvec = din("cvec", [2, D])
    w_ada = din("w_ada", [DEPTH, D, 6 * D])
    b_ada = din("b_ada", [DEPTH, 6 * D])
    norm1 = din("norm1", [DEPTH, D])
    norm2 = din("norm2", [DEPTH, D])
    w_in = din("w_in", [DEPTH, D, N_IN])
    m_bi = din("m_bi", [DEPTH, 8])
    m_bf = din("m_bf", [DEPTH, 8])
    m_norm = din("m_norm", [DEPTH, 512])
    h_lb = din("h_lb", [2, DEPTH, 512])
    h_norm = din("h_norm", [DEPTH, 512])
    s_conv_w = din("s_conv_w", [DEPTH, 3, 2048])
    s_conv_b = din("s_conv_b", [DEPTH, 2048])
    s_dt_bias = din("s_dt_bias", [DEPTH, 32])
    s_a_log = din("s_a_log", [DEPTH, 32])
    s_d = din("s_d", [DEPTH, 16])
    s_norm = din("s_norm", [DEPTH, 1024])
    w_bm = din("w_bm", [DEPTH, 512, D])
    w_bh = din("w_bh", [DEPTH, 512, D])
    w_bs = din("w_bs", [DEPTH, 1024, D])
    w_out = din("w_out", [DEPTH, D, D])
    w_gu = din("w_gu", [DEPTH, D, 2 * D_FF])
    w_down = din("w_down", [DEPTH, D_FF, D])
    norm_f = din("norm_f", [D])
    cst = din("cst", [128, 768])

    y_out = dout("y_out", [NT, D])
    o_c = dout("o_c", [4, DEPTH, 2, 4, 128, 128])
    o_n = dout("o_n", [4, DEPTH, 2, 4, 128])
    o_m = dout("o_m", [4, DEPTH, 2, 4])
    o_h = dout("o_h", [4, DEPTH, 2, 4, 128, 128])
    o_s = dout("o_s", [4, DEPTH, 2, 16, 64, 128])

    xT = dscr("xT", [8, 128, NT])
    wb = {
        "in": (w_in, nc.dram_tensor("wb_in", [DEPTH, D, N_IN], BF16, kind="Internal").ap(), D, N_IN),
        "bm": (w_bm, nc.dram_tensor("wb_bm", [DEPTH, 512, D], BF16, kind="Internal").ap(), 512, D),
        "bh": (w_bh, nc.dram_tensor("wb_bh", [DEPTH, 512, D], BF16, kind="Internal").ap(), 512, D),
        "bs": (w_bs, nc.dram_tensor("wb_bs", [DEPTH, 1024, D], BF16, kind="Internal").ap(), 1024, D),
        "out": (w_out, nc.dram_tensor("wb_out", [DEPTH, D, D], BF16, kind="Internal").ap(), D, D),
        "gu": (w_gu, nc.dram_tensor("wb_gu", [DEPTH, D, 2 * D_FF], BF16, kind="Internal").ap(), D, 2 * D_FF),
        "down": (w_down, nc.dram_tensor("wb_down", [DEPTH, D_FF, D], BF16, kind="Internal").ap(), D_FF, D),
    }
    S = {}
    for nm, rows in (("qTm", 512), ("kTm", 512), ("hqT", 512), ("kkT", 1024), ("cmT", 512), ("bmT", 512), ("bgT", 3072),
                     ("ymT", 512), ("yhT", 512), ("ysT", 1024)):
        S[nm] = dscr(nm, [rows, NT], BF16)
    for nm, rows in (("gi", 8), ("glf", 8), ("lfT", 1024), ("dtT", 32)):
        S[nm] = dscr(nm, [rows, NT], F32)
    for nm, cols in (("k_tm", 512), ("v_tm", 512), ("go_tm", 512), ("hv_tm", 512), ("hg_tm", 512), ("sz_tm", 1024),
                     ("xs_tm", 1024), ("bm_tm", 512)):
        S[nm] = dscr(nm, [NT, cols], BF16)
    S["yb_all"] = dscr("yb_all", [NT, 2048], F32)

    with ExitStack() as es:
        tk = TK(nc, es)
        _n = [0]

        pes = ExitStack()
        cur_es = [es]

        basename = {}
        usage = {}
        dbg['usage'] = usage

        def sb(shape, dt=F32, name=None):
            _n[0] += 1
            nm = f"{name or 't'}_{_n[0]}"
            basename[nm] = name or nm
            sz = int(np.prod(shape[1:])) * (2 if dt == BF16 else 4)
            usage[id(cur_es[0])] = usage.get(id(cur_es[0]), 0) + sz
            return cur_es[0].enter_context(nc.sbuf_tensor(nm, list(shape), dt))

        def ps(shape, dt=F32, name=None):
            _n[0] += 1
            return es.enter_context(nc.psum_tensor(name or f"p{_n[0]}", list(shape), dt))

        PS = [ps([128, 512], F32, f"psb{i}") for i in range(8)]
        psi = {"i": 0}

        def nps():
            psi["i"] += 1
            return PS[psi["i"] % 7]

        def mm(out, lhsT, rhs, start=True, stop=True):
            tk.op("pe", lambda e: e.matmul(out, lhsT=lhsT, rhs=rhs, start=start, stop=stop), [out], [lhsT, rhs])

        def tr(out, in_, ident):
            tk.op("pe", lambda e: e.transpose(out, in_, ident), [out], [in_, ident])

        def act(out, in_, func, bias=0.0, scale=1.0, eng="act"):
            tk.op(eng, lambda e: e.activation(out=out, in_=in_, func=func, bias=bias, scale=scale), [out], [in_, bias, scale])

        def tt(eng, out, in0, in1, op):
            tk.op(eng, lambda e: e.tensor_tensor(out=out, in0=in0, in1=in1, op=op), [out], [in0, in1])

        def ts(eng, out, in0, s1, s2, op0, op1=None):
            if op1 is None:
                tk.op(eng, lambda e: e.tensor_scalar(out=out, in0=in0, scalar1=s1, scalar2=None, op0=op0), [out], [in0, s1])
            else:
                tk.op(eng, lambda e: e.tensor_scalar(out=out, in0=in0, scalar1=s1, scalar2=s2, op0=op0, op1=op1), [out], [in0, s1, s2])

        def stt(eng, out, in0, scalar, in1, op0, op1):
            tk.op(eng, lambda e: e.scalar_tensor_tensor(out=out, in0=in0, scalar=scalar, in1=in1, op0=op0, op1=op1), [out], [in0, scalar, in1])

        def cp(eng, out, in_):
            if eng == "act":
                tk.op(eng, lambda e: e.copy(out=out, in_=in_), [out], [in_])
            else:
                tk.op(eng, lambda e: e.tensor_copy(out=out, in_=in_), [out], [in_])

        def mset(eng, out, val):
            tk.op(eng, lambda e: e.memset(out, val), [out], [])

        def red(eng, out, in_, op, axis=AX.X):
            tk.op(eng, lambda e: e.tensor_reduce(out=out, in_=in_, axis=axis, op=op), [out], [in_])

        def scan(out, d0, d1, init, op0, op1):
            tk.op("dve", lambda e: e.tensor_tensor_scan(out=out, data0=d0, data1=d1, initial=init, op0=op0, op1=op1), [out], [d0, d1, init])

        def ld(out, in_, eng="sp", slow=False):
            tk.dma(out, in_, track="L_" + basename[out.tensor.name], outs=[out], eng=("pool" if eng == "pool" else "sp"), slow=slow)

        def st(out, in_, eng="pool", key=None, slow=False):
            tk.dma(out, in_, track="S_" + basename[in_.tensor.name], ins=[in_], outs=([key] if key else []), eng=("pool" if eng == "poolq" else "sp"), slow=slow)

        vstage = sb([128, 128], F32, "vstage")

        def vec_to_fm(dst, src2d, n):
            ld(vstage[0:n, :], src2d)
            pt = nps()
            tr(pt[:, 0:n], vstage[0:n, :], ident[0:n, 0:n])
            cp("dve", dst, pt[:, 0:n])

        rr = {"i": 0}

        def anyeng(choices=("dve", "pool", "act")):
            rr["i"] += 1
            return choices[rr["i"] % len(choices)]

        cst_f = sb([128, 768], F32, "cst_f")
        ld(cst_f[:], cst[:, :])
        ident = cst_f[:, 0:128]
        ident_b = sb([128, 128], BF16, "ident_b")
        cp("dve", ident_b[:], ident)
        maskf = sb([128, 128], BF16, "maskf")
        maskb = sb([128, 128], BF16, "maskb")
        cp("dve", maskf[:], cst_f[:, 128:256])
        cp("dve", maskb[:], cst_f[:, 256:384])
        negf = cst_f[:, 384:512]
        negb = cst_f[:, 512:640]
        iota_r = cst_f[:, 640:704]
        pidx = cst_f[:, 704:705]
        ones_b = sb([128, 128], BF16, "ones_b")
        mset("dve", ones_b[:], 1.0)
        ones_f = sb([128, 128], F32, "ones_f")
        mset("dve", ones_f[:], 1.0)
        zeros_f = sb([128, 32], F32, "zeros_f")
        mset("pool", zeros_f[:], 0.0)

        MOD = sb([128, DEPTH, 6, 8, 2], F32, "MOD")
        LB = sb([128, 2, 4, DEPTH], F32, "LB")
        OMLB = sb([128, 2, 4, DEPTH], F32, "OMLB")
        E = sb([128, 4, 64], F32, "Etab")
        cur_es[0] = pes
        cw_in = [sb([128, 2048], F32, f"cw_in{i}") for i in range(4)]
        cw_out = [sb([128, 2048], BF16, f"cw_out{i}") for i in range(4)]
        k = 0
        for nm, (src, dst, rows, cols) in (wb.items() if "A" in secs else []):
            for l in range(nlayers):
                for r0 in range(0, rows, 128):
                    for c0 in range(0, cols, 2048):
                        cc = min(2048, cols - c0)
                        a, b = cw_in[k % 4], cw_out[k % 4]
                        ld(a[:, 0:cc], src[l, r0:r0 + 128, c0:c0 + cc])
                        cp(("dve", "act")[k % 2], b[:, 0:cc], a[:, 0:cc])
                        st(dst[l, r0:r0 + 128, c0:c0 + cc], b[:, 0:cc], eng="poolq")
                        k += 1


        cT = sb([128, 8, 2], F32, "cT")
        sgc = sb([128, 8, 2], F32, "sgc")
        for j in range(2):
            vec_to_fm(cT[:, :, j], cvec[j, :].rearrange("(c p) -> c p", p=128), 8)
        act(sgc[:], cT[:], AF.Silu)
        badaT = sb([128, DEPTH, 48], F32, "badaT")
        n1T = sb([128, DEPTH, 8], F32, "n1T")
        n2T = sb([128, DEPTH, 8], F32, "n2T")
        for l in range(nlayers):
            vec_to_fm(badaT[:, l, :], b_ada[l, :].rearrange("(c p) -> c p", p=128), 48)
            vec_to_fm(n1T[:, l, :], norm1[l, :].rearrange("(c p) -> c p", p=128), 8)
            vec_to_fm(n2T[:, l, :], norm2[l, :].rearrange("(c p) -> c p", p=128), 8)
        wa = [sb([128, 8, 768], F32, f"wa{i}") for i in range(2)]
        k = 0
        for l in range(nlayers if "B" in secs else 0):
            for cb in range(8):
                w = wa[k % 2]
                k += 1
                for dc in range(8):
                    ld(w[:, dc, :], w_ada[l, dc * 128:(dc + 1) * 128, cb * 768:(cb + 1) * 768], eng=("sp" if dc % 2 == 0 else "act"))
                pt = nps()
                for fc in range(6):
                    for dc in range(8):
                        mm(pt[:, fc * 2:fc * 2 + 2], w[:, dc, fc * 128:(fc + 1) * 128], sgc[:, dc, :], start=(dc == 0), stop=(dc == 7))
                for fc in range(6):
                    ch = cb * 6 + fc
                    kind, c8 = ch // 8, ch % 8
                    ts("dve", MOD[:, l, kind, c8, :], pt[:, fc * 2:fc * 2 + 2], badaT[:, l, ch:ch + 1], None, ALU.add)
            for kind, nT in ((1, n1T), (4, n2T)):
                ts("dve", MOD[:, l, kind, :, :], MOD[:, l, kind, :, :], 1.0, None, ALU.add)
                tt("dve", MOD[:, l, kind, :, :], MOD[:, l, kind, :, :], nT[:, l, :].unsqueeze(2).to_broadcast([128, 8, 2]), ALU.mult)

        hlb = sb([128, 2, 4, DEPTH], F32, "hlb")
        for dr in range(2):
            for l in range(DEPTH):
                vec_to_fm(hlb[:, dr, :, l], h_lb[dr, l, :].rearrange("(h p) -> h p", p=128), 4)
        hmx = sb([128, 2, 4], F32, "hmx")
        red("dve", hmx[:], hlb[:], ALU.max)
        tt("dve", hlb[:], hlb[:], hmx[:].unsqueeze(3).to_broadcast([128, 2, 4, DEPTH]), ALU.subtract)
        act(hlb[:], hlb[:], AF.Exp)
        red("dve", hmx[:], hlb[:], ALU.add)
        tk.op("dve", lambda e: e.reciprocal(out=hmx[:], in_=hmx[:]), [hmx], [hmx])
        tt("dve", hlb[:], hlb[:], hmx[:].unsqueeze(3).to_broadcast([128, 2, 4, DEPTH]), ALU.mult)
        mset("dve", LB[:, :, :, 0], 0.0)
        for l in range(1, DEPTH):
            if l == 1:
                cp("dve", LB[:, :, :, 1], hlb[:, :, :, 1])
            else:
                tt("dve", LB[:, :, :, l], LB[:, :, :, l - 1], hlb[:, :, :, l], ALU.add)
        ts("dve", OMLB[:], LB[:], -1.0, 1.0, ALU.mult, ALU.add)

        fr = sb([128, 2], F32, "fr")
        LN1E4 = float(np.log(10000.0))
        act(fr[:, 0:1], pidx, AF.Exp, 0.0, -LN1E4 / 256.0)
        half_b = sb([128, 1], F32, "half_b")
        mset("dve", half_b[:], -LN1E4 * 0.5)
        act(fr[:, 1:2], pidx, AF.Exp, half_b[:, 0:1], -LN1E4 / 256.0)
        s1 = sb([128, 2], F32, "s1")
        c1 = sb([128, 2], F32, "c1")
        hpi = sb([128, 1], F32, "hpi")
        mset("dve", hpi[:], float(np.pi / 2))
        act(s1[:], fr[:], AF.Sin)
        act(c1[:], fr[:], AF.Sin, hpi[:, 0:1], 1.0)
        mset("dve", E[:, 0:2, 0], 0.0)
        mset("dve", E[:, 2:4, 0], 1.0)
        tmpa = sb([128, 2], F32, "tmpa")
        tmpb = sb([128, 2], F32, "tmpb")
        for r in range(1, 64 if "D" in secs else 1):
            tt("dve", tmpa[:], E[:, 2:4, r - 1], s1[:], ALU.mult)
            tt("dve", tmpb[:], E[:, 0:2, r - 1], s1[:], ALU.mult)
            tt("dve", E[:, 0:2, r], E[:, 0:2, r - 1], c1[:], ALU.mult)
            tt("dve", E[:, 0:2, r], E[:, 0:2, r], tmpa[:], ALU.add)
            tt("dve", E[:, 2:4, r], E[:, 2:4, r - 1], c1[:], ALU.mult)
            tt("dve", E[:, 2:4, r], E[:, 2:4, r], tmpb[:], ALU.subtract)

        xin_t = [sb([128, D], F32, f"xin{i}") for i in range(2)]
        xo_t = [sb([128, 8, 128], F32, f"xo{i}") for i in range(2)]
        for rt in range(NT // 128 if "E" in secs else 0):
            a, o = xin_t[rt % 2], xo_t[rt % 2]
            import os
            ld(a[:], x_in[rt * 128:(rt + 1) * 128, :], eng=("sp" if (rt % 2 == 0 or "spld" in os.environ.get("EDBG", "")) else "act"))
            for half in range(2):
                pt = nps()
                for c4 in range(4):
                    c = half * 4 + c4
                    tr(pt[:, c4 * 128:(c4 + 1) * 128], a[:, c * 128:(c + 1) * 128], ident)
                if rt < 8:
                    cp(anyeng(("dve",) if "dvecp" in os.environ.get("EDBG", "") else ("dve", "act")), o[:, half * 4:half * 4 + 4, :], pt[:].rearrange("p (c t) -> p c t", c=4))
                else:
                    r0 = ((rt - 8) * 128) // 64
                    if half == 0:
                        tt("dve", o[:, 0:4, :].rearrange("p c (r q) -> p c r q", r=2),
                           pt[:].rearrange("p (c r q) -> p c r q", c=4, r=2),
                           E[:, :, r0:r0 + 2].unsqueeze(3).to_broadcast([128, 4, 2, 64]), ALU.add)
                    else:
                        tt("dve", o[:, 4:8, :].rearrange("p c (r q) -> p c r q", r=2),
                           pt[:].rearrange("p (c r q) -> p c r q", c=4, r=2),
                           E[:, :, :].unsqueeze(2).to_broadcast([128, 4, 2, 64]), ALU.add)
            import os
            EDBG = os.environ.get("EDBG", "")
            if "nost" in EDBG:
                pass
            elif "spst" in EDBG:
                st(xT[:, :, rt * 128:(rt + 1) * 128].rearrange("c p t -> p c t"), o[:], eng="sp")
            elif "perc" in EDBG:
                for c in range(8):
                    st(xT[c, :, rt * 128:(rt + 1) * 128], o[:, c, :], eng="sp")
            else:
                st(xT[:, :, rt * 128:(rt + 1) * 128].rearrange("c p t -> p c t"), o[:])
        tk.barrier()


        pes.close()
        cur_es[0] = es
        psT = PS[7]
        psTb = psT[:].bitcast(BF16)
        PSF = PS[0:7]

        def npsf():
            psi["i"] += 1
            return PSF[psi["i"] % 7]

        mbiT = sb([8, DEPTH], F32, "mbiT")
        nmbfT = sb([8, DEPTH], F32, "nmbfT")
        dtbT = sb([32, DEPTH], F32, "dtbT")
        negaT = sb([32, DEPTH], F32, "negaT")
        for l in range(nlayers):
            ld(mbiT[0:8, l:l + 1], m_bi[l, :].rearrange("(p o) -> p o", o=1))
            ld(nmbfT[0:8, l:l + 1], m_bf[l, :].rearrange("(p o) -> p o", o=1))
            ld(dtbT[0:32, l:l + 1], s_dt_bias[l, :].rearrange("(p o) -> p o", o=1))
            ld(negaT[0:32, l:l + 1], s_a_log[l, :].rearrange("(p o) -> p o", o=1))
        ts("dve", nmbfT[:], nmbfT[:], -1.0, None, ALU.mult)
        act(negaT[:], negaT[:], AF.Exp)
        ts("dve", negaT[:], negaT[:], -1.0, None, ALU.mult)
        cwT = sb([128, DEPTH, 3, 16], F32, "cwT")
        cbT = sb([128, DEPTH, 16], F32, "cbT")
        for l in range(nlayers):
            for kk in range(3):
                vec_to_fm(cwT[:, l, kk, :], s_conv_w[l, kk, :].rearrange("(c p) -> c p", p=128), 16)
            vec_to_fm(cbT[:, l, :], s_conv_b[l, :].rearrange("(c p) -> c p", p=128), 16)

        B_ = {}

        def alloc_P(full=True):
            B_["xt_b"] = [sb([128, 8, TT], F32, f"xt{i}") for i in range(2)]
            B_["sq_b"] = sb([128, 8, TT], BF16, "sq")
            B_["rstd"] = sb([128, TT], F32, "rstd")
            B_["hn_f"] = sb([128, 8, TT], F32, "hn_f")
            B_["hT"] = sb([128, 8, TT], BF16, "hT")
            B_["wbuf"] = [sb([128, 8, 512], BF16, f"wbuf{i}") for i in range(3)]
            if not full:
                return
            B_["stg_b"] = [sb([128, 512], BF16, f"stgb{i}") for i in range(4)]
            B_["stg_f"] = [sb([128, 512], F32, f"stgf{i}") for i in range(4)]
            B_["acc2"] = [sb([128, 512], F32, f"acc_f{i}") for i in range(2)]
            B_["xs_stage"] = sb([128, 4, 1024], BF16, "xs_stage")
            B_["bm_stage"] = sb([128, 4, 512], BF16, "bm_stage")
            B_["u_b2"] = [sb([128, 512], BF16, f"u_b{i}") for i in range(2)]

        wk = {"i": 0}
        sgi = {"b": 0, "f": 0}

        def nstg(kind):
            sgi[kind] += 1
            return (B_["stg_b"] if kind == "b" else B_["stg_f"])[sgi[kind] % 4]

        def norm_mod(l, ti, kind_gam, kind_sh, xt, hT_out=None):
            cidx = 0 if ti < 2 else 1
            hT_out = hT_out if hT_out is not None else B_["hT"]
            act(B_["sq_b"][:], xt[:], AF.Square)
            pt = npsf()
            for c in range(8):
                mm(pt[:], ones_b[:], B_["sq_b"][:, c, :], start=(c == 0), stop=(c == 7))
            act(B_["rstd"][:], pt[:], AF.Ln, EPS, 1.0 / D)
            act(B_["rstd"][:], B_["rstd"][:], AF.Exp, 0.0, -0.5)
            tt("dve", B_["hn_f"][:], xt[:], B_["rstd"][:].unsqueeze(1).to_broadcast([128, 8, TT]), ALU.mult)
            for c in range(8):
                act(hT_out[:, c, :], B_["hn_f"][:, c, :], AF.Identity, MOD[:, l, kind_sh, c, cidx:cidx + 1], MOD[:, l, kind_gam, c, cidx:cidx + 1])

        def load_w(name, l, r0, nrows, c0, ncols):
            wk["i"] += 1
            w = B_["wbuf"][wk["i"] % 3]
            src = wb[name][1]
            ld(w[:, 0:nrows // 128, 0:ncols], src[l, r0:r0 + nrows, c0:c0 + ncols].rearrange("(dc p) n -> p dc n", p=128), eng="pool")
            return w

        def fm_block(w, j0, m, pt_rows=128):
            pt = npsf()
            for dc in range(8):
                mm(pt[0:m, :], w[:, dc, j0:j0 + m], B_["hT"][:, dc, :], start=(dc == 0), stop=(dc == 7))
            return pt

        def phase_P(l):
            pst = ExitStack()
            cur_es[0] = pst
            alloc_P()
            hTs = [B_["hT"], sb([128, 8, TT], BF16, "hT2")]
            ld(B_["xt_b"][0][:], xT[:, :, 0:TT].rearrange("c p t -> p c t"))
            norm_mod(l, 0, 1, 0, B_["xt_b"][0], hTs[0])
            for ti in range(NTT):
                t0 = ti * TT
                rowlen = 256 if ti < 2 else 64
                nrow = TT // rowlen
                B_["hT"] = hTs[ti % 2]
                if ti + 1 < NTT:
                    ld(B_["xt_b"][(ti + 1) % 2][:], xT[:, :, t0 + TT:t0 + 2 * TT].rearrange("c p t -> p c t"))
                for (c0, nch, dst, func, scale) in ((C_MQ, 4, S["qTm"], AF.Copy, 1.0), (C_MK, 4, S["kTm"], AF.Copy, 128 ** -0.5),
                                                    (C_HQ, 4, S["hqT"], AF.Copy, 1.0),
                                                    (C_BG, 4, S["bgT"], AF.Sigmoid, 1.0), (C_BG + 512, 4, S["bgT"], AF.Sigmoid, 1.0),
                                                    (C_BG + 1024, 4, S["bgT"], AF.Sigmoid, 1.0), (C_BG + 1536, 4, S["bgT"], AF.Sigmoid, 1.0),
                                                    (C_BG + 2048, 4, S["bgT"], AF.Sigmoid, 1.0), (C_BG + 2560, 4, S["bgT"], AF.Sigmoid, 1.0)):
                    w = load_w("in", l, 0, D, c0, nch * 128)
                    rbase = (c0 - C_BG) if c0 >= C_BG else 0
                    for j in range(nch):
                        pt = fm_block(w, j * 128, 128)
                        sg = nstg("b")
                        act(sg[:], pt[:], func, 0.0, scale)
                        st(dst[rbase + j * 128:rbase + (j + 1) * 128, t0:t0 + TT], sg[:])
                for (c0, ncol, dst, func, scale) in ((C_MK, 512, S["k_tm"], AF.Copy, 128 ** -0.5), (C_MV, 512, S["v_tm"], AF.Copy, 1.0),
                                                     (C_MO, 512, S["go_tm"], AF.Sigmoid, 1.0), (C_HI, 512, S["hv_tm"], AF.Copy, 1.0),
                                                     (C_HG, 512, S["hg_tm"], AF.Silu, 1.0), (C_SZ, 512, S["sz_tm"], AF.Silu, 1.0),
                                                     (C_SZ + 512, 512, S["sz_tm"], AF.Silu, 1.0)):
                    w = load_w("in", l, 0, D, c0, 512)
                    cb0 = 512 if c0 == C_SZ + 512 else 0
                    for s4 in range(4):
                        pt = npsf()
                        for dc in range(8):
                            mm(pt[:], B_["hT"][:, dc, s4 * 128:(s4 + 1) * 128], w[:, dc, 0:512], start=(dc == 0), stop=(dc == 7))
                        sg = nstg("b")
                        act(sg[:], pt[:], func, 0.0, scale)
                        st(dst[t0 + s4 * 128:t0 + (s4 + 1) * 128, cb0:cb0 + 512], sg[:])
                if ti + 1 < NTT:
                    norm_mod(l, ti + 1, 1, 0, B_["xt_b"][(ti + 1) % 2], hTs[(ti + 1) % 2])
                w = load_w("in", l, 0, D, C_MI, 16)
                pt = fm_block(w, 0, 8)
                sg = nstg("f")
                ts("dve", sg[0:8, :], pt[0:8, :], mbiT[0:8, l:l + 1], None, ALU.add)
                st(S["gi"][:, t0:t0 + TT], sg[0:8, :])
                pt = fm_block(w, 8, 8)
                sg = nstg("f")
                act(sg[0:8, :], pt[0:8, :], AF.Exp, nmbfT[0:8, l:l + 1], -1.0)
                act(sg[0:8, :], sg[0:8, :], AF.Ln, 1.0, 1.0)
                ts("dve", sg[0:8, :], sg[0:8, :], -1.0, None, ALU.mult)
                st(S["glf"][:, t0:t0 + TT], sg[0:8, :])
                w = load_w("in", l, 0, D, C_SDT, 32)
                pt = fm_block(w, 0, 32)
                sg = nstg("f")
                act(sg[0:32, :], pt[0:32, :], AF.Exp, dtbT[0:32, l:l + 1], 1.0)
                act(sg[0:32, :], sg[0:32, :], AF.Ln, 1.0, 1.0)
                st(S["dtT"][:, t0:t0 + TT], sg[0:32, :])
                for half in range(2):
                    w = load_w("in", l, 0, D, C_HF + half * 512, 512)
                    for j in range(4):
                        pt = fm_block(w, j * 128, 128)
                        sg = nstg("f")
                        act(sg[:], pt[:], AF.Sigmoid)
                        ts("dve", sg[:], sg[:], OMLB[:, half, j, l:l + 1], LB[:, half, j, l:l + 1], ALU.mult, ALU.add)
                        sgk = nstg("b")
                        ts("pool", sgk[:], sg[:], -1.0, 1.0, ALU.mult, ALU.add)
                        st(S["kkT"][half * 512 + j * 128:half * 512 + (j + 1) * 128, t0:t0 + TT], sgk[:])
                        sgl = nstg("f")
                        act(sgl[:], sg[:], AF.Ln)
                        st(S["lfT"][half * 512 + j * 128:half * 512 + (j + 1) * 128, t0:t0 + TT], sgl[:])
                pend = [None]

                def make_tail(ch, ub):
                    def tail():
                        for s4 in range(4):
                            tr(psTb[:, s4 * 128:(s4 + 1) * 128], ub[:, s4 * 128:(s4 + 1) * 128], ident_b[:])
                        if ch < 8:
                            cp("dve", B_["xs_stage"][:, :, ch * 128:(ch + 1) * 128], psTb[:, 0:512].rearrange("p (s f) -> p s f", s=4))
                        else:
                            cp("dve", B_["bm_stage"][:, :, (ch - 8) * 128:(ch - 7) * 128], psTb[:, 0:512].rearrange("p (s f) -> p s f", s=4))
                    return tail
                for q in range(4):
                    w = load_w("in", l, 0, D, C_SX + q * 512, 512)
                    for j in range(4):
                        ch = q * 4 + j
                        pt = fm_block(w, j * 128, 128)
                        if pend[0] is not None:
                            pend[0]()
                            pend[0] = None
                        acc = B_["acc2"][ch % 2]
                        p3 = pt[:].rearrange("p (r q) -> p r q", q=rowlen)
                        a3 = acc[:].rearrange("p (r q) -> p r q", q=rowlen)
                        ts("dve", acc[:], pt[:], cwT[:, l, 1, ch:ch + 1], None, ALU.mult)
                        stt("dve", a3[:, :, 1:rowlen], p3[:, :, 0:rowlen - 1], cwT[:, l, 0, ch:ch + 1], a3[:, :, 1:rowlen], ALU.mult, ALU.add)
                        stt("dve", a3[:, :, 0:rowlen - 1], p3[:, :, 1:rowlen], cwT[:, l, 2, ch:ch + 1], a3[:, :, 0:rowlen - 1], ALU.mult, ALU.add)
                        if ch < 12:
                            ub = B_["u_b2"][ch % 2]
                            act(ub[:], acc[:], AF.Silu, cbT[:, l, ch:ch + 1], 1.0)
                            if ch >= 8:
                                st(S["bmT"][(ch - 8) * 128:(ch - 7) * 128, t0:t0 + TT], ub[:])
                            pend[0] = make_tail(ch, ub)
                        else:
                            sg = nstg("b")
                            act(sg[:], acc[:], AF.Silu, cbT[:, l, ch:ch + 1], 1.0)
                            st(S["cmT"][(ch - 12) * 128:(ch - 11) * 128, t0:t0 + TT], sg[:])
                if pend[0] is not None:
                    pend[0]()
                    pend[0] = None
                st(S["xs_tm"][t0:t0 + TT, :].rearrange("(s p) f -> p s f", p=128), B_["xs_stage"][:])
                st(S["bm_tm"][t0:t0 + TT, :].rearrange("(s p) f -> p s f", p=128), B_["bm_stage"][:])
            tk.barrier()
            pst.close()
            cur_es[0] = es


        def row_bcast(dst, src_row, n):
            r1 = B_["r1"]
            for c0 in range(0, n, 256):
                cc = min(256, n - c0)
                ld(r1[0:1, 0:cc], src_row[:, c0:c0 + cc])
                pt = npsf()
                mm(pt[:, 0:cc], ones_f[0:1, :], r1[0:1, 0:cc])
                cp("dve", dst[:, c0:c0 + cc], pt[:, 0:cc])

        def rms_rows(eng, out_rst, x3, nh, dsz, tmp):
            if eng == "act":
                act(tmp, x3, AF.Square)
            else:
                tt(eng, tmp, x3, x3, ALU.mult)
            red("dve", out_rst, tmp, ALU.add)
            act(out_rst, out_rst, AF.Ln, EPS, 1.0 / dsz)
            act(out_rst, out_rst, AF.Exp, 0.0, -0.5)

        def run_gens(gens):
            gens = list(gens)
            while gens:
                for g_ in list(gens):
                    try:
                        next(g_)
                    except StopIteration:
                        gens.remove(g_)

        def phase_S(l):
            pst = ExitStack()
            cur_es[0] = pst
            LM = 4096
            GCH = 512
            B_["r1"] = sb([1, 256], F32, "r1")
            GA = sb([36, LM], F32, "GA")
            GB = sb([36, LM], F32, "GB")
            GC = sb([36, GCH], F32, "GC")
            gtot = sb([36, 1], F32, "gtot")
            gcar = sb([36, 1], F32, "gcar")
            cmx = sb([36, 32], F32, "cmx")
            Gt = sb([36, 32], F32, "Gt")
            Gp = sb([36, 32], F32, "Gp")
            fs = sb([36, 32], F32, "fs")
            fsm = sb([36, 32], F32, "fsm")
            m0t = sb([36, 1], F32, "m0t")
            mfin = sb([36, 1], F32, "mfin")
            fsb = sb([128, 8, 32], F32, "fsb")
            gsc = sb([128, 16], F32, "gsc")
            mnb = sb([128, 512], F32, "mnb")
            hnb = sb([128, 512], F32, "hnb")
            snb = sb([128, 1024], F32, "snb")
            sdb = sb([128, 16], F32, "sdb")
            row_bcast(mnb, m_norm[l:l + 1, :], 512)
            row_bcast(hnb, h_norm[l:l + 1, :], 512)
            row_bcast(snb, s_norm[l:l + 1, :], 1024)
            row_bcast(sdb, s_d[l:l + 1, :], 16)
            nega2 = sb([16, 2], F32, "nega2")
            for dr in range(2):
                ld(nega2[0:16, dr:dr + 1], s_a_log[l, dr * 16:(dr + 1) * 16].rearrange("(p o) -> p o", o=1))
            act(nega2[:], nega2[:], AF.Exp)
            ts("dve", nega2[:], nega2[:], -1.0, None, ALU.mult)
            PM0, PM1 = PS[0], PS[1]
            PH0, PH1, PH2 = PS[2], PS[3], PS[4]
            PSS = [PS[5], PS[6]]
            ssi = {"i": 0}

            def nb_s():
                ssi["i"] += 1
                return PSS[ssi["i"] % 2]

            qT_t = [sb([128, 4, 128], BF16, f"qT_t{i}") for i in range(2)]
            kT_t = [sb([128, 4, 128], BF16, f"kT_t{i}") for i in range(2)]
            ktm_t = [sb([128, 512], BF16, f"ktm_t{i}") for i in range(2)]
            vtm_t = [sb([128, 512], BF16, f"vtm_t{i}") for i in range(2)]
            gate_m = sb([128, 512], BF16, "gate_m")
            vaug = sb([128, 4, 129], BF16, "vaug")
            PTm = sb([128, 4, 128], BF16, "PTm")
            C32 = [[sb([128, 129], F32, f"C32_{dr}{h}") for h in range(4)] for dr in range(2)]
            Cb = [[sb([128, 129], BF16, f"Cb_{dr}{h}") for h in range(4)] for dr in range(2)]
            dn = sb([128, 4], F32, "dn")
            yblm = sb([128, 512], F32, "yblm")
            ycm = sb([128, 512], F32, "ycm")
            tmpm = yblm
            ybfm = sb([128, 512], BF16, "ybfm")
            yTm = sb([128, 4, 128], BF16, "yTm")
            rstm = sb([128, 4], F32, "rstm")
            hq_t = [sb([128, 4, 128], BF16, f"hq_t{i}") for i in range(2)]
            kk_t = [sb([128, 4, 128], BF16, f"kk_t{i}") for i in range(2)]
            lf_t = [sb([128, 4, 128], F32, f"lf_t{i}") for i in range(2)]
            hvc = [[sb([32, 512], BF16, f"hvc{i}_{c}") for c in range(4)] for i in range(2)]
            gate_h = sb([128, 512], BF16, "gate_h")
            cp_t = sb([128, 4, 128], F32, "cp_t")
            dd_t = sb([128, 4, 128], F32, "dd_t")
            ep_t = sb([128, 4, 128], F32, "ep_t")
            em_t = sb([128, 4, 128], F32, "em_t")
            qt_t = sb([128, 4, 128], BF16, "qt_t")
            kt_t = sb([128, 4, 128], BF16, "kt_t")
            kt2 = sb([128, 4, 128], BF16, "kt2")
            qt2p = [sb([128, 4, 128], BF16, f"qt2p{c}") for c in range(4)]
            refg = sb([128, 4, 4], F32, "refg")
            egm = sb([128, 4, 4], F32, "egm")
            egd = sb([128, 4, 4], F32, "egd")
            egt = sb([128, 4, 4], F32, "egt")
            PThp = sb([32, 16, 128], BF16, "PThp")
            kttm = sb([32, 16, 128], BF16, "kttm")
            SA = [sb([128, 4, 128], F32, f"SA{dr}") for dr in range(2)]
            SB = sb([128, 4, 128], F32, "SBh")
            Sbb = [sb([128, 4, 128], BF16, f"Sbb{c}") for c in range(2)]
            yblh = sb([128, 512], F32, "yblh")
            ych = sb([128, 512], F32, "ych")
            tmph = yblh
            ybfh = sb([128, 512], BF16, "ybfh")
            yTh = sb([128, 4, 128], BF16, "yTh")
            rsth = sb([128, 4], F32, "rsth")
            mset("pool", PThp[:], 0.0)
            zb = sb([1, 512], BF16, "zb")
            mset("pool", zb[:], 0.0)
            for c in range(4):
                mset("pool", qt2p[c][:], 0.0)
            dta = [sb([16, 128], F32, f"dta{i}") for i in range(2)]
            cm_t = [sb([128, 4, 128], BF16, f"cm_t{i}") for i in range(2)]
            bmT_t = [sb([128, 4, 128], BF16, f"bmT_t{i}") for i in range(2)]
            bmtm_t = [sb([128, 512], BF16, f"bmtm_t{i}") for i in range(2)]
            xs_t = [sb([128, 1024], BF16, f"xs_t{i}") for i in range(2)]
            gate_s = sb([128, 1024], BF16, "gate_s")
            da = sb([16, 128], F32, "da")
            lcs = sb([16, 128], F32, "lcs")
            fm4 = sb([16, 4, 128], F32, "fm4")
            gs2 = [sb([128, 4, 16], F32, f"gs{i}") for i in range(2)]
            ngla = sb([128, 16], F32, "ngla")
            labd = sb([16, 16, 128], F32, "labd")
            elae = sb([16, 1], F32, "elae")
            ebd = sb([16, 16], F32, "ebd")
            eb2 = [sb([128, 16], F32, f"eb{i}") for i in range(2)]
            cb_all = sb([128, 4, 128], F32, "cb_all")
            ddm = [sb([128, 4, 128], F32, f"ddm{i}") for i in range(2)]
            PTs2 = [[sb([128, 4, 128], BF16, f"PTs{p}_{i}") for i in range(4)] for p in range(2)]
            xp2 = [sb([128, 16, 64], BF16, f"xp{i}") for i in range(2)]
            xpp2 = [sb([128, 1024], BF16, f"xpp{i}") for i in range(2)]
            pis = [sb([128, 256], F32, f"pis{i}") for i in range(2)]
            hT32 = [[sb([128, 4, 64], F32, f"hT32_{dr}{g}") for g in range(4)] for dr in range(2)]
            hTb = [[sb([128, 256], BF16, f"hTb_{dr}{g}") for g in range(4)] for dr in range(2)]
            tmps = sb([128, 4, 64], F32, "tmps")
            sst = sb([64, 128], F32, "sst")
            ybls = sb([128, 1024], F32, "ybls")
            ycs = sb([128, 1024], F32, "ycs")
            tmpS = ybls
            ybfs = sb([128, 1024], BF16, "ybfs")
            yTs = sb([128, 8, 128], BF16, "yTs")
            rsts = sb([128, 1], F32, "rsts")

            NC, CS = 4, 32

            def loads_m(dr, tb, par):
                ld(qT_t[par][:], S["qTm"][:, tb:tb + 128].rearrange("(h d) t -> d h t", d=128))
                ld(kT_t[par][:], S["kTm"][:, tb:tb + 128].rearrange("(h d) t -> d h t", d=128))
                ld(ktm_t[par][:], S["k_tm"][tb:tb + 128, :])
                ld(vtm_t[par][:], S["v_tm"][tb:tb + 128, :])

            def loads_h(dr, tb, par):
                ld(lf_t[par][:], S["lfT"][dr * 512:(dr + 1) * 512, tb:tb + 128].rearrange("(h d) t -> d h t", d=128))
                ld(kk_t[par][:], S["kkT"][dr * 512:(dr + 1) * 512, tb:tb + 128].rearrange("(h d) t -> d h t", d=128))
                ld(hq_t[par][:], S["hqT"][:, tb:tb + 128].rearrange("(h d) t -> d h t", d=128))
                for c2 in range(NC):
                    ld(hvc[par][c2][:], S["hv_tm"][tb + c2 * CS:tb + (c2 + 1) * CS, :])

            def loads_s(dr, tb, par):
                ld(dta[par][:], S["dtT"][dr * 16:(dr + 1) * 16, tb:tb + 128])
                ld(cm_t[par][:], S["cmT"][:, tb:tb + 128].rearrange("(g n) t -> n g t", n=128))
                ld(bmT_t[par][:], S["bmT"][:, tb:tb + 128].rearrange("(g n) t -> n g t", n=128))
                ld(bmtm_t[par][:], S["bm_tm"][tb:tb + 128, :])
                ld(xs_t[par][:], S["xs_tm"][tb:tb + 128, :])

            for sq, (s0, L, rowlen, cidx) in enumerate(SEQS):
                nb = L // 128
                is_p = cidx == 0
                for dr in range(2):
                    p0 = 32 * dr
                    P = slice(p0, p0 + 4)
                    ld(GA[P, 0:L], S["gi"][dr * 4:dr * 4 + 4, s0:s0 + L])
                    ld(GB[P, 0:L], S["glf"][dr * 4:dr * 4 + 4, s0:s0 + L])
                    if is_p:
                        mset("dve", m0t[P, :], 0.0)
                    else:
                        ld(m0t[P, 0:1], st_m[l, dr, :].rearrange("(p o) -> p o", o=1))
                    if dr == 1:
                        red("dve", gtot[P, :], GB[P, 0:L], ALU.add)
                    for c0 in range(0, L, GCH):
                        cc = min(GCH, L - c0)
                        init = 0.0 if c0 == 0 else gcar[P, 0:1]
                        scan(GC[P, 0:cc], ones_f[P, 0:1].to_broadcast([4, cc]), GB[P, c0:c0 + cc], init, ALU.mult, ALU.add)
                        if c0 + cc < L:
                            cp("dve", gcar[P, :], GC[P, cc - 1:cc])
                        if dr == 0:
                            cp("dve", GB[P, c0:c0 + cc], GC[P, 0:cc])
                        else:
                            ts("dve", GB[P, c0:c0 + cc], GB[P, c0:c0 + cc], gtot[P, 0:1], None, ALU.add)
                            tt("dve", GB[P, c0:c0 + cc], GB[P, c0:c0 + cc], GC[P, 0:cc], ALU.subtract)
                    tt("dve", GA[P, 0:L], GA[P, 0:L], GB[P, 0:L], ALU.subtract)
                    red("dve", cmx[P, 0:nb], GA[P, 0:L].rearrange("p (c t) -> p c t", t=128), ALU.max)
                    if dr == 0:
                        scan(Gt[P, 0:nb], zeros_f[P, 0:nb], cmx[P, 0:nb], m0t[P, 0:1], ALU.add, ALU.max)
                        cp("dve", Gp[P, 0:1], m0t[P, 0:1])
                        if nb > 1:
                            cp("dve", Gp[P, 1:nb], Gt[P, 0:nb - 1])
                    else:
                        for c in range(nb - 1, -1, -1):
                            prev = m0t[P, 0:1] if c == nb - 1 else Gt[P, c + 1:c + 2]
                            tt("dve", Gt[P, c:c + 1], prev, cmx[P, c:c + 1], ALU.max)
                        cp("dve", Gp[P, nb - 1:nb], m0t[P, 0:1])
                        if nb > 1:
                            cp("dve", Gp[P, 0:nb - 1], Gt[P, 1:nb])
                    tt("dve", fs[P, 0:nb], Gp[P, 0:nb], Gt[P, 0:nb], ALU.subtract)
                    act(fs[P, 0:nb], fs[P, 0:nb], AF.Exp)
                    if dr == 0:
                        tt("dve", mfin[P, :], GB[P, L - 1:L], Gt[P, nb - 1:nb], ALU.add)
                    else:
                        tt("dve", mfin[P, :], GB[P, 0:1], Gt[P, 0:1], ALU.add)
                    if is_p:
                        st(o_m[sq, l, dr, :].rearrange("(p o) -> p o", o=1), mfin[P, :])
                    gb3 = Gt[P, 0:nb].unsqueeze(2).to_broadcast([4, nb, 128])
                    tt("dve", GA[P, 0:L].rearrange("p (c t) -> p c t", t=128), GA[P, 0:L].rearrange("p (c t) -> p c t", t=128), gb3, ALU.subtract)
                    act(GA[P, 0:L], GA[P, 0:L], AF.Exp)
                    tt("dve", GB[P, 0:L].rearrange("p (c t) -> p c t", t=128), GB[P, 0:L].rearrange("p (c t) -> p c t", t=128), gb3, ALU.add)
                    act(GB[P, 0:L], GB[P, 0:L], AF.Exp, 0.0, -1.0)
                    for h in range(4):
                        ts("dve", fsm[P, 0:nb], fs[P, 0:nb], ident[P, p0 + h:p0 + h + 1], None, ALU.mult)
                        pt = PSS[h % 2]
                        mm(pt[:, 0:nb], ones_f[P, :], fsm[P, 0:nb])
                        cp("dve", fsb[:, dr * 4 + h, 0:nb], pt[:, 0:nb])
                for dr in range(2):
                    for h in range(4):
                        if is_p:
                            mset("pool", C32[dr][h][:], 0.0)
                        else:
                            ld(C32[dr][h][:, 0:128], st_c[l, dr, h, :, :])
                            ld(C32[dr][h][:, 128:129], st_n[l, dr, h, :].rearrange("(p o) -> p o", o=1))
                        b0 = 0 if dr == 0 else nb - 1
                        ts("dve", C32[dr][h][:], C32[dr][h][:], fsb[:, dr * 4 + h, b0:b0 + 1], None, ALU.mult)
                        cp("pool", Cb[dr][h][:], C32[dr][h][:])
                    if is_p:
                        mset("pool", SA[dr][:], 0.0)
                    else:
                        for h in range(4):
                            ld(SA[dr][:, h, :], st_h[l, dr, h, :, :])
                    for g in range(4):
                        if is_p:
                            mset("pool", hT32[dr][g][:], 0.0)
                        else:
                            for hh in range(4):
                                ld(sst[:], st_s[l, dr, g * 4 + hh, :, :])
                                pt = PSS[hh % 2]
                                tr(pt[:, 0:64], sst[:], ident[0:64, 0:64])
                                cp("dve", hT32[dr][g][:, hh, :], pt[:, 0:64])
                        cp("pool", hTb[dr][g][:].rearrange("p (h e) -> p h e", h=4), hT32[dr][g][:])
                iters = [(1, b) for b in range(nb - 1, -1, -1)] + [(0, b) for b in range(nb)]

                def gen_m(dr, b, par):
                    p0 = 32 * dr
                    P = slice(p0, p0 + 4)
                    mask = maskf if dr == 0 else maskb
                    tb = s0 + b * 128
                    last = (b == nb - 1) if dr == 0 else (b == 0)
                    bnext = b + 1 if dr == 0 else b - 1
                    ykey = f"yb_{sq}_{b}"
                    qT, kT, ktm, vtm = qT_t[par], kT_t[par], ktm_t[par], vtm_t[par]
                    if dr == 0:
                        tk.dma(yblm[:], S["yb_all"][tb:tb + 128, 0:512], track="L_yblm", outs=[yblm], ins=[ykey + "m"])
                        ld(gate_m[:], S["go_tm"][tb:tb + 128, :])
                    tr(PM1[:, 0:4], GA[P, b * 128:(b + 1) * 128], ident[P, p0:p0 + 4])
                    tr(PM1[:, 4:8], GB[P, b * 128:(b + 1) * 128], ident[P, p0:p0 + 4])
                    cp("dve", gsc[:, 0:8], PM1[:, 0:8])
                    yield
                    tt("pool", vaug[:, :, 0:128], vtm[:].rearrange("p (h e) -> p h e", h=4), gsc[:, 0:4].unsqueeze(2).to_broadcast([128, 4, 128]), ALU.mult)
                    cp("act", vaug[:, :, 128], gsc[:, 0:4])
                    for h in range(4):
                        mm(PM0[:, h * 128:(h + 1) * 128], kT[:, h, :], qT[:, h, :])
                    yield
                    tt("dve", PTm[:], PM0[:].rearrange("p (h t) -> p h t", h=4), mask[:].unsqueeze(1).to_broadcast([128, 4, 128]), ALU.mult)
                    yield

                    def pov(h, n):
                        return (PM1 if h < 2 else PM0)[:, (h % 2) * 256:(h % 2) * 256 + n]
                    for h in range(4):
                        mm(pov(h, 129), PTm[:, h, :], vaug[:, h, :], start=True, stop=False)
                        mm(pov(h, 129), qT[:, h, :], Cb[dr][h][:], start=False, stop=True)
                    yield
                    for h in range(4):
                        act(dn[:, h:h + 1], pov(h, 129)[:, 128:129], AF.Abs)
                    yield
                    tt("dve", dn[:], dn[:], gsc[:, 4:8], ALU.max)
                    tk.op("dve", lambda e: e.reciprocal(out=dn[:], in_=dn[:]), [dn], [dn])
                    yield
                    for h in range(4):
                        if dr == 1:
                            act(ycm[:, h * 128:(h + 1) * 128], pov(h, 128), AF.Copy, 0.0, dn[:, h:h + 1])
                        else:
                            stt("dve", ycm[:, h * 128:(h + 1) * 128], pov(h, 128), dn[:, h:h + 1], yblm[:, h * 128:(h + 1) * 128], ALU.mult, ALU.add)
                    yield
                    for h in range(4):
                        pd = (PM1 if h < 2 else PM0)[:, (h % 2) * 256:(h % 2) * 256 + 129]
                        mm(pd, ktm[:, h * 128:(h + 1) * 128], vaug[:, h, :])
                    yield
                    for h in range(4):
                        pd = (PM1 if h < 2 else PM0)[:, (h % 2) * 256:(h % 2) * 256 + 129]
                        tt("dve", C32[dr][h][:], C32[dr][h][:], pd, ALU.add)
                        if not last:
                            act(C32[dr][h][:], C32[dr][h][:], AF.Copy, 0.0, fsb[:, dr * 4 + h, bnext:bnext + 1])
                            cp("act", Cb[dr][h][:], C32[dr][h][:])
                        elif is_p:
                            st(o_c[sq, l, dr, h, :, :], C32[dr][h][:, 0:128])
                            st(o_n[sq, l, dr, h, :].rearrange("(p o) -> p o", o=1), C32[dr][h][:, 128:129])
                    yield
                    if dr == 1:
                        tk.dma(S["yb_all"][tb:tb + 128, 0:512], ycm[:], track="S_ycm", ins=[ycm], outs=[ykey + "m"])
                    else:
                        h3 = ycm[:].rearrange("p (h e) -> p h e", h=4)
                        rms_rows("act", rstm[:, 0:4], h3, 4, 128, tmpm[:].rearrange("p (h e) -> p h e", h=4))
                        yield
                        for h in range(4):
                            stt("dve", ycm[:, h * 128:(h + 1) * 128], ycm[:, h * 128:(h + 1) * 128], rstm[:, h:h + 1], mnb[:, h * 128:(h + 1) * 128], ALU.mult, ALU.mult)
                        yield
                        tt("dve", ybfm[:], ycm[:], gate_m[:], ALU.mult)
                        yield
                        for h in range(4):
                            tr(psTb[:, h * 128:(h + 1) * 128], ybfm[:, h * 128:(h + 1) * 128], ident_b[:])
                        cp("act", yTm[:], psTb[:, 0:512].rearrange("p (h t) -> p h t", h=4))
                        st(S["ymT"][:, tb:tb + 128].rearrange("(h f) t -> f h t", f=128), yTm[:])

                def front_h(dr, b, par):
                    hq, kk, lf = hq_t[par], kk_t[par], lf_t[par]
                    for h in range(4):
                        scan(cp_t[:, h, :], ones_f[:, 0:128], lf[:, h, :], 0.0, ALU.mult, ALU.add)
                    c4 = cp_t[:].rearrange("p h (c t) -> p h c t", c=NC)
                    if dr == 0:
                        src4 = c4
                        cp("dve", refg[:], c4[:, :, :, 15])
                        cp("dve", egm[:, :, 0], c4[:, :, 0, 15])
                        tt("dve", egm[:, :, 1:NC], c4[:, :, 1:NC, 15], c4[:, :, 0:NC - 1, CS - 1], ALU.subtract)
                        tt("dve", egd[:], c4[:, :, :, CS - 1], c4[:, :, :, 15], ALU.subtract)
                        sgn = 1.0
                    else:
                        tt("dve", dd_t[:], cp_t[:], lf[:], ALU.subtract)
                        e4 = dd_t[:].rearrange("p h (c t) -> p h c t", c=NC)
                        src4 = e4
                        cp("dve", refg[:], e4[:, :, :, 16])
                        tt("dve", egm[:], c4[:, :, :, CS - 1], e4[:, :, :, 16], ALU.subtract)
                        tt("dve", egd[:], e4[:, :, :, 16], e4[:, :, :, 0], ALU.subtract)
                        sgn = -1.0
                    tt("dve", ep_t[:].rearrange("p h (c t) -> p h c t", c=NC), src4, refg[:].unsqueeze(3).to_broadcast([128, 4, NC, CS]), ALU.subtract)
                    ts("dve", ep_t[:], ep_t[:], -43.0, 43.0, ALU.max, ALU.min)
                    yield
                    act(egm[:], egm[:], AF.Exp)
                    act(egd[:], egd[:], AF.Exp)
                    act(em_t[:], ep_t[:], AF.Exp, 0.0, -sgn)
                    act(ep_t[:], ep_t[:], AF.Exp, 0.0, sgn)
                    yield
                    em2, ep2 = dd_t, cp_t
                    tt("dve", em2[:].rearrange("p h (c t) -> p h c t", c=NC), em_t[:].rearrange("p h (c t) -> p h c t", c=NC),
                       egd[:].unsqueeze(3).to_broadcast([128, 4, NC, CS]), ALU.mult)
                    tt("pool", kt_t[:], kk[:], em_t[:], ALU.mult)
                    tt("pool", qt_t[:], hq[:], ep_t[:], ALU.mult)
                    yield
                    tt("dve", kt2[:], kk[:], em2[:], ALU.mult)
                    tt("dve", ep2[:].rearrange("p h (c t) -> p h c t", c=NC), ep_t[:].rearrange("p h (c t) -> p h c t", c=NC),
                       egm[:].unsqueeze(3).to_broadcast([128, 4, NC, CS]), ALU.mult)

                def back_h(dr, b, par):
                    mask = maskf if dr == 0 else maskb
                    tb = s0 + b * 128
                    last = (b == nb - 1) if dr == 0 else (b == 0)
                    ykey = f"yb_{sq}_{b}"
                    hq, hv = hq_t[par], hvc[par]
                    ep2 = cp_t
                    order = list(range(NC)) if dr == 0 else list(range(NC - 1, -1, -1))
                    if dr == 0:
                        tk.dma(yblh[:], S["yb_all"][tb:tb + 128, 512:1024], track="L_yblh", outs=[yblh], ins=[ykey + "h"])
                        ld(gate_h[:], S["hg_tm"][tb:tb + 128, :])
                    tt("dve", egt[:], egm[:], egd[:], ALU.mult)
                    for c2 in range(NC):
                        tt("pool", qt2p[c2][:, :, c2 * CS:(c2 + 1) * CS], hq[:, :, c2 * CS:(c2 + 1) * CS], ep2[:, :, c2 * CS:(c2 + 1) * CS], ALU.mult)
                    for half in range(2):
                        for cc in range(2):
                            c2 = half * 2 + cc
                            for h in range(4):
                                j = c2 * 4 + h
                                mm(PH0[0:CS, j * CS:(j + 1) * CS], kt_t[:, h, c2 * CS:(c2 + 1) * CS], qt_t[:, h, c2 * CS:(c2 + 1) * CS])
                                jj = cc * 4 + h
                                tr(psTb[0:CS, jj * 128:(jj + 1) * 128], kt2[:, h, c2 * CS:(c2 + 1) * CS], ident_b[:])
                        cp("act", kttm[:, half * 8:(half + 1) * 8, :], psTb[0:CS, :].rearrange("p (j k) -> p j k", j=8))
                    yield
                    PT4 = PThp[:].rearrange("p (c h) t -> p c h t", c=NC)
                    pS4 = PH0[0:CS, :].rearrange("p (c h t) -> p c h t", c=NC, h=4)
                    for c2 in range(NC):
                        tt("dve", PT4[:, c2, :, c2 * CS:(c2 + 1) * CS], pS4[:, c2, :, :], mask[0:CS, 0:CS].unsqueeze(1).to_broadcast([CS, 4, CS]), ALU.mult)
                    PD = [PH1, PH2]

                    def emit_pd(i):
                        c2 = order[i]
                        for h in range(4):
                            mm(PD[i % 2][:, h * 128:(h + 1) * 128], kttm[:, c2 * 4 + h, :], hv[c2][:, h * 128:(h + 1) * 128])
                    emit_pd(0)
                    emit_pd(1)
                    yield
                    Scur, Soth = SA[dr], SB
                    mm(PH0[:, :], zb[0:1, 0:128], zb[0:1, 0:512], start=True, stop=False)
                    for i, c2 in enumerate(order):
                        cp("act", Sbb[i % 2][:], Scur[:])
                        for h in range(4):
                            j = c2 * 4 + h
                            mm(PH0[:, h * 128:(h + 1) * 128], PThp[:, j, :], hv[c2][:, h * 128:(h + 1) * 128], start=False, stop=False)
                            mm(PH0[:, h * 128:(h + 1) * 128], qt2p[c2][:, h, :], Sbb[i % 2][:, h, :], start=False, stop=(i == NC - 1))
                        tt("dve", Soth[:], Scur[:], egt[:, :, c2].unsqueeze(2).to_broadcast([128, 4, 128]), ALU.mult)
                        tt("dve", Soth[:], Soth[:], PD[i % 2][:].rearrange("p (h e) -> p h e", h=4), ALU.add)
                        Scur, Soth = Soth, Scur
                        if i + 2 < NC:
                            emit_pd(i + 2)
                        yield
                    assert Scur is SA[dr]
                    if last and is_p:
                        for h in range(4):
                            st(o_h[sq, l, dr, h, :, :], SA[dr][:, h, :])
                    if dr == 1:
                        cp("act", ych[:], PH0[:])
                        tk.dma(S["yb_all"][tb:tb + 128, 512:1024], ych[:], track="S_ych", ins=[ych], outs=[ykey + "h"])
                    else:
                        tt("dve", ych[:], PH0[:], yblh[:], ALU.add)
                        h3 = ych[:].rearrange("p (h e) -> p h e", h=4)
                        rms_rows("act", rsth[:, 0:4], h3, 4, 128, tmph[:].rearrange("p (h e) -> p h e", h=4))
                        yield
                        for h in range(4):
                            stt("dve", ych[:, h * 128:(h + 1) * 128], ych[:, h * 128:(h + 1) * 128], rsth[:, h:h + 1], hnb[:, h * 128:(h + 1) * 128], ALU.mult, ALU.mult)
                        yield
                        tt("dve", ybfh[:], ych[:], gate_h[:], ALU.mult)
                        yield
                        for h in range(4):
                            tr(psTb[:, h * 128:(h + 1) * 128], ybfh[:, h * 128:(h + 1) * 128], ident_b[:])
                        cp("act", yTh[:], psTb[:, 0:512].rearrange("p (h t) -> p h t", h=4))
                        st(S["yhT"][:, tb:tb + 128].rearrange("(h f) t -> f h t", f=128), yTh[:])

                def front_s(dr, b, par):
                    negm = negf if dr == 0 else negb
                    dt_, cm, bmT, xs = dta[par], cm_t[par], bmT_t[par], xs_t[par]
                    gs, eb, xp, xpp, PTs = gs2[par], eb2[par], xp2[par], xpp2[par], PTs2[par]
                    ts("dve", da[:], dt_[:], nega2[:, dr:dr + 1], None, ALU.mult)
                    scan(lcs[:], ones_f[0:16, 0:128], da[:], 0.0, ALU.mult, ALU.add)
                    if dr == 0:
                        cp("dve", fm4[:, 0, :], lcs[:])
                        lae = fm4[:, 0, 127:128]
                    else:
                        stt("dve", fm4[:, 0, :], da[:], lcs[:, 127:128], lcs[:], ALU.add, ALU.subtract)
                        lae = fm4[:, 0, 0:1]
                    cp("dve", fm4[:, 1, :], dt_[:])
                    ts("dve", fm4[:, 3, :], fm4[:, 0, :], lae, -1.0, ALU.subtract, ALU.mult)
                    yield
                    act(fm4[:, 2, :], fm4[:, 0, :], AF.Exp)
                    act(elae[:], lae, AF.Exp)
                    act(fm4[:, 3, :], fm4[:, 3, :], AF.Exp)
                    yield
                    tt("dve", fm4[:, 3, :], fm4[:, 3, :], dt_[:], ALU.mult)
                    ts("dve", ebd[:], ident[0:16, 0:16], elae[:, 0:1], None, ALU.mult)
                    tt("dve", labd[:], fm4[:, 0, :].unsqueeze(1).to_broadcast([16, 16, 128]), ident[0:16, 0:16].unsqueeze(2).to_broadcast([16, 16, 128]), ALU.mult)
                    yield
                    pt = nb_s()
                    for q in range(4):
                        tr(pt[:, q * 16:(q + 1) * 16], fm4[:, q, :], ident[0:16, 0:16])
                    mm(pt[:, 64:80], ones_f[0:16, :], ebd[:])
                    cp("dve", gs[:], pt[:, 0:64].rearrange("p (q h) -> p q h", q=4))
                    cp("dve", eb[:], pt[:, 64:80])
                    ts("dve", ngla[:], gs[:, 0, :], -1.0, None, ALU.mult)
                    pcb = nb_s()
                    for g in range(4):
                        mm(pcb[:, g * 128:(g + 1) * 128], bmT[:, g, :], cm[:, g, :])
                    cp("act", cb_all[:], pcb[:].rearrange("p (g t) -> p g t", g=4))
                    yield
                    tt("pool", xp[:], xs[:].rearrange("p (h e) -> p h e", h=16), gs[:, 1, :].unsqueeze(2).to_broadcast([128, 16, 64]), ALU.mult)
                    tt("pool", xpp[:].rearrange("p (h e) -> p h e", h=16), xs[:].rearrange("p (h e) -> p h e", h=16), gs[:, 3, :].unsqueeze(2).to_broadcast([128, 16, 64]), ALU.mult)
                    for g in range(4):
                        pb = nb_s()
                        mm(pb[:], ones_f[0:16, :], labd[:, g * 4:(g + 1) * 4, :])
                        dg = ddm[g % 2]
                        tt("dve", dg[:], pb[:].rearrange("p (h t) -> p h t", h=4), negm.unsqueeze(1).to_broadcast([128, 4, 128]), ALU.add)
                        yield
                        for hh in range(4):
                            act(dg[:, hh, :], dg[:, hh, :], AF.Exp, ngla[:, g * 4 + hh:g * 4 + hh + 1], 1.0)
                        yield
                        tt("dve", PTs[g][:], dg[:], cb_all[:, g, :].unsqueeze(1).to_broadcast([128, 4, 128]), ALU.mult)

                def back_s(dr, b, par):
                    tb = s0 + b * 128
                    last = (b == nb - 1) if dr == 0 else (b == 0)
                    ykey = f"yb_{sq}_{b}"
                    cm, bmtm, xs = cm_t[par], bmtm_t[par], xs_t[par]
                    gs, eb, xp, xpp, PTs = gs2[par], eb2[par], xp2[par], xpp2[par], PTs2[par]
                    if dr == 0:
                        tk.dma(ybls[:], S["yb_all"][tb:tb + 128, 1024:2048], track="L_ybls", outs=[ybls], ins=[ykey + "s"])
                        ld(gate_s[:], S["sz_tm"][tb:tb + 128, :])
                    for g in range(4):
                        pyi = nb_s()
                        for hh in range(4):
                            hd = g * 4 + hh
                            mm(pyi[:, hh * 64:(hh + 1) * 64], PTs[g][:, hh, :], xp[:, hd, :])
                        mm(pyi[:, 256:512], cm[:, g, :], hTb[dr][g][:])
                        pg = pis[g % 2]
                        tt("dve", pg[:].rearrange("p (h e) -> p h e", h=4), pyi[:, 256:512].rearrange("p (h e) -> p h e", h=4), gs[:, 2, g * 4:(g + 1) * 4].unsqueeze(2).to_broadcast([128, 4, 64]), ALU.mult)
                        yo = ycs[:, g * 256:(g + 1) * 256]
                        tt("dve", yo, pg[:], pyi[:, 0:256], ALU.add)
                        if g % 2 == 1:
                            yield
                    if dr == 0:
                        tt("pool", ycs[:], ycs[:], ybls[:], ALU.add)
                    for g in range(4):
                        pd = nb_s()
                        mm(pd[:, 0:256], bmtm[:, g * 128:(g + 1) * 128], xpp[:, g * 256:(g + 1) * 256])
                        tt("dve", tmps[:], hT32[dr][g][:], eb[:, g * 4:(g + 1) * 4].unsqueeze(2).to_broadcast([128, 4, 64]), ALU.mult)
                        tt("dve", hT32[dr][g][:], tmps[:], pd[:, 0:256].rearrange("p (h e) -> p h e", h=4), ALU.add)
                        if not last:
                            cp("act", hTb[dr][g][:].rearrange("p (h e) -> p h e", h=4), hT32[dr][g][:])
                        elif is_p:
                            for hh in range(4):
                                pq = nb_s()
                                tr(pq[0:64, 0:128], hT32[dr][g][:, hh, :], ident[:, :])
                                cp("dve", sst[:], pq[0:64, 0:128])
                                st(o_s[sq, l, dr, g * 4 + hh, :, :], sst[:])
                        if g % 2 == 1:
                            yield
                    if dr == 1:
                        tk.dma(S["yb_all"][tb:tb + 128, 1024:2048], ycs[:], track="S_ycs", ins=[ycs], outs=[ykey + "s"])
                    else:
                        tt("pool", tmpS[:].rearrange("p (h e) -> p h e", h=16), xs[:].rearrange("p (h e) -> p h e", h=16), sdb[:].unsqueeze(2).to_broadcast([128, 16, 64]), ALU.mult)
                        yield
                        tt("dve", ycs[:], ycs[:], tmpS[:], ALU.add)
                        tt("dve", ycs[:], ycs[:], gate_s[:], ALU.mult)
                        rms_rows("act", rsts[:, 0:1], ycs[:].unsqueeze(1), 1, 1024, tmpS[:].unsqueeze(1))
                        yield
                        stt("dve", ybfs[:], ycs[:], rsts[:, 0:1], snb[:], ALU.mult, ALU.mult)
                        yield
                        for q in range(2):
                            for h in range(4):
                                j = q * 4 + h
                                tr(psTb[:, h * 128:(h + 1) * 128], ybfs[:, j * 128:(j + 1) * 128], ident_b[:])
                            cp("act", yTs[:, q * 4:q * 4 + 4, :], psTb[:, 0:512].rearrange("p (h t) -> p h t", h=4))
                        st(S["ysT"][:, tb:tb + 128].rearrange("(h f) t -> f h t", f=128), yTs[:])

                def emit_loads(k):
                    dr_, b_ = iters[k]
                    tb_ = s0 + b_ * 128
                    loads_h(dr_, tb_, k % 2)
                    loads_s(dr_, tb_, k % 2)
                    loads_m(dr_, tb_, k % 2)

                emit_loads(0)
                run_gens([front_h(iters[0][0], iters[0][1], 0), front_s(iters[0][0], iters[0][1], 0)])
                for k, (dr, b) in enumerate(iters):
                    gl = [back_h(dr, b, k % 2), back_s(dr, b, k % 2), gen_m(dr, b, k % 2)]
                    if k + 1 < len(iters):
                        emit_loads(k + 1)
                        dn_, bn_ = iters[k + 1]
                        gl += [front_h(dn_, bn_, (k + 1) % 2), front_s(dn_, bn_, (k + 1) % 2)]
                    run_gens(gl)
            tk.barrier()
            pst.close()
            cur_es[0] = es

        def phase_O(l, final):
            pst = ExitStack()
            cur_es[0] = pst
            alloc_P(full=False)
            xt_b, hT = B_["xt_b"], B_["hT"]
            yT_t = sb([128, 16, TT], BF16, "yT_t")
            bg_t = sb([128, 8, TT], BF16, "bg_t")
            mrg = sb([128, 8, TT], BF16, "mrg")
            t1 = sb([128, TT], F32, "t1")
            t2 = sb([128, TT], F32, "t2")
            actT = sb([128, 22, TT], BF16, "actT")
            sa = sb([128, TT], F32, "sa")
            wdn = [sb([128, 22, 128], BF16, f"wdn{i}") for i in range(3)]
            wdi = {"i": 0}
            nfT = sb([128, 8], F32, "nfT")
            otm = sb([128, D], F32, "otm")
            if final:
                vec_to_fm(nfT[:, :], norm_f.rearrange("(c p) -> c p", p=128), 8)
            def ld_yT(t0_):
                ld(yT_t[:, 0:4, :], S["ymT"][:, t0_:t0_ + TT].rearrange("(c p) t -> p c t", p=128))
                ld(yT_t[:, 4:8, :], S["yhT"][:, t0_:t0_ + TT].rearrange("(c p) t -> p c t", p=128))
                ld(yT_t[:, 8:16, :], S["ysT"][:, t0_:t0_ + TT].rearrange("(c p) t -> p c t", p=128))

            for ti in range(NTT):
                t0 = ti * TT
                cidx = 0 if ti < 2 else 1
                xt = xt_b[ti % 2]
                if ti == 0:
                    ld(xt[:], xT[:, :, t0:t0 + TT].rearrange("c p t -> p c t"))
                    ld_yT(0)
                if ti + 1 < NTT:
                    ld(xt_b[(ti + 1) % 2][:], xT[:, :, t0 + TT:t0 + 2 * TT].rearrange("c p t -> p c t"))
                for bi, (wn, kc0, nk) in enumerate((("bm", 0, 4), ("bh", 4, 4), ("bs", 8, 8))):
                    ld(bg_t[:], S["bgT"][bi * 1024:(bi + 1) * 1024, t0:t0 + TT].rearrange("(c p) t -> p c t", p=128))
                    for half in range(2):
                        w = load_w(wn, l, 0, nk * 128, half * 512, 512)
                        for j in range(4):
                            fc = half * 4 + j
                            pt = npsf()
                            for kc in range(nk):
                                mm(pt[:], w[:, kc, j * 128:(j + 1) * 128], yT_t[:, kc0 + kc, :], start=(kc == 0), stop=(kc == nk - 1))
                            if bi == 0:
                                tt("dve", mrg[:, fc, :], pt[:], bg_t[:, fc, :], ALU.mult)
                            else:
                                tt("dve", t1[:], pt[:], bg_t[:, fc, :], ALU.mult)
                                tt("pool", mrg[:, fc, :], mrg[:, fc, :], t1[:], ALU.add)
                if ti + 1 < NTT:
                    ld_yT(t0 + TT)
                for half in range(2):
                    w = load_w("out", l, 0, D, half * 512, 512)
                    for j in range(4):
                        fc = half * 4 + j
                        pt = npsf()
                        for kc in range(8):
                            mm(pt[:], w[:, kc, j * 128:(j + 1) * 128], mrg[:, kc, :], start=(kc == 0), stop=(kc == 7))
                        stt("dve", xt[:, fc, :], pt[:], MOD[:, l, 2, fc, cidx:cidx + 1], xt[:, fc, :], ALU.mult, ALU.add)
                norm_mod(l, ti, 4, 3, xt)
                for fb in range(0, 22, 4):
                    nf = min(4, 22 - fb)
                    wa_ = load_w("gu", l, 0, D, fb * 128, nf * 128)
                    wb_ = load_w("gu", l, 0, D, D_FF + fb * 128, nf * 128)
                    for j in range(nf):
                        pa = npsf()
                        for kc in range(8):
                            mm(pa[:], wa_[:, kc, j * 128:(j + 1) * 128], hT[:, kc, :], start=(kc == 0), stop=(kc == 7))
                        pb = npsf()
                        for kc in range(8):
                            mm(pb[:], wb_[:, kc, j * 128:(j + 1) * 128], hT[:, kc, :], start=(kc == 0), stop=(kc == 7))
                        act(sa[:], pa[:], AF.Silu)
                        tt("dve", actT[:, fb + j, :], sa[:], pb[:], ALU.mult)
                for fc in range(8):
                    wdi["i"] += 1
                    wd = wdn[wdi["i"] % 3]
                    ld(wd[:], wb["down"][1][l, :, fc * 128:(fc + 1) * 128].rearrange("(kc p) n -> p kc n", p=128))
                    pt = npsf()
                    for kc in range(22):
                        mm(pt[:], wd[:, kc, :], actT[:, kc, :], start=(kc == 0), stop=(kc == 21))
                    stt("dve", xt[:, fc, :], pt[:], MOD[:, l, 5, fc, cidx:cidx + 1], xt[:, fc, :], ALU.mult, ALU.add)
                if not final:
                    st(xT[:, :, t0:t0 + TT].rearrange("c p t -> p c t"), xt[:])
                else:
                    act(B_["sq_b"][:], xt[:], AF.Square)
                    pt = npsf()
                    for c in range(8):
                        mm(pt[:], ones_b[:], B_["sq_b"][:, c, :], start=(c == 0), stop=(c == 7))
                    act(B_["rstd"][:], pt[:], AF.Ln, EPS, 1.0 / D)
                    act(B_["rstd"][:], B_["rstd"][:], AF.Exp, 0.0, -0.5)
                    tt("dve", B_["hn_f"][:], xt[:], B_["rstd"][:].unsqueeze(1).to_broadcast([128, 8, TT]), ALU.mult)
                    tt("dve", B_["hn_f"][:], B_["hn_f"][:], nfT[:].unsqueeze(2).to_broadcast([128, 8, TT]), ALU.mult)
                    for s4 in range(4):
                        for half in range(2):
                            pt = npsf()
                            for c4 in range(4):
                                c = half * 4 + c4
                                tr(pt[:, c4 * 128:(c4 + 1) * 128], B_["hn_f"][:, c, s4 * 128:(s4 + 1) * 128], ident)
                            cp("act" if half else "dve", otm[:, half * 512:(half + 1) * 512], pt[:])
                        st(y_out[t0 + s4 * 128:t0 + (s4 + 1) * 128, :], otm[:])
            tk.barrier()
            pst.close()
            cur_es[0] = es

        for l in range(nlayers):
            if "P" in secs:
                phase_P(l)
            if "S" in secs:
                phase_S(l)
            if "O" in secs:
                phase_O(l, l == nlayers - 1)

        tk.barrier()
        block = es.enter_context(nc.Block())
        tk.emit(block)
    return nc, dbg


def make_consts():
    c = np.zeros((128, 768), np.float32)
    c[:, 0:128] = np.eye(128)
    s = np.arange(128)[:, None]
    t = np.arange(128)[None, :]
    c[:, 128:256] = (s <= t)
    c[:, 256:384] = (s >= t)
    c[:, 384:512] = np.where(s <= t, 0.0, -1e30)
    c[:, 512:640] = np.where(s >= t, 0.0, -1e30)
    c[:, 640:704] = np.arange(64)[None, :]
    c[:, 704] = np.arange(128)
    return c


def make_in_maps(inputs):
    f = lambda a: np.ascontiguousarray(a, dtype=np.float32)
    shared = {k: f(inputs[k]) for k in ("w_ada", "b_ada", "norm1", "norm2", "w_in", "m_norm", "h_lb", "h_norm", "s_conv_w",
                                        "s_conv_b", "s_d", "s_norm", "w_bm", "w_bh", "w_bs", "w_out", "w_gu", "w_down", "norm_f")}
    shared["m_bi"] = f(inputs["m_bi"]).reshape(DEPTH, 8)
    shared["m_bf"] = f(inputs["m_bf"]).reshape(DEPTH, 8)
    shared["s_dt_bias"] = f(inputs["s_dt_bias"]).reshape(DEPTH, 32)
    shared["s_a_log"] = f(inputs["s_a_log"]).reshape(DEPTH, 32)
    shared["cst"] = make_consts()
    maps = []
    for i in range(8):
        m = dict(shared)
        m["x_in"] = np.concatenate([f(inputs["x_prompt"][4 * i:4 * i + 4]).reshape(1024, D), f(inputs["x_sample"][i])], 0)
        m["st_c"] = f(inputs["state_mlstm_c"][i])
        m["st_n"] = f(inputs["state_mlstm_n"][i])
        m["st_m"] = f(inputs["state_mlstm_m"][i])
        m["st_h"] = f(inputs["state_hgrn"][i])
        m["st_s"] = f(inputs["state_ssm"][i])
        m["cvec"] = np.stack([f(inputs["c_ctx"]), f(inputs["c"][i])], 0)
        maps.append(m)
    return maps


def kernel(**inputs):
    nc, _ = build()
    maps = make_in_maps(inputs)
    res = run_bass_kernel_spmd(nc, maps, core_ids=list(range(8)))
    R = res.results
    y_prompt = np.concatenate([r["y_out"][0:1024].reshape(4, 256, D) for r in R], 0)
    y_sample = np.stack([r["y_out"][1024:] for r in R], 0)
    outs = [y_prompt, y_sample]
    for nm in ("o_c", "o_n", "o_m", "o_h", "o_s"):
        outs.append(np.concatenate([r[nm] for r in R], 0))
    return tuple(np.ascontiguousarray(o, dtype=np.float32) for o in outs)
```
